# Optimizing a Trainium2 kernel written in Bass

```python
import math
import jax, jax.numpy as jnp
from jax import lax
import numpy as np

D_MODEL = 1024
BATCH = 8
SEQ = 4096
DEPTH = 4

MIX_WIDTH = D_MODEL
CONV_CH = MIX_WIDTH // 2
CONV_K = 31
ATT_HEADS = 4
ATT_HD = 64
ATT_VD = 2 * ATT_HD
ATT_OUT = ATT_HEADS * ATT_VD
QK_COLS = ATT_HEADS * 2 * ATT_HD
IN_COLS = 2 * CONV_CH + 2 * QK_COLS + ATT_OUT
ROT_DIM = ATT_HD // 4
ROPE_THETA = 500000.0
D_FF = 2816
FFN_K = 3
Q_BLOCK = 128
EPS = 1e-6

kernel_name = "hybrid_conformer_diffattn_convffn"


def rms_norm(x, g):
    xf = x.astype(jnp.float32)
    y = xf * lax.rsqrt(jnp.mean(xf * xf, axis=-1, keepdims=True) + EPS)
    return (y * g.astype(jnp.float32)).astype(x.dtype)


def layer_norm(x, g, b):
    xf = x.astype(jnp.float32)
    mu = jnp.mean(xf, axis=-1, keepdims=True)
    var = jnp.mean(jnp.square(xf - mu), axis=-1, keepdims=True)
    y = (xf - mu) * lax.rsqrt(var + EPS)
    return (y * g.astype(jnp.float32) + b.astype(jnp.float32)).astype(x.dtype)


def causal_dwconv(x, w, b):
    k, c = w.shape
    y = lax.conv_general_dilated(
        x, w[:, None, :].astype(x.dtype), window_strides=(1,), padding=[(k - 1, 0)],
        dimension_numbers=("NWC", "WIO", "NWC"), feature_group_count=c)
    return y + b.astype(x.dtype)


def rope_tables(seq):
    pos = jnp.arange(seq, dtype=jnp.float32)
    inv_freq = ROPE_THETA ** (-jnp.arange(0, ROT_DIM, 2, dtype=jnp.float32) / ROT_DIM)
    ang = pos[:, None] * inv_freq[None, :]
    return jnp.cos(ang), jnp.sin(ang)


def apply_partial_rope(t, cos, sin):
    c = cos[None, :, None, None, :].astype(t.dtype)
    s = sin[None, :, None, None, :].astype(t.dtype)
    half = ROT_DIM // 2
    t1, t2, rest = t[..., :half], t[..., half:ROT_DIM], t[..., ROT_DIM:]
    return jnp.concatenate([t1 * c - t2 * s, t2 * c + t1 * s, rest], axis=-1)


def diff_attention(q1, q2, k1, k2, v, lam):
    b, h, s, d = q1.shape
    nb = s // Q_BLOCK
    scale = 1.0 / math.sqrt(d)

    def to_blocks(t):
        return t.reshape(b, h, nb, Q_BLOCK, d).transpose(2, 0, 1, 3, 4)

    kpos = jnp.arange(s)

    def one_block(args):
        q1b, q2b, start = args
        qpos = start + jnp.arange(Q_BLOCK)
        causal = kpos[None, :] <= qpos[:, None]
        s1 = jnp.einsum("bhqd,bhkd->bhqk", q1b, k1).astype(jnp.float32) * scale
        s2 = jnp.einsum("bhqd,bhkd->bhqk", q2b, k2).astype(jnp.float32) * scale
        neg = jnp.finfo(jnp.float32).min
        p1 = jax.nn.softmax(jnp.where(causal, s1, neg), axis=-1)
        p2 = jax.nn.softmax(jnp.where(causal, s2, neg), axis=-1)
        a = (p1 - lam * p2).astype(v.dtype)
        return jnp.einsum("bhqk,bhkv->bhqv", a, v)

    starts = jnp.arange(nb, dtype=jnp.int32) * Q_BLOCK
    out = lax.map(one_block, (to_blocks(q1), to_blocks(q2), starts))
    return out.transpose(1, 2, 0, 3, 4).reshape(b, h, s, v.shape[-1])


def setup_inputs(seed: int = 0) -> dict:
    key = jax.random.key(seed)
    ks = jax.random.split(key, 24)
    f32 = jnp.float32

    def nrm(k, shape, scale):
        return jax.random.normal(k, shape, f32) * scale

    def gain(k, shape):
        return 1.0 + 0.02 * jax.random.normal(k, shape, f32)

    return {
        "x": jax.random.normal(ks[0], (BATCH, SEQ, D_MODEL), f32),
        "pre_mix_norm": gain(ks[1], (DEPTH, D_MODEL)),
        "w_in": nrm(ks[2], (DEPTH, D_MODEL, IN_COLS), D_MODEL ** -0.5),
        "conv_w": nrm(ks[3], (DEPTH, CONV_K, CONV_CH), CONV_K ** -0.5),
        "conv_b": nrm(ks[4], (DEPTH, CONV_CH), 0.02),
        "conv_ln_g": gain(ks[5], (DEPTH, CONV_CH)),
        "conv_ln_b": nrm(ks[6], (DEPTH, CONV_CH), 0.02),
        "lambda_q1": nrm(ks[7], (DEPTH, ATT_HD), 0.1),
        "lambda_k1": nrm(ks[8], (DEPTH, ATT_HD), 0.1),
        "lambda_q2": nrm(ks[9], (DEPTH, ATT_HD), 0.1),
        "lambda_k2": nrm(ks[10], (DEPTH, ATT_HD), 0.1),
        "subln_g": gain(ks[11], (DEPTH, ATT_VD)),
        "w_out": nrm(ks[12], (DEPTH, CONV_CH + ATT_OUT, D_MODEL), (CONV_CH + ATT_OUT) ** -0.5),
        "post_mix_norm": gain(ks[13], (DEPTH, D_MODEL)),
        "pre_ffn_norm": gain(ks[14], (DEPTH, D_MODEL)),
        "w_up": nrm(ks[15], (DEPTH, D_MODEL, 2 * D_FF), D_MODEL ** -0.5),
        "ffn_conv_w": nrm(ks[16], (DEPTH, FFN_K, 2 * D_FF), FFN_K ** -0.5),
        "ffn_conv_b": nrm(ks[17], (DEPTH, 2 * D_FF), 0.02),
        "w_down": nrm(ks[18], (DEPTH, D_FF, D_MODEL), D_FF ** -0.5),
        "post_ffn_norm": gain(ks[19], (DEPTH, D_MODEL)),
    }


def reference(x, pre_mix_norm, w_in, conv_w, conv_b, conv_ln_g, conv_ln_b,
              lambda_q1, lambda_k1, lambda_q2, lambda_k2, subln_g, w_out,
              post_mix_norm, pre_ffn_norm, w_up, ffn_conv_w, ffn_conv_b, w_down,
              post_ffn_norm):
    b, s, _ = x.shape
    cos, sin = rope_tables(s)
    for l in range(DEPTH):
        lam_init = 0.8 - 0.6 * math.exp(-0.3 * l)
        h = rms_norm(x, pre_mix_norm[l])
        z = jnp.einsum("bsd,dc->bsc", h, w_in[l])
        c_a, c_g, zq, zk, zv = jnp.split(
            z, np.cumsum([CONV_CH, CONV_CH, QK_COLS, QK_COLS]).tolist(), axis=-1)
        u = c_a * jax.nn.sigmoid(c_g)
        u = causal_dwconv(u, conv_w[l], conv_b[l])
        u = jax.nn.silu(layer_norm(u, conv_ln_g[l], conv_ln_b[l]))
        q = apply_partial_rope(zq.reshape(b, s, ATT_HEADS, 2, ATT_HD), cos, sin)
        k = apply_partial_rope(zk.reshape(b, s, ATT_HEADS, 2, ATT_HD), cos, sin)
        v = zv.reshape(b, s, ATT_HEADS, ATT_VD).transpose(0, 2, 1, 3)
        q1, q2 = q[..., 0, :].transpose(0, 2, 1, 3), q[..., 1, :].transpose(0, 2, 1, 3)
        k1, k2 = k[..., 0, :].transpose(0, 2, 1, 3), k[..., 1, :].transpose(0, 2, 1, 3)
        lam = (jnp.exp(jnp.sum(lambda_q1[l].astype(jnp.float32) * lambda_k1[l].astype(jnp.float32)))
               - jnp.exp(jnp.sum(lambda_q2[l].astype(jnp.float32) * lambda_k2[l].astype(jnp.float32)))
               + lam_init)
        o = diff_attention(q1, q2, k1, k2, v, lam)
        o = rms_norm(o, subln_g[l]) * (1.0 - lam_init)
        o = o.transpose(0, 2, 1, 3).reshape(b, s, ATT_OUT)
        m = jnp.einsum("bsc,cd->bsd", jnp.concatenate([u, o], axis=-1), w_out[l])
        x = x + rms_norm(m, post_mix_norm[l])
        h = rms_norm(x, pre_ffn_norm[l])
        up = jnp.einsum("bsd,df->bsf", h, w_up[l])
        up = causal_dwconv(up, ffn_conv_w[l], ffn_conv_b[l])
        g, val = jnp.split(up, 2, axis=-1)
        f = jnp.einsum("bsf,fd->bsd", jax.nn.gelu(g, approximate=True) * val, w_down[l])
        x = x + rms_norm(f, post_ffn_norm[l])
    return x
```

```python
import math
import numpy as np
import concourse.bass as bass
import concourse.mybir as mybir
from concourse.bass_utils import run_bass_kernel_spmd

F32 = mybir.dt.float32
BF16 = mybir.dt.bfloat16
AF = mybir.ActivationFunctionType
ALU = mybir.AluOpType
AX = mybir.AxisListType

P = 128
D = 1024
NK = 8
CIN = 2560
DFF = 2816
NJ = 22
CUP = 5632
TG = 512
EPS = 1e-6
CONV_K = 31
PV_CW = 0
PV_CB = PV_CW + 4 * 31
PV_LG = PV_CB + 4
PV_LB = PV_LG + 4
PV_FW = PV_LB + 4
PV_FB = PV_FW + 44 * 3
NPV = PV_FB + 44


class Buf:
    __slots__ = ("name", "w", "r", "excl")

    def __init__(self, name, excl=False):
        self.name = name
        self.w = None
        self.r = {}
        self.excl = excl


class Holder:
    def __init__(self, nc, name):
        self.name = name
        self.sem = nc.alloc_semaphore(name)
        self.cnt = 0


class Queue(Holder):
    def __init__(self, nc, eng, name):
        super().__init__(nc, "q_" + name)
        self.eng = eng
        self.seen = {}
        self.nwait = 0
        self.nins = 0
        self.inorder = False

    def wait_tok(self, holder, val):
        if holder is self and self.inorder:
            return
        if self.seen.get(holder, 0) >= val:
            return
        assert val <= holder.cnt, (self.name, holder.name, val, holder.cnt)
        self.eng.wait_ge(holder.sem, val)
        self.seen[holder] = val
        self.nwait += 1


def _deps(q, reads, writes):
    toks = {}
    for b in reads:
        if b.w is not None:
            h, v = b.w
            if toks.get(h, 0) < v:
                toks[h] = v
        if b.excl:
            for h, v in b.r.items():
                if h is not q and toks.get(h, 0) < v:
                    toks[h] = v
    for b in writes:
        if b.w is not None:
            h, v = b.w
            if toks.get(h, 0) < v:
                toks[h] = v
        for h, v in b.r.items():
            if toks.get(h, 0) < v:
                toks[h] = v
    for h, v in toks.items():
        q.wait_tok(h, v)


def _record(tok, reads, writes):
    h, v = tok
    for b in reads:
        if b.r.get(h, 0) < v:
            b.r[h] = v
    for b in writes:
        b.w = tok
        b.r = {}


def op(q, ins_fn, reads=(), writes=(), signal=True):
    _deps(q, reads, writes)
    ins = ins_fn()
    q.nins += 1
    if signal:
        ins.then_inc(q.sem, 1)
        q.cnt += 1
        tok = (q, q.cnt)
    else:
        tok = (q, q.cnt + 1)
    _record(tok, reads, writes)
    return ins


def dma(q, dsem, out, in_, reads=(), writes=()):
    _deps(q, reads, writes)
    ins = q.eng.dma_start(out=out, in_=in_)
    ins.then_inc(dsem.sem, 16)
    dsem.cnt += 16
    _record((dsem, dsem.cnt), reads, writes)
    return ins


class Arena:
    def __init__(self, nc, nbytes):
        self.nbytes = nbytes
        self.t = nc.alloc_sbuf_tensor("arena", [P, nbytes // 2], BF16)

    def view(self, off, shape, dt):
        esz = 4 if dt == F32 else 2
        n = 1
        for d_ in shape:
            n *= d_
        nb = n * esz
        assert off % 4 == 0 and off + nb <= self.nbytes, (off, nb, self.nbytes)
        ap = self.t[:, off // 2:(off + nb) // 2]
        if dt == F32:
            ap = ap.bitcast(F32)
        if len(shape) == 2:
            ap = ap.rearrange("p (a b) -> p a b", a=shape[0])
        elif len(shape) == 3:
            ap = ap.rearrange("p (a b c) -> p a b c", a=shape[0], b=shape[1])
        return ap


class Layout:
    def __init__(self, arena, base=0):
        self.arena = arena
        self.off = base
        self.hi = base

    def get(self, shape, dt):
        esz = 4 if dt == F32 else 2
        n = esz
        for d_ in shape:
            n *= d_
        off = (self.off + 31) // 32 * 32
        self.off = off + n
        self.hi = max(self.hi, self.off)
        return self.arena.view(off, shape, dt)


def build_program(L, S, lam_inits):
    NG = S // TG
    NT = S // P
    nc = bass.Bass("TRN2", target_bir_lowering=False)

    def din(name, shape):
        return nc.dram_tensor(name, list(shape), F32, kind="ExternalInput")

    x_d = din("x", [S, D])
    w_in_d = din("w_in", [L, D, CIN])
    w_out_d = din("w_out", [L, D, D])
    w_up_d = din("w_up", [L, D, CUP])
    w_dn_d = din("w_down", [L, DFF, D])
    pvec_d = din("pvec", [L, P, NPV])
    fvec_d = din("fvec", [L, 4, D])
    lamv_d = din("lamv", [L, 1, 256])
    sg_d = din("sublng", [L, 1, P])
    cmat_d = din("cmat", [3, P, P])
    rope_d = din("rope", [2, P, S])
    out_d = nc.dram_tensor("out", [S, D], F32, kind="ExternalOutput")
    xs_d = nc.dram_tensor("xs_scratch", [S, D], F32)

    pe = Queue(nc, nc.tensor, "pe")
    pe.inorder = True
    act = Queue(nc, nc.scalar, "act")
    dve = Queue(nc, nc.vector, "dve")
    pool = Queue(nc, nc.gpsimd, "pool")
    sp = Queue(nc, nc.sync, "sp")

    ARENA_BYTES = 212800
    arena = Arena(nc, ARENA_BYTES)
    com = Layout(arena, 0)
    ident = com.get([P], BF16)
    pm = com.get([P], BF16)
    cmask = com.get([P], BF16)
    ones = com.get([P], BF16)
    identf = com.get([P], F32)
    pv = com.get([NPV], F32)
    sg = com.get([P], F32)
    small = com.get([64], F32)
    g1 = com.get([D], F32)
    g2 = com.get([D], F32)
    xin = [com.get([D], F32) for _ in range(2)]
    xres = [com.get([D], F32)]
    hbf = [com.get([D], BF16) for _ in range(2)]
    hT = com.get([NK, TG], BF16)
    tmps = com.get([4, TG], F32)
    ropeCS = com.get([2, TG], F32)
    ropeC = ropeCS[:, 0, :]
    ropeS = ropeCS[:, 1, :]
    xres_rope = ropeCS.rearrange("p a b -> p (a b)")
    lamb = tmps[:, 3, 0:256]
    base_phase = (com.hi + 31) // 32 * 32
    mx = Layout(arena, base_phase)
    w_in = mx.get([NK, CIN], BF16)
    w_out = mx.get([NK, D], BF16)
    KT = mx.get([4, S], BF16)
    V = mx.get([NT, 4, 130], BF16)
    QT = mx.get([4, TG], BF16)
    Ubuf = mx.get([4, 30 + TG], BF16)
    UO = mx.get([NK, TG], BF16)
    ybf_off = (mx.off + 31) // 32 * 32
    ybf = mx.get([4, TG], BF16)
    xres_ybf = arena.view(ybf_off, [D], F32)
    ysq = mx.get([4, TG], BF16)
    rl = mx.get([2, 4], F32)
    ssr = mx.get([4], F32)
    xoff = (mx.off + 31) // 32 * 32
    Dg = mx.get([CONV_K, P], BF16)
    Pt = [[arena.view(xoff + (2 * c + s_) * 1024, [TG], BF16) for s_ in range(2)] for c in range(2)]
    obf = arena.view(xoff + 4096, [4, P], BF16)
    zbf = arena.view(xoff + 5120, [TG], BF16)
    DgB = mx.get([CONV_K, P], BF16)
    ff = Layout(arena, base_phase)
    w_up = ff.get([NK, CUP], BF16)
    w_dn = ff.get([NJ, D], BF16)
    HID = ff.get([NJ, TG], BF16)
    Rb = [ff.get([2 + TG], BF16) for _ in range(3)]
    halo = ff.get([44, 2], BF16)
    D3 = [ff.get([3, P], BF16) for _ in range(2)]
    print("SBUF layout: common %d, mixer %d, ffn %d (limit %d)" % (com.hi, mx.hi, ff.hi, ARENA_BYTES))
    assert mx.hi <= ARENA_BYTES and ff.hi <= ARENA_BYTES

    psum = nc.alloc_psum_tensor("psum", [P, 8 * 512], F32)

    def bank(i, n=1):
        return psum[:, i * 512:(i + n) * 512]

    pT = bank(0).bitcast(BF16)

    B = {}

    def b(name):
        if name not in B:
            B[name] = Buf(name)
        return B[name]

    pb = [b("psum%d" % i) for i in range(8)]
    for x_ in pb:
        x_.excl = True
    WI_BLK = ["wi_a", "wi_g", "wi_q", "wi_k", "wi_v"]
    xalias = ["Pt00", "Pt01", "Pt10", "Pt11", "obf", "zbf"]
    mixer_names = WI_BLK + ["w_out0", "w_out1", "QT", "Ubuf", "Dg", "DgB", "UO", "ybf", "ysq", "rl0", "rl1", "ssr", "Vones"] + xalias + \
        ["KT%d" % g for g in range(NG)] + ["V%d" % g for g in range(NG)]
    ffn_names = ["wu_g%d" % i for i in range(6)] + ["wu_v%d" % i for i in range(6)] + ["wd%d" % i for i in range(6)] + \
        ["HID", "R0", "R1", "R2", "halo", "D30", "D31"]
    mixer_bufs = [b(n) for n in mixer_names]
    ffn_bufs = [b(n) for n in ffn_names]
    xalias_bufs = [b(n) for n in xalias]
    hT_bufs = [b("hT%d" % i) for i in range(4)]

    dsem = {}

    def ds(name):
        if name not in dsem:
            dsem[name] = Holder(nc, "d_" + name)
        return dsem[name]

    def sc(i):
        return small[:, i:i + 1]
    S1, S2, E1, E2, NLAM = range(5)
    SSQ0, RSTD0, PSSQ0, PRSTD0 = 8, 10, 12, 14
    EPSC = 16

    dma(pool, ds("c0"), ident, cmat_d[0], writes=[b("ident")])
    dma(pool, ds("c0b"), pm, cmat_d[1], writes=[b("pm")])
    dma(pool, ds("c0c"), cmask, cmat_d[2], writes=[b("cmask")])
    dma(sp, ds("c1"), identf, cmat_d[0], writes=[b("identf")])
    op(dve, lambda: nc.vector.memset(ones, 1.0), writes=[b("ones")])
    op(dve, lambda: nc.vector.memset(sc(EPSC), EPS), writes=[b("epsc")])

    def norm_chain(T, src_d, srcbuf):
        s_ = T % 2
        xb, hb = b("xin%d" % s_), b("hbf%d" % s_)
        sq, rs = b("ssq%d" % s_), b("rstd%d" % s_)
        dma(sp, ds("xin%d" % s_), xin[s_], src_d[T * P:(T + 1) * P, :], reads=[b(srcbuf)], writes=[xb])
        op(dve, lambda: nc.vector.memset(sc(SSQ0 + s_), 0.0), writes=[sq])
        op(act, lambda: nc.scalar.activation(out=hbf[s_], in_=xin[s_], func=AF.Square, accum_out=sc(SSQ0 + s_)),
           reads=[xb], writes=[hb, sq])
        op(act, lambda: nc.scalar.activation(out=sc(RSTD0 + s_), in_=sc(SSQ0 + s_), func=AF.Ln, bias=sc(EPSC), scale=1.0 / D),
           reads=[sq, b("epsc")], writes=[rs])
        op(act, lambda: nc.scalar.activation(out=sc(RSTD0 + s_), in_=sc(RSTD0 + s_), func=AF.Exp, scale=-0.5), reads=[rs], writes=[rs])
        op(dve, lambda: nc.vector.scalar_tensor_tensor(out=hbf[s_], in0=xin[s_], scalar=sc(RSTD0 + s_), in1=g1,
                                                       op0=ALU.mult, op1=ALU.mult),
           reads=[xb, rs, b("g1")], writes=[hb])

    def norm_T(T):
        s_ = T % 2
        tt = T % 4
        hb = b("hbf%d" % s_)
        for kc in range(NK):
            op(pe, lambda kc=kc: nc.tensor.transpose(out=pT[:, kc * P:(kc + 1) * P], in_=hbf[s_][:, kc * P:(kc + 1) * P],
                                                   identity=ident),
               reads=[hb, b("ident")], writes=[pb[0]], signal=(kc == NK - 1))
        op(dve, lambda: nc.vector.tensor_copy(out=hT[:, :, tt * P:(tt + 1) * P],
                                              in_=pT[:, 0:NK * P].rearrange("p (k c) -> p k c", k=NK)),
           reads=[pb[0]], writes=[hT_bufs[tt]])

    def mm_group(out_ap, pairs, reads, wbuf, start=True):
        n = len(pairs)
        for i, (l_, r_) in enumerate(pairs):
            op(pe, lambda l_=l_, r_=r_, i=i: nc.tensor.matmul(out=out_ap, lhsT=l_, rhs=r_, start=(start and i == 0),
                                                              stop=(i == n - 1)),
               reads=reads, writes=[wbuf], signal=(i == n - 1))

    def post_norm_residual(T, mbanks, src_ap, dst_ap, src_buf_name, dst_buf_name):
        s_ = T % 2
        m_ap = bank(mbanks[0], 2)
        mb = [pb[mbanks[0]], pb[mbanks[1]]]
        if s_ == 0:
            xr, xbs = xres[0], [b("xres0")]
        elif phase[0] == "mixer":
            xr, xbs = xres_ybf, [b("ybf")]
        else:
            xr, xbs = xres_rope, [b("ropeC"), b("ropeS")]
        sq, rs = b("pssq%d" % s_), b("prstd%d" % s_)
        dma(sp, ds("xres%d" % s_), xr, src_ap, reads=[b(src_buf_name)], writes=xbs)
        op(dve, lambda: nc.vector.memset(sc(PSSQ0 + s_), 0.0), writes=[sq])
        tmp32 = tmps[:, 2 * s_:2 * s_ + 2, :].rearrange("p a b -> p (a b)")
        tb = [b("tmp%d" % (2 * s_)), b("tmp%d" % (2 * s_ + 1))]
        op(act, lambda: nc.scalar.activation(out=tmp32, in_=m_ap, func=AF.Square, accum_out=sc(PSSQ0 + s_)),
           reads=mb, writes=tb + [sq])
        op(act, lambda: nc.scalar.activation(out=sc(PRSTD0 + s_), in_=sc(PSSQ0 + s_), func=AF.Ln, bias=sc(EPSC), scale=1.0 / D),
           reads=[sq, b("epsc")], writes=[rs])
        op(act, lambda: nc.scalar.activation(out=sc(PRSTD0 + s_), in_=sc(PRSTD0 + s_), func=AF.Exp, scale=-0.5), reads=[rs], writes=[rs])
        op(dve, lambda: nc.vector.scalar_tensor_tensor(out=tmp32, in0=m_ap, scalar=sc(PRSTD0 + s_), in1=g2,
                                                       op0=ALU.mult, op1=ALU.mult),
           reads=mb + [rs, b("g2")], writes=tb)
        op(dve, lambda: nc.vector.tensor_tensor(out=xr, in0=xr, in1=tmp32, op=ALU.add),
           reads=tb + xbs, writes=xbs)
        dma(sp, ds("xst%d" % s_), dst_ap, xr, reads=xbs, writes=[b(dst_buf_name)])

    rot = [1, 2, 3]
    rr = [0]
    phase = ["mixer"]
    srot = [0]

    def nbs():
        i = srot[0] % 4
        srot[0] += 1
        return i

    def nb():
        i = rot[rr[0] % len(rot)]
        rr[0] += 1
        return i

    for l in range(L):
        lam_init = lam_inits[l]
        src_d = x_d if l == 0 else xs_d
        dma(sp, ds("pv"), pv, pvec_d[l], writes=[b("pv")])
        dma(sp, ds("lamb"), lamb, lamv_d[l].broadcast_to([P, 256]), writes=[b("tmp3")])
        dma(sp, ds("sg"), sg, sg_d[l].broadcast_to([P, P]), writes=[b("sg")])
        dma(sp, ds("g1"), g1, fvec_d[l, 0:1, :].broadcast_to([P, D]), writes=[b("g1")])
        dma(sp, ds("g2"), g2, fvec_d[l, 1:2, :].broadcast_to([P, D]), writes=[b("g2")])
        jt = tmps[:, 2, 0:64]
        op(dve, lambda: nc.vector.scalar_tensor_tensor(out=jt, in0=lamb[:, 0:64], scalar=1.0, in1=lamb[:, 64:128],
                                                       op0=ALU.mult, op1=ALU.mult, accum_out=sc(S1)),
           reads=[b("tmp3")], writes=[b("tmp2"), b("s1")])
        op(dve, lambda: nc.vector.scalar_tensor_tensor(out=jt, in0=lamb[:, 128:192], scalar=1.0, in1=lamb[:, 192:256],
                                                       op0=ALU.mult, op1=ALU.mult, accum_out=sc(S2)),
           reads=[b("tmp3")], writes=[b("tmp2"), b("s2")])
        op(act, lambda: nc.scalar.activation(out=sc(E1), in_=sc(S1), func=AF.Exp), reads=[b("s1")], writes=[b("e1")])
        op(act, lambda: nc.scalar.activation(out=sc(E2), in_=sc(S2), func=AF.Exp), reads=[b("s2")], writes=[b("e2")])
        op(dve, lambda: nc.vector.tensor_tensor(out=sc(NLAM), in0=sc(E2), in1=sc(E1), op=ALU.subtract),
           reads=[b("e1"), b("e2")], writes=[b("nlam")])
        op(dve, lambda: nc.vector.tensor_scalar(out=sc(NLAM), in0=sc(NLAM), scalar1=-float(lam_init), scalar2=None,
                                                op0=ALU.add),
           reads=[b("nlam")], writes=[b("nlam")])
        op(act, lambda: nc.scalar.mul(out=sg, in_=sg, mul=float(1.0 - lam_init)), reads=[b("sg")], writes=[b("sg")])

        first = True
        for blk in (1, 0, 2, 3, 4):
            c0 = blk * 512
            dma(pool, ds("w_in%d" % blk), w_in[:, :, c0:c0 + 512],
                w_in_d[l, :, c0:c0 + 512].rearrange("(k p) c -> p k c", p=P),
                writes=[b(WI_BLK[blk])] + (ffn_bufs if first else []))
            first = False
        for hf in range(2):
            dma(pool, ds("w_out%d" % hf), w_out[:, hf * 4:(hf + 1) * 4, :],
                w_out_d[l, hf * 512:(hf + 1) * 512, :].rearrange("(k p) c -> p k c", p=P), writes=[b("w_out%d" % hf)])
        op(pool, lambda: nc.gpsimd.memset(Ubuf[:, :, 0:30], 0.0), writes=[b("Ubuf")])
        op(pool, lambda: nc.gpsimd.memset(V[:, :, :, 128:130], 1.0), writes=[b("Vones")])

        phase[0] = "mixer"
        for T in range(4):
            norm_chain(T, src_d, "xs%d" % T) if T < 2 else None
        norm_T(0)
        norm_T(1)
        norm_chain(2, src_d, "xs2")
        norm_chain(3, src_d, "xs3")
        norm_T(2)
        norm_T(3)
        for G in range(NG):
            t0 = G * TG
            nxt = G + 1 < NG
            dma(sp, ds("ropeC"), ropeC, rope_d[0, :, t0:t0 + TG], writes=[b("ropeC")])
            dma(sp, ds("ropeS"), ropeS, rope_d[1, :, t0:t0 + TG], writes=[b("ropeS")])
            if nxt and G == 0:
                for T in (4, 5):
                    norm_chain(T, src_d, "xs%d" % T)

            def proj(col0, bi):
                mm_group(bank(bi), [(w_in[:, kc, col0:col0 + P], hT[:, kc, :]) for kc in range(NK)],
                         [b(WI_BLK[col0 // 512])] + hT_bufs, pb[bi])

            def dg_gen(ct):
                wv = pv[:, PV_CW + ct * CONV_K:PV_CW + (ct + 1) * CONV_K]
                dgt, dgb = (DgB, b("DgB")) if ct % 2 == 0 else (Dg, b("Dg"))
                op(pool, lambda: nc.gpsimd.tensor_tensor(out=dgt, in0=identf.unsqueeze(1).broadcast_to([P, CONV_K, P]),
                                                         in1=wv.unsqueeze(2).broadcast_to([P, CONV_K, P]), op=ALU.mult),
                   reads=[b("identf"), b("pv")], writes=[dgb] + (xalias_bufs if ct % 2 == 1 else []))

            def glu_step(ct):
                bg = nb()
                proj(512 + ct * P, bg)
                sgt = tmps[:, 3, :]
                sgb = b("tmp3")
                op(act, lambda: nc.scalar.activation(out=sgt, in_=bank(bg), func=AF.Sigmoid), reads=[pb[bg]], writes=[sgb])
                ba = nb()
                proj(ct * P, ba)
                op(dve, lambda: nc.vector.tensor_tensor(out=Ubuf[:, ct, 30:30 + TG], in0=bank(ba), in1=sgt, op=ALU.mult),
                   reads=[pb[ba], sgb], writes=[b("Ubuf")])

            def halo_copy():
                op(pool, lambda: nc.gpsimd.tensor_copy(out=Ubuf[:, :, 0:30], in_=Ubuf[:, :, TG:TG + 30]),
                   reads=[b("Ubuf")], writes=[b("Ubuf")])

            dg_gen(0)
            if G == 0:
                for ct in range(4):
                    glu_step(ct)
            for which in range(2):
                for h in range(4):
                    col0 = 1024 + which * 512 + h * P
                    bz = nb()
                    proj(col0, bz)
                    op(act, lambda bz=bz: nc.scalar.copy(out=zbf, in_=bank(bz)), reads=[pb[bz]], writes=[b("zbf")])
                    bp = nb()
                    mm_group(bank(bp), [(pm, zbf)], [b("pm"), b("zbf")], pb[bp])
                    op(dve, lambda bz=bz: nc.vector.tensor_tensor(out=tmps[:, 2, :], in0=bank(bz), in1=ropeC, op=ALU.mult),
                       reads=[pb[bz], b("ropeC")], writes=[b("tmp2")])
                    op(dve, lambda bp=bp: nc.vector.tensor_tensor(out=tmps[:, 3, :], in0=bank(bp), in1=ropeS, op=ALU.mult),
                       reads=[pb[bp], b("ropeS")], writes=[b("tmp3")])
                    if which == 0:
                        dst, dbuf = QT[:, h, :], b("QT")
                    else:
                        dst, dbuf = KT[:, h, t0:t0 + TG], b("KT%d" % G)
                    op(dve, lambda dst=dst: nc.vector.tensor_tensor(out=dst, in0=tmps[:, 2, :], in1=tmps[:, 3, :], op=ALU.add),
                       reads=[b("tmp2"), b("tmp3")], writes=[dbuf])
            for tt in range(4):
                T = G * 4 + tt
                bv = nb()
                mm_group(bank(bv), [(hT[:, kc, tt * P:(tt + 1) * P], w_in[:, kc, 2048:2560]) for kc in range(NK)],
                         [b("wi_v"), hT_bufs[tt]], pb[bv])
                op(act, lambda bv=bv, T=T: nc.scalar.copy(out=V[:, T, :, 0:128],
                                                          in_=bank(bv).rearrange("p (h c) -> p h c", h=4)),
                   reads=[pb[bv]], writes=[b("V%d" % G)])
            if nxt:
                norm_T(4 * G + 4)
                norm_T(4 * G + 5)
                for T in (4 * G + 6, 4 * G + 7):
                    norm_chain(T, src_d, "xs%d" % T)
            dg_gen(1)
            for ct in range(4):
                dgt, dgb = (DgB, b("DgB")) if ct % 2 == 0 else (Dg, b("Dg"))
                bc = nb()
                mm_group(bank(bc), [(dgt[:, k, :], Ubuf[:, ct, k:k + TG]) for k in range(CONV_K)],
                         [dgb, b("Ubuf")], pb[bc])
                if ct + 2 < 4:
                    dg_gen(ct + 2)
                cb = pv[:, PV_CB + ct:PV_CB + ct + 1]
                op(act, lambda bc=bc, ct=ct, cb=cb: nc.scalar.activation(out=ybf[:, ct, :], in_=bank(bc), func=AF.Identity, bias=cb),
                   reads=[pb[bc], b("pv")], writes=[b("ybf")])
                op(act, lambda bc=bc, ct=ct, cb=cb: nc.scalar.activation(out=ysq[:, ct, :], in_=bank(bc), func=AF.Square, bias=cb),
                   reads=[pb[bc], b("pv")], writes=[b("ysq")])
            b1 = nb()
            mm_group(bank(b1), [(ones, ybf[:, ct, :]) for ct in range(4)], [b("ones"), b("ybf")], pb[b1])
            b2 = nb()
            mm_group(bank(b2), [(ones, ysq[:, ct, :]) for ct in range(4)], [b("ones"), b("ysq")], pb[b2])
            mean, msq, rstd_t = tmps[:, 0, :], tmps[:, 1, :], tmps[:, 2, :]
            op(act, lambda: nc.scalar.mul(out=mean, in_=bank(b1), mul=1.0 / 512), reads=[pb[b1]], writes=[b("tmp0")])
            op(act, lambda: nc.scalar.activation(out=msq, in_=bank(b1), func=AF.Square, scale=1.0 / 512),
               reads=[pb[b1]], writes=[b("tmp1")])
            op(dve, lambda: nc.vector.scalar_tensor_tensor(out=rstd_t, in0=bank(b2), scalar=1.0 / 512, in1=msq,
                                                           op0=ALU.mult, op1=ALU.subtract),
               reads=[pb[b2], b("tmp1")], writes=[b("tmp2")])
            t3 = tmps[:, 3, :]

            def ln_a2():
                op(act, lambda: nc.scalar.activation(out=rstd_t, in_=rstd_t, func=AF.Ln, bias=sc(EPSC), scale=1.0),
                   reads=[b("tmp2"), b("epsc")], writes=[b("tmp2")])
                op(act, lambda: nc.scalar.activation(out=rstd_t, in_=rstd_t, func=AF.Exp, scale=-0.5), reads=[b("tmp2")], writes=[b("tmp2")])

            def ln_d2(ct):
                op(dve, lambda: nc.vector.tensor_tensor(out=t3, in0=ybf[:, ct, :], in1=mean, op=ALU.subtract),
                   reads=[b("ybf"), b("tmp0")], writes=[b("tmp3")])
                op(dve, lambda: nc.vector.tensor_tensor(out=t3, in0=t3, in1=rstd_t, op=ALU.mult),
                   reads=[b("tmp3"), b("tmp2")], writes=[b("tmp3")])

            def ln_a3(ct):
                op(act, lambda: nc.scalar.activation(out=UO[:, ct, :], in_=t3, func=AF.Silu,
                                                     scale=pv[:, PV_LG + ct:PV_LG + ct + 1],
                                                     bias=pv[:, PV_LB + ct:PV_LB + ct + 1]),
                   reads=[b("tmp3"), b("pv")], writes=[b("UO")])

            accv = [psum[:, (4 + 2 * c) * 512:(6 + 2 * c) * 512].rearrange("p (q w) -> p q w", q=4) for c in range(2)]
            accb = [[pb[4], pb[5]], [pb[6], pb[7]]]
            nkt = 4 * G + 4
            sbank = {}

            def emit_qk(h, kt, c):
                r = kt - 4 * G
                c0 = max(r, 0) * P
                sbk = nbs()
                sbank[(kt, c)] = sbk
                mm_group(bank(sbk)[:, c0:TG],
                         [(KT[c * 64:(c + 1) * 64, h, kt * P:(kt + 1) * P], QT[c * 64:(c + 1) * 64, h, c0:TG])],
                         [b("KT%d" % (kt // 4)), b("QT")], pb[sbk])

            def emit_exp(h, kt, c):
                r = kt - 4 * G
                c0 = max(r, 0) * P
                sbk = sbank[(kt, c)]
                sl = kt % 2
                ptb = b("Pt%d%d" % (c, sl))
                pt = Pt[c][sl]
                op(act, lambda: nc.scalar.activation(out=pt[:, c0:TG], in_=bank(sbk)[:, c0:TG], func=AF.Exp, scale=0.125),
                   reads=[pb[sbk]], writes=[ptb])
                if r >= 0:
                    op(dve, lambda: nc.vector.tensor_tensor(out=pt[:, c0:c0 + P], in0=pt[:, c0:c0 + P], in1=cmask, op=ALU.mult),
                       reads=[ptb, b("cmask")], writes=[ptb])

            def emit_av(h, kt, c):
                r = kt - 4 * G
                q0 = max(r, 0)
                sl = kt % 2
                ptb = b("Pt%d%d" % (c, sl))
                pt = Pt[c][sl]
                for ql in range(q0, 4):
                    bk = 4 + 2 * c + ql // 2
                    reg = psum[:, bk * 512 + (ql % 2) * 256: bk * 512 + (ql % 2) * 256 + 129]
                    first_ = (kt == 0 and ql % 2 == 0)
                    last_ = (kt == 4 * G + ql)
                    op(pe, lambda reg=reg, ql=ql, first_=first_, last_=last_:
                       nc.tensor.matmul(out=reg, lhsT=pt[:, ql * P:(ql + 1) * P], rhs=V[:, kt, h, 0:129],
                                        start=first_, stop=last_, skip_group_check=True),
                       reads=[ptb, b("V%d" % (kt // 4)), b("Vones")], writes=[pb[bk]], signal=(ql == 3))

            def finalize_a(h):
                t1 = tmps[:, 0, :].rearrange("p (q w) -> p q w", q=4)
                t2 = tmps[:, 1, :].rearrange("p (q w) -> p q w", q=4)
                o3 = tmps[:, 2, :].rearrange("p (q w) -> p q w", q=4)
                for c in range(2):
                    op(dve, lambda c=c: nc.vector.reciprocal(out=rl[:, c, :], in_=accv[c][:, :, 128]),
                       reads=accb[c], writes=[b("rl%d" % c)])
                op(dve, lambda: nc.vector.tensor_copy(out=t1, in_=accv[0][:, :, 0:128]), reads=accb[0], writes=[b("tmp0")])
                op(act, lambda: nc.scalar.copy(out=t2, in_=accv[1][:, :, 0:128]), reads=accb[1], writes=[b("tmp1")])
                op(dve, lambda: nc.vector.tensor_scalar(out=rl[:, 1, :], in0=rl[:, 1, :], scalar1=sc(NLAM), scalar2=None,
                                                        op0=ALU.mult),
                   reads=[b("rl1"), b("nlam")], writes=[b("rl1")])
                op(dve, lambda: nc.vector.tensor_tensor(out=t1, in0=t1,
                                                        in1=rl[:, 0, :].unsqueeze(2).broadcast_to([P, 4, P]), op=ALU.mult),
                   reads=[b("tmp0"), b("rl0")], writes=[b("tmp0")])
                op(dve, lambda: nc.vector.tensor_tensor(out=t2, in0=t2,
                                                        in1=rl[:, 1, :].unsqueeze(2).broadcast_to([P, 4, P]), op=ALU.mult),
                   reads=[b("tmp1"), b("rl1")], writes=[b("tmp1")])
                op(dve, lambda: nc.vector.tensor_tensor(out=o3, in0=t1, in1=t2, op=ALU.add),
                   reads=[b("tmp0"), b("tmp1")], writes=[b("tmp2")])
                op(dve, lambda: nc.vector.tensor_tensor(out=t1, in0=o3, in1=o3, op=ALU.mult),
                   reads=[b("tmp2")], writes=[b("tmp0")])
                op(dve, lambda: nc.vector.tensor_reduce(out=ssr, in_=t1, axis=AX.X, op=ALU.add),
                   reads=[b("tmp0")], writes=[b("ssr")])
                op(act, lambda: nc.scalar.activation(out=ssr, in_=ssr, func=AF.Ln, bias=sc(EPSC), scale=1.0 / 128),
                   reads=[b("ssr"), b("epsc")], writes=[b("ssr")])
                op(act, lambda: nc.scalar.activation(out=ssr, in_=ssr, func=AF.Exp, scale=-0.5), reads=[b("ssr")], writes=[b("ssr")])
                op(dve, lambda: nc.vector.tensor_tensor(out=o3, in0=o3, in1=ssr.unsqueeze(2).broadcast_to([P, 4, P]), op=ALU.mult),
                   reads=[b("tmp2"), b("ssr")], writes=[b("tmp2")])
                op(dve, lambda: nc.vector.tensor_tensor(out=obf, in0=o3, in1=sg.unsqueeze(1).broadcast_to([P, 4, P]), op=ALU.mult),
                   reads=[b("tmp2"), b("sg")], writes=[b("obf")])

            def finalize_b(h):
                for ql in range(4):
                    op(pe, lambda ql=ql: nc.tensor.transpose(out=pT[:, ql * P:(ql + 1) * P], in_=obf[:, ql, :], identity=ident),
                       reads=[b("obf"), b("ident")], writes=[pb[0]], signal=(ql == 3))
                op(act, lambda: nc.scalar.copy(out=UO[:, 4 + h, :], in_=pT[:, 0:TG]), reads=[pb[0]], writes=[b("UO")])

            deferred = None
            for h in range(4):
                emit_qk(h, 0, 0)
                emit_qk(h, 0, 1)
                for kt in range(nkt):
                    if kt + 1 < nkt:
                        emit_qk(h, kt + 1, 0)
                        emit_qk(h, kt + 1, 1)
                    emit_exp(h, kt, 0)
                    emit_exp(h, kt, 1)
                    if h == 0 and kt < 4:
                        if kt == 0:
                            ln_a2()
                        else:
                            ln_a3(kt - 1)
                        ln_d2(kt)
                    emit_av(h, kt, 0)
                    emit_av(h, kt, 1)
                    if kt == 2 and deferred is not None:
                        finalize_b(deferred)
                        deferred = None
                if h == 0:
                    ln_a3(3)
                if h == 3 and G + 2 < NG:
                    for T in (4 * G + 8, 4 * G + 9):
                        norm_chain(T, src_d, "xs%d" % T)
                if h == 0 and nxt:
                    norm_T(4 * G + 6)
                    norm_T(4 * G + 7)
                finalize_a(h)
                deferred = h
                if h == 3 and nxt:
                    halo_copy()
                    for ct in range(4):
                        glu_step(ct)
            finalize_b(deferred)
            def wout_tile(tt):
                T = G * 4 + tt
                mbanks = (4, 5) if tt % 2 == 0 else (6, 7)
                for half in range(2):
                    mm_group(bank(mbanks[half]),
                             [(UO[:, cc, tt * P:(tt + 1) * P], w_out[:, cc, half * 512:(half + 1) * 512]) for cc in range(NK)],
                             [b("UO"), b("w_out0"), b("w_out1")], pb[mbanks[half]])
                post_norm_residual(T, mbanks, src_d[T * P:(T + 1) * P, :], xs_d[T * P:(T + 1) * P, :], "xs%d" % T, "xs%d" % T)

            wout_tile(0)
            wout_tile(1)
            wout_tile(2)
            wout_tile(3)

        phase[0] = "ffn"
        dma(sp, ds("g1"), g1, fvec_d[l, 2:3, :].broadcast_to([P, D]), writes=[b("g1")])
        dma(sp, ds("g2"), g2, fvec_d[l, 3:4, :].broadcast_to([P, D]), writes=[b("g2")])
        first = True
        for blk in range(6):
            w_ = min(512, DFF - blk * 512)
            for which in range(2):
                c0 = which * DFF + blk * 512
                dma(pool, ds("w_up%d%d" % (which, blk)), w_up[:, :, c0:c0 + w_],
                    w_up_d[l, :, c0:c0 + w_].rearrange("(k p) c -> p k c", p=P),
                    writes=[b(("wu_g%d" if which == 0 else "wu_v%d") % blk)] + (mixer_bufs if first else []))
                first = False
        for gq in range(6):
            j0 = gq * 4
            j1 = min(j0 + 4, NJ)
            dma(pool, ds("w_dn%d" % gq), w_dn[:, j0:j1, :],
                w_dn_d[l, j0 * P:j1 * P, :].rearrange("(j p) c -> p j c", p=P), writes=[b("wd%d" % gq)])
        op(dve, lambda: nc.vector.memset(halo, 0.0), writes=[b("halo")])
        dst_d = out_d if l == L - 1 else xs_d
        norm_chain(0, xs_d, "xs0")
        norm_chain(1, xs_d, "xs1")
        norm_T(0)
        norm_T(1)
        norm_chain(2, xs_d, "xs2")
        norm_chain(3, xs_d, "xs3")
        norm_T(2)
        norm_T(3)
        for G in range(NG):
            nxt = G + 1 < NG
            if nxt and G == 0:
                for T in (4, 5):
                    norm_chain(T, xs_d, "xs%d" % T)
            ubank = {}

            def up_tile(n):
                jp, which = n // 2, n % 2
                j = which * NJ + jp
                col0 = which * DFF + jp * P
                s_ = n % 3
                bu = 1 + n % 3
                ubank[n] = bu
                wb_ = b(("wu_g%d" if which == 0 else "wu_v%d") % (jp // 4))
                mm_group(bank(bu), [(w_up[:, kc, col0:col0 + P], hT[:, kc, :]) for kc in range(NK)],
                         [wb_] + hT_bufs, pb[bu])
                rbuf = b("R%d" % s_)
                op(dve, lambda: nc.vector.tensor_copy(out=Rb[s_][:, 0:2], in_=halo[:, j, :]),
                   reads=[b("halo")], writes=[rbuf])
                op(act, lambda: nc.scalar.copy(out=Rb[s_][:, 2:2 + TG], in_=bank(bu)), reads=[pb[bu]], writes=[rbuf])
                op(dve, lambda: nc.vector.tensor_copy(out=halo[:, j, :], in_=Rb[s_][:, TG:TG + 2]),
                   reads=[rbuf], writes=[b("halo")])
                base_ = 4 if jp % 2 == 0 else 6
                bcv = base_ + which
                if which == 0:
                    op(act, lambda: nc.scalar.activation(out=bank(bcv), in_=bank(bu), func=AF.Identity,
                                                         scale=pv[:, PV_FW + j * 3 + 2:PV_FW + j * 3 + 3],
                                                         bias=pv[:, PV_FB + j:PV_FB + j + 1]),
                       reads=[pb[bu], b("pv")], writes=[pb[bcv]])
                else:
                    fw = pv[:, PV_FW + j * 3:PV_FW + j * 3 + 3]
                    op(pool, lambda: nc.gpsimd.tensor_tensor(out=D3[jp % 2], in0=identf.unsqueeze(1).broadcast_to([P, 3, P]),
                                                             in1=fw.unsqueeze(2).broadcast_to([P, 3, P]), op=ALU.mult),
                       reads=[b("identf"), b("pv")], writes=[b("D3%d" % (jp % 2))])

            def conv_tile(n):
                jp, which = n // 2, n % 2
                j = which * NJ + jp
                s_ = n % 3
                base_ = 4 if jp % 2 == 0 else 6
                bcv = base_ + which
                rbuf = b("R%d" % s_)
                if which == 0:
                    for k in (1, 0):
                        op(dve, lambda k=k: nc.vector.scalar_tensor_tensor(out=bank(bcv), in0=Rb[s_][:, k:k + TG],
                                                                           scalar=pv[:, PV_FW + j * 3 + k:PV_FW + j * 3 + k + 1],
                                                                           in1=bank(bcv), op0=ALU.mult, op1=ALU.add),
                           reads=[rbuf, pb[bcv], b("pv")], writes=[pb[bcv]])
                    gl = tmps[:, 2 + jp % 2, :]
                    glb = b("tmp%d" % (2 + jp % 2))
                    op(act, lambda: nc.scalar.activation(out=gl, in_=bank(base_), func=AF.Gelu_apprx_tanh),
                       reads=[pb[base_]], writes=[glb])
                else:
                    mm_group(bank(bcv), [(D3[jp % 2][:, k, :], Rb[s_][:, k:k + TG]) for k in range(3)],
                             [b("D3%d" % (jp % 2)), rbuf], pb[bcv])
                    gl = tmps[:, 2 + jp % 2, :]
                    glb = b("tmp%d" % (2 + jp % 2))
                    op(dve, lambda: nc.vector.scalar_tensor_tensor(out=HID[:, jp, :], in0=bank(bcv),
                                                                   scalar=pv[:, PV_FB + j:PV_FB + j + 1],
                                                                   in1=gl, op0=ALU.add, op1=ALU.mult),
                       reads=[pb[bcv], glb, b("pv")], writes=[b("HID")])

            def down_tile(tt):
                T = G * 4 + tt
                mbanks = (6, 7) if tt % 2 == 0 else (4, 5)
                for half in range(2):
                    for jp in range(NJ):
                        op(pe, lambda jp=jp, half=half: nc.tensor.matmul(out=bank(mbanks[half]), lhsT=HID[:, jp, tt * P:(tt + 1) * P],
                                                                         rhs=w_dn[:, jp, half * 512:(half + 1) * 512],
                                                                         start=(jp == 0), stop=(jp == NJ - 1)),
                           reads=[b("HID"), b("wd%d" % (jp // 4))], writes=[pb[mbanks[half]]], signal=(jp == NJ - 1))
                post_norm_residual(T, mbanks, xs_d[T * P:(T + 1) * P, :], dst_d[T * P:(T + 1) * P, :],
                                   "xs%d" % T, ("out%d" % T) if l == L - 1 else ("xs%d" % T))

            NTL = 2 * NJ
            up_tile(0)
            for n in range(NTL):
                if n + 1 < NTL:
                    up_tile(n + 1)
                conv_tile(n)
            if nxt:
                norm_T(4 * G + 4)
                norm_T(4 * G + 5)
                for T in (4 * G + 6, 4 * G + 7):
                    norm_chain(T, xs_d, "xs%d" % T)
            down_tile(0)
            down_tile(1)
            if nxt:
                norm_T(4 * G + 6)
                norm_T(4 * G + 7)
            if G + 2 < NG:
                for T in (4 * G + 8, 4 * G + 9):
                    norm_chain(T, xs_d, "xs%d" % T)
            down_tile(2)
            down_tile(3)

    for nm in ("xst0", "xst1"):
        sp.eng.wait_ge(dsem[nm].sem, dsem[nm].cnt)
    stats = {q.name: (q.nins, q.nwait) for q in (pe, act, dve, pool, sp)}
    print("instr/wait counts:", stats)
    return nc


def _consts(S):
    ident = np.eye(P, dtype=np.float32)
    pmat = np.zeros((P, P), np.float32)
    for base in (0, 64):
        for i in range(8):
            pmat[base + i + 8, base + i] = -1.0
            pmat[base + i, base + i + 8] = 1.0
    mask = (np.arange(P)[:, None] <= np.arange(P)[None, :]).astype(np.float32)
    pos = np.arange(S, dtype=np.float32)
    inv_freq = (np.float32(500000.0) ** (-np.arange(0, 16, 2, dtype=np.float32) / np.float32(16))).astype(np.float32)
    ang = (pos[:, None] * inv_freq[None, :]).astype(np.float32)
    cos = np.cos(ang).astype(np.float32)
    sin = np.sin(ang).astype(np.float32)
    C = np.ones((P, S), np.float32)
    Sn = np.zeros((P, S), np.float32)
    for base in (0, 64):
        for i in range(16):
            C[base + i] = cos[:, i % 8]
            Sn[base + i] = sin[:, i % 8]
    return np.stack([ident, pmat, mask]), np.stack([C, Sn])


def _pack_small(inp, L):
    pvec = np.zeros((L, P, NPV), np.float32)
    cw = np.asarray(inp["conv_w"], np.float32)
    pvec[:, :, PV_CW:PV_CW + 124] = cw.reshape(L, 31, 4, P).transpose(0, 3, 2, 1).reshape(L, P, 124)
    for name, off in (("conv_b", PV_CB), ("conv_ln_g", PV_LG), ("conv_ln_b", PV_LB)):
        pvec[:, :, off:off + 4] = np.asarray(inp[name], np.float32).reshape(L, 4, P).transpose(0, 2, 1)
    fw = np.asarray(inp["ffn_conv_w"], np.float32)
    pvec[:, :, PV_FW:PV_FW + 132] = fw.reshape(L, 3, 44, P).transpose(0, 3, 2, 1).reshape(L, P, 132)
    pvec[:, :, PV_FB:PV_FB + 44] = np.asarray(inp["ffn_conv_b"], np.float32).reshape(L, 44, P).transpose(0, 2, 1)
    fvec = np.stack([np.asarray(inp[k], np.float32) for k in
                     ("pre_mix_norm", "post_mix_norm", "pre_ffn_norm", "post_ffn_norm")], axis=1)
    lamv = np.concatenate([np.asarray(inp[k], np.float32) for k in
                           ("lambda_q1", "lambda_k1", "lambda_q2", "lambda_k2")], axis=1).reshape(L, 1, 256)
    sgv = np.asarray(inp["subln_g"], np.float32).reshape(L, 1, P)
    return pvec, np.ascontiguousarray(fvec), np.ascontiguousarray(lamv), sgv


_PROG_CACHE = {}


def kernel(**inputs):
    x = np.asarray(inputs["x"], np.float32)
    Bsz, S, _ = x.shape
    L = np.asarray(inputs["w_in"]).shape[0]
    lam_inits = [0.8 - 0.6 * math.exp(-0.3 * l) for l in range(L)]
    key = (L, S)
    if key not in _PROG_CACHE:
        _PROG_CACHE[key] = build_program(L, S, lam_inits)
    nc = _PROG_CACHE[key]
    cmat, rope = _consts(S)
    pvec, fvec, lamv, sgv = _pack_small(inputs, L)
    shared = {
        "w_in": np.ascontiguousarray(inputs["w_in"], np.float32),
        "w_out": np.ascontiguousarray(inputs["w_out"], np.float32),
        "w_up": np.ascontiguousarray(inputs["w_up"], np.float32),
        "w_down": np.ascontiguousarray(inputs["w_down"], np.float32),
        "pvec": pvec, "fvec": fvec, "lamv": lamv, "sublng": sgv, "cmat": cmat, "rope": rope,
    }
    in_maps = []
    for c in range(Bsz):
        m = dict(shared)
        m["x"] = np.ascontiguousarray(x[c])
        in_maps.append(m)
    res = run_bass_kernel_spmd(nc, in_maps, core_ids=list(range(Bsz)))
    return np.stack([np.asarray(r["out"], np.float32) for r in res.results], axis=0)
```

```python
import math
import numpy as np
import concourse.bass as bass
import concourse.mybir as mybir
from concourse.bass_utils import run_bass_kernel_spmd

F32 = mybir.dt.float32
BF16 = mybir.dt.bfloat16
AF = mybir.ActivationFunctionType
ALU = mybir.AluOpType
AX = mybir.AxisListType

P = 128
D = 1024
NK = 8
CIN = 2560
DFF = 2816
NJ = 22
CUP = 5632
TG = 512
EPS = 1e-6
CONV_K = 31
PV_CW = 0
PV_CB = PV_CW + 4 * 31
PV_LG = PV_CB + 4
PV_LB = PV_LG + 4
PV_FW = PV_LB + 4
PV_FB = PV_FW + 44 * 3
NPV = PV_FB + 44


class Buf:
    __slots__ = ("name", "w", "r", "excl")

    def __init__(self, name, excl=False):
        self.name = name
        self.w = None
        self.r = {}
        self.excl = excl


class Holder:
    def __init__(self, nc, name):
        self.name = name
        self.sem = nc.alloc_semaphore(name)
        self.cnt = 0


class Queue(Holder):
    def __init__(self, nc, eng, name):
        super().__init__(nc, "q_" + name)
        self.eng = eng
        self.seen = {}
        self.nwait = 0
        self.nins = 0
        self.inorder = False

    def wait_tok(self, holder, val):
        if holder is self and self.inorder:
            return
        if self.seen.get(holder, 0) >= val:
            return
        assert val <= holder.cnt, (self.name, holder.name, val, holder.cnt)
        self.eng.wait_ge(holder.sem, val)
        self.seen[holder] = val
        self.nwait += 1


def _deps(q, reads, writes):
    toks = {}
    for b in reads:
        if b.w is not None:
            h, v = b.w
            if toks.get(h, 0) < v:
                toks[h] = v
        if b.excl:
            for h, v in b.r.items():
                if h is not q and toks.get(h, 0) < v:
                    toks[h] = v
    for b in writes:
        if b.w is not None:
            h, v = b.w
            if toks.get(h, 0) < v:
                toks[h] = v
        for h, v in b.r.items():
            if toks.get(h, 0) < v:
                toks[h] = v
    for h, v in toks.items():
        q.wait_tok(h, v)


def _record(tok, reads, writes):
    h, v = tok
    for b in reads:
        if b.r.get(h, 0) < v:
            b.r[h] = v
    for b in writes:
        b.w = tok
        b.r = {}


def op(q, ins_fn, reads=(), writes=(), signal=True):
    _deps(q, reads, writes)
    ins = ins_fn()
    q.nins += 1
    if signal:
        ins.then_inc(q.sem, 1)
        q.cnt += 1
        tok = (q, q.cnt)
    else:
        tok = (q, q.cnt + 1)
    _record(tok, reads, writes)
    return ins


def dma(q, dsem, out, in_, reads=(), writes=()):
    _deps(q, reads, writes)
    ins = q.eng.dma_start(out=out, in_=in_)
    ins.then_inc(dsem.sem, 16)
    dsem.cnt += 16
    _record((dsem, dsem.cnt), reads, writes)
    return ins


class Arena:
    def __init__(self, nc, nbytes):
        self.nbytes = nbytes
        self.t = nc.alloc_sbuf_tensor("arena", [P, nbytes // 2], BF16)

    def view(self, off, shape, dt):
        esz = 4 if dt == F32 else 2
        n = 1
        for d_ in shape:
            n *= d_
        nb = n * esz
        assert off % 4 == 0 and off + nb <= self.nbytes, (off, nb, self.nbytes)
        ap = self.t[:, off // 2:(off + nb) // 2]
        if dt == F32:
            ap = ap.bitcast(F32)
        if len(shape) == 2:
            ap = ap.rearrange("p (a b) -> p a b", a=shape[0])
        elif len(shape) == 3:
            ap = ap.rearrange("p (a b c) -> p a b c", a=shape[0], b=shape[1])
        return ap


class Layout:
    def __init__(self, arena, base=0):
        self.arena = arena
        self.off = base
        self.hi = base

    def get(self, shape, dt):
        esz = 4 if dt == F32 else 2
        n = esz
        for d_ in shape:
            n *= d_
        off = (self.off + 31) // 32 * 32
        self.off = off + n
        self.hi = max(self.hi, self.off)
        return self.arena.view(off, shape, dt)


def build_program(L, S, lam_inits):
    NG = S // TG
    NT = S // P
    nc = bass.Bass("TRN2", target_bir_lowering=False)

    def din(name, shape):
        return nc.dram_tensor(name, list(shape), F32, kind="ExternalInput")

    x_d = din("x", [S, D])
    w_in_d = din("w_in", [L, D, CIN])
    w_out_d = din("w_out", [L, D, D])
    w_up_d = din("w_up", [L, D, CUP])
    w_dn_d = din("w_down", [L, DFF, D])
    pvec_d = din("pvec", [L, P, NPV])
    fvec_d = din("fvec", [L, 4, D])
    lamv_d = din("lamv", [L, 1, 256])
    sg_d = din("sublng", [L, 1, P])
    cmat_d = din("cmat", [3, P, P])
    rope_d = din("rope", [2, P, S])
    out_d = nc.dram_tensor("out", [S, D], F32, kind="ExternalOutput")
    xs_d = nc.dram_tensor("xs_scratch", [S, D], F32)

    pe = Queue(nc, nc.tensor, "pe")
    pe.inorder = True
    act = Queue(nc, nc.scalar, "act")
    dve = Queue(nc, nc.vector, "dve")
    pool = Queue(nc, nc.gpsimd, "pool")
    sp = Queue(nc, nc.sync, "sp")

    ARENA_BYTES = 212800
    arena = Arena(nc, ARENA_BYTES)
    com = Layout(arena, 0)
    ident = com.get([P], BF16)
    pm = com.get([P], BF16)
    cmask = com.get([P], BF16)
    ones = com.get([P], BF16)
    identf = com.get([P], F32)
    pv = com.get([NPV], F32)
    sg = com.get([P], F32)
    small = com.get([64], F32)
    g1 = com.get([D], F32)
    g2 = com.get([D], F32)
    xin = [com.get([D], F32) for _ in range(2)]
    xres = [com.get([D], F32)]
    hbf = [com.get([D], BF16) for _ in range(2)]
    hT = com.get([NK, TG], BF16)
    tmps = com.get([4, TG], F32)
    ropeCS = com.get([2, TG], F32)
    ropeC = ropeCS[:, 0, :]
    ropeS = ropeCS[:, 1, :]
    xres_rope = ropeCS.rearrange("p a b -> p (a b)")
    lamb = tmps[:, 3, 0:256]
    base_phase = (com.hi + 31) // 32 * 32
    mx = Layout(arena, base_phase)
    w_in = mx.get([NK, CIN], BF16)
    w_out = mx.get([NK, D], BF16)
    KT = mx.get([4, S], BF16)
    V = mx.get([NT, 4, 130], BF16)
    QT = mx.get([4, TG], BF16)
    Ubuf = mx.get([4, 30 + TG], BF16)
    UO = mx.get([NK, TG], BF16)
    ybf_off = (mx.off + 31) // 32 * 32
    ybf = mx.get([4, TG], BF16)
    xres_ybf = arena.view(ybf_off, [D], F32)
    ysq = mx.get([4, TG], BF16)
    rl = mx.get([2, 4], F32)
    ssr = mx.get([4], F32)
    xoff = (mx.off + 31) // 32 * 32
    Dg = mx.get([CONV_K, P], BF16)
    Pt = [[arena.view(xoff + (2 * s_ + c) * 1024, [TG], BF16) for s_ in range(2)] for c in range(2)]
    Ptpair = [arena.view(xoff + 2 * s_ * 1024, [2, TG], BF16) for s_ in range(2)]
    obf = arena.view(xoff + 4096, [4, P], BF16)
    zbf = arena.view(xoff + 5120, [TG], BF16)
    DgB = mx.get([CONV_K, P], BF16)
    ff = Layout(arena, base_phase)
    w_up = ff.get([NK, CUP], BF16)
    w_dn = ff.get([NJ, D], BF16)
    HID = ff.get([NJ, TG], BF16)
    Rb = [ff.get([2 + TG], BF16) for _ in range(3)]
    halo = ff.get([44, 2], BF16)
    D3 = [ff.get([3, P], BF16) for _ in range(2)]
    print("SBUF layout: common %d, mixer %d, ffn %d (limit %d)" % (com.hi, mx.hi, ff.hi, ARENA_BYTES))
    assert mx.hi <= ARENA_BYTES and ff.hi <= ARENA_BYTES

    psum = nc.alloc_psum_tensor("psum", [P, 8 * 512], F32)

    def bank(i, n=1):
        return psum[:, i * 512:(i + n) * 512]

    pT = bank(0).bitcast(BF16)

    B = {}

    def b(name):
        if name not in B:
            B[name] = Buf(name)
        return B[name]

    pb = [b("psum%d" % i) for i in range(8)]
    for x_ in pb:
        x_.excl = True
    WI_BLK = ["wi_a", "wi_g", "wi_q", "wi_k", "wi_v"]
    xalias = ["Pt00", "Pt01", "Pt10", "Pt11", "obf", "zbf"]
    mixer_names = WI_BLK + ["w_out0", "w_out1", "QT", "Ubuf", "Dg", "DgB", "UO", "ybf", "ysq", "rl0", "rl1", "ssr", "Vones"] + xalias + \
        ["KT%d" % g for g in range(NG)] + ["V%d" % g for g in range(NG)]
    ffn_names = ["wu_g%d" % i for i in range(6)] + ["wu_v%d" % i for i in range(6)] + ["wd%d" % i for i in range(6)] + \
        ["HID", "R0", "R1", "R2", "halo", "D30", "D31"]
    mixer_bufs = [b(n) for n in mixer_names]
    ffn_bufs = [b(n) for n in ffn_names]
    xalias_bufs = [b(n) for n in xalias]
    hT_bufs = [b("hT%d" % i) for i in range(4)]

    dsem = {}

    def ds(name):
        if name not in dsem:
            dsem[name] = Holder(nc, "d_" + name)
        return dsem[name]

    def sc(i):
        return small[:, i:i + 1]
    S1, S2, E1, E2, NLAM = range(5)
    SSQ0, RSTD0, PSSQ0, PRSTD0 = 8, 10, 12, 14
    EPSC = 16

    dma(pool, ds("c0"), ident, cmat_d[0], writes=[b("ident")])
    dma(pool, ds("c0b"), pm, cmat_d[1], writes=[b("pm")])
    dma(pool, ds("c0c"), cmask, cmat_d[2], writes=[b("cmask")])
    dma(sp, ds("c1"), identf, cmat_d[0], writes=[b("identf")])
    op(dve, lambda: nc.vector.memset(ones, 1.0), writes=[b("ones")])
    op(dve, lambda: nc.vector.memset(sc(EPSC), EPS), writes=[b("epsc")])

    def norm_chain(T, src_d, srcbuf):
        s_ = T % 2
        xb, hb = b("xin%d" % s_), b("hbf%d" % s_)
        sq, rs = b("ssq%d" % s_), b("rstd%d" % s_)
        dma(sp, ds("xin%d" % s_), xin[s_], src_d[T * P:(T + 1) * P, :], reads=[b(srcbuf)], writes=[xb])
        op(dve, lambda: nc.vector.memset(sc(SSQ0 + s_), 0.0), writes=[sq])
        op(act, lambda: nc.scalar.activation(out=hbf[s_], in_=xin[s_], func=AF.Square, accum_out=sc(SSQ0 + s_)),
           reads=[xb], writes=[hb, sq])
        op(act, lambda: nc.scalar.activation(out=sc(RSTD0 + s_), in_=sc(SSQ0 + s_), func=AF.Ln, bias=sc(EPSC), scale=1.0 / D),
           reads=[sq, b("epsc")], writes=[rs])
        op(act, lambda: nc.scalar.activation(out=sc(RSTD0 + s_), in_=sc(RSTD0 + s_), func=AF.Exp, scale=-0.5), reads=[rs], writes=[rs])
        op(dve, lambda: nc.vector.scalar_tensor_tensor(out=hbf[s_], in0=xin[s_], scalar=sc(RSTD0 + s_), in1=g1,
                                                       op0=ALU.mult, op1=ALU.mult),
           reads=[xb, rs, b("g1")], writes=[hb])

    def norm_T(T):
        s_ = T % 2
        tt = T % 4
        hb = b("hbf%d" % s_)
        for kc in range(NK):
            op(pe, lambda kc=kc: nc.tensor.transpose(out=pT[:, kc * P:(kc + 1) * P], in_=hbf[s_][:, kc * P:(kc + 1) * P],
                                                   identity=ident),
               reads=[hb, b("ident")], writes=[pb[0]], signal=(kc == NK - 1))
        op(dve, lambda: nc.vector.tensor_copy(out=hT[:, :, tt * P:(tt + 1) * P],
                                              in_=pT[:, 0:NK * P].rearrange("p (k c) -> p k c", k=NK)),
           reads=[pb[0]], writes=[hT_bufs[tt]])

    def mm_group(out_ap, pairs, reads, wbuf, start=True):
        n = len(pairs)
        for i, (l_, r_) in enumerate(pairs):
            op(pe, lambda l_=l_, r_=r_, i=i: nc.tensor.matmul(out=out_ap, lhsT=l_, rhs=r_, start=(start and i == 0),
                                                              stop=(i == n - 1)),
               reads=reads, writes=[wbuf], signal=(i == n - 1))

    def post_norm_residual(T, mbanks, src_ap, dst_ap, src_buf_name, dst_buf_name):
        s_ = T % 2
        m_ap = bank(mbanks[0], 2)
        mb = [pb[mbanks[0]], pb[mbanks[1]]]
        if s_ == 0:
            xr, xbs = xres[0], [b("xres0")]
        elif phase[0] == "mixer":
            xr, xbs = xres_ybf, [b("ybf")]
        else:
            xr, xbs = xres_rope, [b("ropeC"), b("ropeS")]
        sq, rs = b("pssq%d" % s_), b("prstd%d" % s_)
        dma(sp, ds("xres%d" % s_), xr, src_ap, reads=[b(src_buf_name)], writes=xbs)
        op(dve, lambda: nc.vector.memset(sc(PSSQ0 + s_), 0.0), writes=[sq])
        tmp32 = tmps[:, 2 * s_:2 * s_ + 2, :].rearrange("p a b -> p (a b)")
        tb = [b("tmp%d" % (2 * s_)), b("tmp%d" % (2 * s_ + 1))]
        op(act, lambda: nc.scalar.activation(out=tmp32, in_=m_ap, func=AF.Square, accum_out=sc(PSSQ0 + s_)),
           reads=mb, writes=tb + [sq])
        op(act, lambda: nc.scalar.activation(out=sc(PRSTD0 + s_), in_=sc(PSSQ0 + s_), func=AF.Ln, bias=sc(EPSC), scale=1.0 / D),
           reads=[sq, b("epsc")], writes=[rs])
        op(act, lambda: nc.scalar.activation(out=sc(PRSTD0 + s_), in_=sc(PRSTD0 + s_), func=AF.Exp, scale=-0.5), reads=[rs], writes=[rs])
        op(dve, lambda: nc.vector.scalar_tensor_tensor(out=tmp32, in0=m_ap, scalar=sc(PRSTD0 + s_), in1=g2,
                                                       op0=ALU.mult, op1=ALU.mult),
           reads=mb + [rs, b("g2")], writes=tb)
        op(dve, lambda: nc.vector.tensor_tensor(out=xr, in0=xr, in1=tmp32, op=ALU.add),
           reads=tb + xbs, writes=xbs)
        dma(sp, ds("xst%d" % s_), dst_ap, xr, reads=xbs, writes=[b(dst_buf_name)])

    rot = [1, 2, 3]
    rr = [0]
    phase = ["mixer"]
    srot = [0]

    def nbs():
        i = srot[0] % 4
        srot[0] += 1
        return i

    def nb():
        i = rot[rr[0] % len(rot)]
        rr[0] += 1
        return i

    for l in range(L):
        lam_init = lam_inits[l]
        src_d = x_d if l == 0 else xs_d
        dma(sp, ds("pv"), pv, pvec_d[l], writes=[b("pv")])
        dma(sp, ds("lamb"), lamb, lamv_d[l].broadcast_to([P, 256]), writes=[b("tmp3")])
        dma(sp, ds("sg"), sg, sg_d[l].broadcast_to([P, P]), writes=[b("sg")])
        dma(sp, ds("g1"), g1, fvec_d[l, 0:1, :].broadcast_to([P, D]), writes=[b("g1")])
        dma(sp, ds("g2"), g2, fvec_d[l, 1:2, :].broadcast_to([P, D]), writes=[b("g2")])
        jt = tmps[:, 2, 0:64]
        op(dve, lambda: nc.vector.scalar_tensor_tensor(out=jt, in0=lamb[:, 0:64], scalar=1.0, in1=lamb[:, 64:128],
                                                       op0=ALU.mult, op1=ALU.mult, accum_out=sc(S1)),
           reads=[b("tmp3")], writes=[b("tmp2"), b("s1")])
        op(dve, lambda: nc.vector.scalar_tensor_tensor(out=jt, in0=lamb[:, 128:192], scalar=1.0, in1=lamb[:, 192:256],
                                                       op0=ALU.mult, op1=ALU.mult, accum_out=sc(S2)),
           reads=[b("tmp3")], writes=[b("tmp2"), b("s2")])
        op(act, lambda: nc.scalar.activation(out=sc(E1), in_=sc(S1), func=AF.Exp), reads=[b("s1")], writes=[b("e1")])
        op(act, lambda: nc.scalar.activation(out=sc(E2), in_=sc(S2), func=AF.Exp), reads=[b("s2")], writes=[b("e2")])
        op(dve, lambda: nc.vector.tensor_tensor(out=sc(NLAM), in0=sc(E2), in1=sc(E1), op=ALU.subtract),
           reads=[b("e1"), b("e2")], writes=[b("nlam")])
        op(dve, lambda: nc.vector.tensor_scalar(out=sc(NLAM), in0=sc(NLAM), scalar1=-float(lam_init), scalar2=None,
                                                op0=ALU.add),
           reads=[b("nlam")], writes=[b("nlam")])
        op(act, lambda: nc.scalar.mul(out=sg, in_=sg, mul=float(1.0 - lam_init)), reads=[b("sg")], writes=[b("sg")])

        first = True
        for blk in (1, 0, 2, 3, 4):
            c0 = blk * 512
            dma(pool, ds("w_in%d" % blk), w_in[:, :, c0:c0 + 512],
                w_in_d[l, :, c0:c0 + 512].rearrange("(k p) c -> p k c", p=P),
                writes=[b(WI_BLK[blk])] + (ffn_bufs if first else []))
            first = False
        for hf in range(2):
            dma(pool, ds("w_out%d" % hf), w_out[:, hf * 4:(hf + 1) * 4, :],
                w_out_d[l, hf * 512:(hf + 1) * 512, :].rearrange("(k p) c -> p k c", p=P), writes=[b("w_out%d" % hf)])
        op(pool, lambda: nc.gpsimd.memset(Ubuf[:, :, 0:30], 0.0), writes=[b("Ubuf")])
        op(pool, lambda: nc.gpsimd.memset(V[:, :, :, 128:130], 1.0), writes=[b("Vones")])

        phase[0] = "mixer"
        for T in range(4):
            norm_chain(T, src_d, "xs%d" % T) if T < 2 else None
        norm_T(0)
        norm_T(1)
        norm_chain(2, src_d, "xs2")
        norm_chain(3, src_d, "xs3")
        norm_T(2)
        norm_T(3)
        for G in range(NG):
            t0 = G * TG
            nxt = G + 1 < NG
            dma(sp, ds("ropeC"), ropeC, rope_d[0, :, t0:t0 + TG], writes=[b("ropeC")])
            dma(sp, ds("ropeS"), ropeS, rope_d[1, :, t0:t0 + TG], writes=[b("ropeS")])
            if nxt and G == 0:
                for T in (4, 5):
                    norm_chain(T, src_d, "xs%d" % T)

            def proj(col0, bi):
                mm_group(bank(bi), [(w_in[:, kc, col0:col0 + P], hT[:, kc, :]) for kc in range(NK)],
                         [b(WI_BLK[col0 // 512])] + hT_bufs, pb[bi])

            def dg_gen(ct):
                wv = pv[:, PV_CW + ct * CONV_K:PV_CW + (ct + 1) * CONV_K]
                dgt, dgb = (DgB, b("DgB")) if ct % 2 == 0 else (Dg, b("Dg"))
                op(pool, lambda: nc.gpsimd.tensor_tensor(out=dgt, in0=identf.unsqueeze(1).broadcast_to([P, CONV_K, P]),
                                                         in1=wv.unsqueeze(2).broadcast_to([P, CONV_K, P]), op=ALU.mult),
                   reads=[b("identf"), b("pv")], writes=[dgb] + (xalias_bufs if ct % 2 == 1 else []))

            def glu_step(ct):
                bg = nb()
                proj(512 + ct * P, bg)
                sgt = tmps[:, 3, :]
                sgb = b("tmp3")
                op(act, lambda: nc.scalar.activation(out=sgt, in_=bank(bg), func=AF.Sigmoid), reads=[pb[bg]], writes=[sgb])
                ba = nb()
                proj(ct * P, ba)
                op(dve, lambda: nc.vector.tensor_tensor(out=Ubuf[:, ct, 30:30 + TG], in0=bank(ba), in1=sgt, op=ALU.mult),
                   reads=[pb[ba], sgb], writes=[b("Ubuf")])

            def halo_copy():
                op(pool, lambda: nc.gpsimd.tensor_copy(out=Ubuf[:, :, 0:30], in_=Ubuf[:, :, TG:TG + 30]),
                   reads=[b("Ubuf")], writes=[b("Ubuf")])

            dg_gen(0)
            if G == 0:
                for ct in range(4):
                    glu_step(ct)
            for which in range(2):
                for h in range(4):
                    col0 = 1024 + which * 512 + h * P
                    bz = nb()
                    proj(col0, bz)
                    op(act, lambda bz=bz: nc.scalar.copy(out=zbf, in_=bank(bz)), reads=[pb[bz]], writes=[b("zbf")])
                    bp = nb()
                    mm_group(bank(bp), [(pm, zbf)], [b("pm"), b("zbf")], pb[bp])
                    op(dve, lambda bz=bz: nc.vector.tensor_tensor(out=tmps[:, 2, :], in0=bank(bz), in1=ropeC, op=ALU.mult),
                       reads=[pb[bz], b("ropeC")], writes=[b("tmp2")])
                    op(dve, lambda bp=bp: nc.vector.tensor_tensor(out=tmps[:, 3, :], in0=bank(bp), in1=ropeS, op=ALU.mult),
                       reads=[pb[bp], b("ropeS")], writes=[b("tmp3")])
                    if which == 0:
                        dst, dbuf = QT[:, h, :], b("QT")
                    else:
                        dst, dbuf = KT[:, h, t0:t0 + TG], b("KT%d" % G)
                    op(dve, lambda dst=dst: nc.vector.tensor_tensor(out=dst, in0=tmps[:, 2, :], in1=tmps[:, 3, :], op=ALU.add),
                       reads=[b("tmp2"), b("tmp3")], writes=[dbuf])
            for tt in range(4):
                T = G * 4 + tt
                bv = nb()
                mm_group(bank(bv), [(hT[:, kc, tt * P:(tt + 1) * P], w_in[:, kc, 2048:2560]) for kc in range(NK)],
                         [b("wi_v"), hT_bufs[tt]], pb[bv])
                op(act, lambda bv=bv, T=T: nc.scalar.copy(out=V[:, T, :, 0:128],
                                                          in_=bank(bv).rearrange("p (h c) -> p h c", h=4)),
                   reads=[pb[bv]], writes=[b("V%d" % G)])
            if nxt:
                norm_T(4 * G + 4)
                norm_T(4 * G + 5)
                for T in (4 * G + 6, 4 * G + 7):
                    norm_chain(T, src_d, "xs%d" % T)
            dg_gen(1)
            for ct in range(4):
                dgt, dgb = (DgB, b("DgB")) if ct % 2 == 0 else (Dg, b("Dg"))
                bc = nb()
                mm_group(bank(bc), [(dgt[:, k, :], Ubuf[:, ct, k:k + TG]) for k in range(CONV_K)],
                         [dgb, b("Ubuf")], pb[bc])
                if ct + 2 < 4:
                    dg_gen(ct + 2)
                cb = pv[:, PV_CB + ct:PV_CB + ct + 1]
                op(act, lambda bc=bc, ct=ct, cb=cb: nc.scalar.activation(out=ybf[:, ct, :], in_=bank(bc), func=AF.Identity, bias=cb),
                   reads=[pb[bc], b("pv")], writes=[b("ybf")])
                op(act, lambda bc=bc, ct=ct, cb=cb: nc.scalar.activation(out=ysq[:, ct, :], in_=bank(bc), func=AF.Square, bias=cb),
                   reads=[pb[bc], b("pv")], writes=[b("ysq")])
            b1 = nb()
            mm_group(bank(b1), [(ones, ybf[:, ct, :]) for ct in range(4)], [b("ones"), b("ybf")], pb[b1])
            b2 = nb()
            mm_group(bank(b2), [(ones, ysq[:, ct, :]) for ct in range(4)], [b("ones"), b("ysq")], pb[b2])
            mean, msq, rstd_t = tmps[:, 0, :], tmps[:, 1, :], tmps[:, 2, :]
            op(act, lambda: nc.scalar.mul(out=mean, in_=bank(b1), mul=1.0 / 512), reads=[pb[b1]], writes=[b("tmp0")])
            op(act, lambda: nc.scalar.activation(out=msq, in_=bank(b1), func=AF.Square, scale=1.0 / 512),
               reads=[pb[b1]], writes=[b("tmp1")])
            op(dve, lambda: nc.vector.scalar_tensor_tensor(out=rstd_t, in0=bank(b2), scalar=1.0 / 512, in1=msq,
                                                           op0=ALU.mult, op1=ALU.subtract),
               reads=[pb[b2], b("tmp1")], writes=[b("tmp2")])
            t3 = tmps[:, 3, :]

            def ln_a2():
                op(act, lambda: nc.scalar.activation(out=rstd_t, in_=rstd_t, func=AF.Ln, bias=sc(EPSC), scale=1.0),
                   reads=[b("tmp2"), b("epsc")], writes=[b("tmp2")])
                op(act, lambda: nc.scalar.activation(out=rstd_t, in_=rstd_t, func=AF.Exp, scale=-0.5), reads=[b("tmp2")], writes=[b("tmp2")])

            def ln_d2(ct):
                op(dve, lambda: nc.vector.tensor_tensor(out=t3, in0=ybf[:, ct, :], in1=mean, op=ALU.subtract),
                   reads=[b("ybf"), b("tmp0")], writes=[b("tmp3")])
                op(dve, lambda: nc.vector.tensor_tensor(out=t3, in0=t3, in1=rstd_t, op=ALU.mult),
                   reads=[b("tmp3"), b("tmp2")], writes=[b("tmp3")])

            def ln_a3(ct):
                op(act, lambda: nc.scalar.activation(out=UO[:, ct, :], in_=t3, func=AF.Silu,
                                                     scale=pv[:, PV_LG + ct:PV_LG + ct + 1],
                                                     bias=pv[:, PV_LB + ct:PV_LB + ct + 1]),
                   reads=[b("tmp3"), b("pv")], writes=[b("UO")])

            accv = [psum[:, (4 + 2 * c) * 512:(6 + 2 * c) * 512].rearrange("p (q w) -> p q w", q=4) for c in range(2)]
            accb = [[pb[4], pb[5]], [pb[6], pb[7]]]
            nkt = 4 * G + 4
            sbank = {}

            def emit_qk(h, kt, c):
                r = kt - 4 * G
                c0 = max(r, 0) * P
                if c == 0:
                    pr = srot[0] % 2
                    srot[0] += 1
                    sbank[kt] = 2 * pr
                sbk = sbank[kt] + c
                mm_group(bank(sbk)[:, c0:TG],
                         [(KT[c * 64:(c + 1) * 64, h, kt * P:(kt + 1) * P], QT[c * 64:(c + 1) * 64, h, c0:TG])],
                         [b("KT%d" % (kt // 4)), b("QT")], pb[sbk])

            def emit_exp(h, kt, c):
                if c == 1:
                    return
                r = kt - 4 * G
                c0 = max(r, 0) * P
                sb0 = sbank[kt]
                sl = kt % 2
                ptbs = [b("Pt0%d" % sl), b("Pt1%d" % sl)]
                src = psum[:, sb0 * 512:(sb0 + 2) * 512].rearrange("p (c w) -> p c w", c=2)[:, :, c0:TG]
                dst = Ptpair[sl][:, :, c0:TG]
                op(act, lambda: nc.scalar.activation(out=dst, in_=src, func=AF.Exp, scale=0.125),
                   reads=[pb[sb0], pb[sb0 + 1]], writes=ptbs)
                if r >= 0:
                    dm = Ptpair[sl][:, :, c0:c0 + P]
                    op(dve, lambda: nc.vector.tensor_tensor(out=dm, in0=dm, in1=cmask.unsqueeze(1).broadcast_to([P, 2, P]),
                                                            op=ALU.mult),
                       reads=ptbs + [b("cmask")], writes=ptbs)

            def emit_av(h, kt, c):
                r = kt - 4 * G
                q0 = max(r, 0)
                sl = kt % 2
                ptb = b("Pt%d%d" % (c, sl))
                pt = Pt[c][sl]
                for ql in range(q0, 4):
                    bk = 4 + 2 * c + ql // 2
                    reg = psum[:, bk * 512 + (ql % 2) * 256: bk * 512 + (ql % 2) * 256 + 129]
                    first_ = (kt == 0 and ql % 2 == 0)
                    last_ = (kt == 4 * G + ql)
                    op(pe, lambda reg=reg, ql=ql, first_=first_, last_=last_:
                       nc.tensor.matmul(out=reg, lhsT=pt[:, ql * P:(ql + 1) * P], rhs=V[:, kt, h, 0:129],
                                        start=first_, stop=last_, skip_group_check=True),
                       reads=[ptb, b("V%d" % (kt // 4)), b("Vones")], writes=[pb[bk]], signal=(ql == 3))

            def finalize_a(h):
                t1 = tmps[:, 0, :].rearrange("p (q w) -> p q w", q=4)
                t2 = tmps[:, 1, :].rearrange("p (q w) -> p q w", q=4)
                o3 = tmps[:, 2, :].rearrange("p (q w) -> p q w", q=4)
                for c in range(2):
                    op(dve, lambda c=c: nc.vector.reciprocal(out=rl[:, c, :], in_=accv[c][:, :, 128]),
                       reads=accb[c], writes=[b("rl%d" % c)])
                op(dve, lambda: nc.vector.tensor_copy(out=t1, in_=accv[0][:, :, 0:128]), reads=accb[0], writes=[b("tmp0")])
                op(act, lambda: nc.scalar.copy(out=t2, in_=accv[1][:, :, 0:128]), reads=accb[1], writes=[b("tmp1")])
                op(dve, lambda: nc.vector.tensor_scalar(out=rl[:, 1, :], in0=rl[:, 1, :], scalar1=sc(NLAM), scalar2=None,
                                                        op0=ALU.mult),
                   reads=[b("rl1"), b("nlam")], writes=[b("rl1")])
                op(dve, lambda: nc.vector.tensor_tensor(out=t1, in0=t1,
                                                        in1=rl[:, 0, :].unsqueeze(2).broadcast_to([P, 4, P]), op=ALU.mult),
                   reads=[b("tmp0"), b("rl0")], writes=[b("tmp0")])
                op(dve, lambda: nc.vector.tensor_tensor(out=t2, in0=t2,
                                                        in1=rl[:, 1, :].unsqueeze(2).broadcast_to([P, 4, P]), op=ALU.mult),
                   reads=[b("tmp1"), b("rl1")], writes=[b("tmp1")])
                op(dve, lambda: nc.vector.tensor_tensor(out=o3, in0=t1, in1=t2, op=ALU.add),
                   reads=[b("tmp0"), b("tmp1")], writes=[b("tmp2")])
                op(dve, lambda: nc.vector.tensor_tensor(out=t1, in0=o3, in1=o3, op=ALU.mult),
                   reads=[b("tmp2")], writes=[b("tmp0")])
                op(dve, lambda: nc.vector.tensor_reduce(out=ssr, in_=t1, axis=AX.X, op=ALU.add),
                   reads=[b("tmp0")], writes=[b("ssr")])
                op(act, lambda: nc.scalar.activation(out=ssr, in_=ssr, func=AF.Ln, bias=sc(EPSC), scale=1.0 / 128),
                   reads=[b("ssr"), b("epsc")], writes=[b("ssr")])
                op(act, lambda: nc.scalar.activation(out=ssr, in_=ssr, func=AF.Exp, scale=-0.5), reads=[b("ssr")], writes=[b("ssr")])
                op(dve, lambda: nc.vector.tensor_tensor(out=o3, in0=o3, in1=ssr.unsqueeze(2).broadcast_to([P, 4, P]), op=ALU.mult),
                   reads=[b("tmp2"), b("ssr")], writes=[b("tmp2")])
                op(dve, lambda: nc.vector.tensor_tensor(out=obf, in0=o3, in1=sg.unsqueeze(1).broadcast_to([P, 4, P]), op=ALU.mult),
                   reads=[b("tmp2"), b("sg")], writes=[b("obf")])

            def finalize_b(h):
                for ql in range(4):
                    op(pe, lambda ql=ql: nc.tensor.transpose(out=pT[:, ql * P:(ql + 1) * P], in_=obf[:, ql, :], identity=ident),
                       reads=[b("obf"), b("ident")], writes=[pb[0]], signal=(ql == 3))
                op(act, lambda: nc.scalar.copy(out=UO[:, 4 + h, :], in_=pT[:, 0:TG]), reads=[pb[0]], writes=[b("UO")])

            deferred = None
            for h in range(4):
                emit_qk(h, 0, 0)
                emit_qk(h, 0, 1)
                for kt in range(nkt):
                    if kt + 1 < nkt:
                        emit_qk(h, kt + 1, 0)
                        emit_qk(h, kt + 1, 1)
                    emit_exp(h, kt, 0)
                    emit_exp(h, kt, 1)
                    if h == 0 and kt < 4:
                        if kt == 0:
                            ln_a2()
                        else:
                            ln_a3(kt - 1)
                        ln_d2(kt)
                    emit_av(h, kt, 0)
                    emit_av(h, kt, 1)
                    if kt == 2 and deferred is not None:
                        finalize_b(deferred)
                        deferred = None
                if h == 0:
                    ln_a3(3)
                if h == 3 and G + 2 < NG:
                    for T in (4 * G + 8, 4 * G + 9):
                        norm_chain(T, src_d, "xs%d" % T)
                if h == 3 and nxt:
                    halo_copy()
                    glu_step(0)
                    glu_step(1)
                if h == 0 and nxt:
                    norm_T(4 * G + 6)
                    norm_T(4 * G + 7)
                finalize_a(h)
                deferred = h
            finalize_b(deferred)
            def wout_tile(tt):
                T = G * 4 + tt
                mbanks = (4, 5) if tt % 2 == 0 else (6, 7)
                for half in range(2):
                    mm_group(bank(mbanks[half]),
                             [(UO[:, cc, tt * P:(tt + 1) * P], w_out[:, cc, half * 512:(half + 1) * 512]) for cc in range(NK)],
                             [b("UO"), b("w_out0"), b("w_out1")], pb[mbanks[half]])
                post_norm_residual(T, mbanks, src_d[T * P:(T + 1) * P, :], xs_d[T * P:(T + 1) * P, :], "xs%d" % T, "xs%d" % T)

            wout_tile(0)
            wout_tile(1)
            if nxt:
                glu_step(2)
                glu_step(3)
            wout_tile(2)
            wout_tile(3)

        phase[0] = "ffn"
        dma(sp, ds("g1"), g1, fvec_d[l, 2:3, :].broadcast_to([P, D]), writes=[b("g1")])
        dma(sp, ds("g2"), g2, fvec_d[l, 3:4, :].broadcast_to([P, D]), writes=[b("g2")])
        first = True
        for blk in range(6):
            w_ = min(512, DFF - blk * 512)
            for which in range(2):
                c0 = which * DFF + blk * 512
                dma(pool, ds("w_up%d%d" % (which, blk)), w_up[:, :, c0:c0 + w_],
                    w_up_d[l, :, c0:c0 + w_].rearrange("(k p) c -> p k c", p=P),
                    writes=[b(("wu_g%d" if which == 0 else "wu_v%d") % blk)] + (mixer_bufs if first else []))
                first = False
        for gq in range(6):
            j0 = gq * 4
            j1 = min(j0 + 4, NJ)
            dma(pool, ds("w_dn%d" % gq), w_dn[:, j0:j1, :],
                w_dn_d[l, j0 * P:j1 * P, :].rearrange("(j p) c -> p j c", p=P), writes=[b("wd%d" % gq)])
        op(dve, lambda: nc.vector.memset(halo, 0.0), writes=[b("halo")])
        dst_d = out_d if l == L - 1 else xs_d
        norm_chain(0, xs_d, "xs0")
        norm_chain(1, xs_d, "xs1")
        norm_T(0)
        norm_T(1)
        norm_chain(2, xs_d, "xs2")
        norm_chain(3, xs_d, "xs3")
        norm_T(2)
        norm_T(3)
        for G in range(NG):
            nxt = G + 1 < NG
            if nxt and G == 0:
                for T in (4, 5):
                    norm_chain(T, xs_d, "xs%d" % T)
            ubank = {}

            def up_tile(n):
                jp, which = n // 2, n % 2
                j = which * NJ + jp
                col0 = which * DFF + jp * P
                s_ = n % 3
                bu = 1 + n % 3
                ubank[n] = bu
                wb_ = b(("wu_g%d" if which == 0 else "wu_v%d") % (jp // 4))
                mm_group(bank(bu), [(w_up[:, kc, col0:col0 + P], hT[:, kc, :]) for kc in range(NK)],
                         [wb_] + hT_bufs, pb[bu])
                rbuf = b("R%d" % s_)
                op(dve, lambda: nc.vector.tensor_copy(out=Rb[s_][:, 0:2], in_=halo[:, j, :]),
                   reads=[b("halo")], writes=[rbuf])
                op(act, lambda: nc.scalar.copy(out=Rb[s_][:, 2:2 + TG], in_=bank(bu)), reads=[pb[bu]], writes=[rbuf])
                op(dve, lambda: nc.vector.tensor_copy(out=halo[:, j, :], in_=Rb[s_][:, TG:TG + 2]),
                   reads=[rbuf], writes=[b("halo")])
                base_ = 4 if jp % 2 == 0 else 6
                bcv = base_ + which
                if which == 0:
                    op(act, lambda: nc.scalar.activation(out=bank(bcv), in_=bank(bu), func=AF.Identity,
                                                         scale=pv[:, PV_FW + j * 3 + 2:PV_FW + j * 3 + 3],
                                                         bias=pv[:, PV_FB + j:PV_FB + j + 1]),
                       reads=[pb[bu], b("pv")], writes=[pb[bcv]])
                else:
                    fw = pv[:, PV_FW + j * 3:PV_FW + j * 3 + 3]
                    op(pool, lambda: nc.gpsimd.tensor_tensor(out=D3[jp % 2], in0=identf.unsqueeze(1).broadcast_to([P, 3, P]),
                                                             in1=fw.unsqueeze(2).broadcast_to([P, 3, P]), op=ALU.mult),
                       reads=[b("identf"), b("pv")], writes=[b("D3%d" % (jp % 2))])

            def conv_tile(n):
                jp, which = n // 2, n % 2
                j = which * NJ + jp
                s_ = n % 3
                base_ = 4 if jp % 2 == 0 else 6
                bcv = base_ + which
                rbuf = b("R%d" % s_)
                if which == 0:
                    for k in (1, 0):
                        op(dve, lambda k=k: nc.vector.scalar_tensor_tensor(out=bank(bcv), in0=Rb[s_][:, k:k + TG],
                                                                           scalar=pv[:, PV_FW + j * 3 + k:PV_FW + j * 3 + k + 1],
                                                                           in1=bank(bcv), op0=ALU.mult, op1=ALU.add),
                           reads=[rbuf, pb[bcv], b("pv")], writes=[pb[bcv]])
                    gl = tmps[:, 2 + jp % 2, :]
                    glb = b("tmp%d" % (2 + jp % 2))
                    op(act, lambda: nc.scalar.activation(out=gl, in_=bank(base_), func=AF.Gelu_apprx_tanh),
                       reads=[pb[base_]], writes=[glb])
                else:
                    mm_group(bank(bcv), [(D3[jp % 2][:, k, :], Rb[s_][:, k:k + TG]) for k in range(3)],
                             [b("D3%d" % (jp % 2)), rbuf], pb[bcv])
                    gl = tmps[:, 2 + jp % 2, :]
                    glb = b("tmp%d" % (2 + jp % 2))
                    op(dve, lambda: nc.vector.scalar_tensor_tensor(out=HID[:, jp, :], in0=bank(bcv),
                                                                   scalar=pv[:, PV_FB + j:PV_FB + j + 1],
                                                                   in1=gl, op0=ALU.add, op1=ALU.mult),
                       reads=[pb[bcv], glb, b("pv")], writes=[b("HID")])

            def down_tile(tt):
                T = G * 4 + tt
                mbanks = (6, 7) if tt % 2 == 0 else (4, 5)
                for half in range(2):
                    for jp in range(NJ):
                        op(pe, lambda jp=jp, half=half: nc.tensor.matmul(out=bank(mbanks[half]), lhsT=HID[:, jp, tt * P:(tt + 1) * P],
                                                                         rhs=w_dn[:, jp, half * 512:(half + 1) * 512],
                                                                         start=(jp == 0), stop=(jp == NJ - 1)),
                           reads=[b("HID"), b("wd%d" % (jp // 4))], writes=[pb[mbanks[half]]], signal=(jp == NJ - 1))
                post_norm_residual(T, mbanks, xs_d[T * P:(T + 1) * P, :], dst_d[T * P:(T + 1) * P, :],
                                   "xs%d" % T, ("out%d" % T) if l == L - 1 else ("xs%d" % T))

            NTL = 2 * NJ
            up_tile(0)
            for n in range(NTL):
                if n + 1 < NTL:
                    up_tile(n + 1)
                conv_tile(n)
            if nxt:
                norm_T(4 * G + 4)
                norm_T(4 * G + 5)
                for T in (4 * G + 6, 4 * G + 7):
                    norm_chain(T, xs_d, "xs%d" % T)
            down_tile(0)
            down_tile(1)
            if nxt:
                norm_T(4 * G + 6)
                norm_T(4 * G + 7)
            if G + 2 < NG:
                for T in (4 * G + 8, 4 * G + 9):
                    norm_chain(T, xs_d, "xs%d" % T)
            down_tile(2)
            down_tile(3)

    for nm in ("xst0", "xst1"):
        sp.eng.wait_ge(dsem[nm].sem, dsem[nm].cnt)
    stats = {q.name: (q.nins, q.nwait) for q in (pe, act, dve, pool, sp)}
    print("instr/wait counts:", stats)
    return nc


def _consts(S):
    ident = np.eye(P, dtype=np.float32)
    pmat = np.zeros((P, P), np.float32)
    for base in (0, 64):
        for i in range(8):
            pmat[base + i + 8, base + i] = -1.0
            pmat[base + i, base + i + 8] = 1.0
    mask = (np.arange(P)[:, None] <= np.arange(P)[None, :]).astype(np.float32)
    pos = np.arange(S, dtype=np.float32)
    inv_freq = (np.float32(500000.0) ** (-np.arange(0, 16, 2, dtype=np.float32) / np.float32(16))).astype(np.float32)
    ang = (pos[:, None] * inv_freq[None, :]).astype(np.float32)
    cos = np.cos(ang).astype(np.float32)
    sin = np.sin(ang).astype(np.float32)
    C = np.ones((P, S), np.float32)
    Sn = np.zeros((P, S), np.float32)
    for base in (0, 64):
        for i in range(16):
            C[base + i] = cos[:, i % 8]
            Sn[base + i] = sin[:, i % 8]
    return np.stack([ident, pmat, mask]), np.stack([C, Sn])


def _pack_small(inp, L):
    pvec = np.zeros((L, P, NPV), np.float32)
    cw = np.asarray(inp["conv_w"], np.float32)
    pvec[:, :, PV_CW:PV_CW + 124] = cw.reshape(L, 31, 4, P).transpose(0, 3, 2, 1).reshape(L, P, 124)
    for name, off in (("conv_b", PV_CB), ("conv_ln_g", PV_LG), ("conv_ln_b", PV_LB)):
        pvec[:, :, off:off + 4] = np.asarray(inp[name], np.float32).reshape(L, 4, P).transpose(0, 2, 1)
    fw = np.asarray(inp["ffn_conv_w"], np.float32)
    pvec[:, :, PV_FW:PV_FW + 132] = fw.reshape(L, 3, 44, P).transpose(0, 3, 2, 1).reshape(L, P, 132)
    pvec[:, :, PV_FB:PV_FB + 44] = np.asarray(inp["ffn_conv_b"], np.float32).reshape(L, 44, P).transpose(0, 2, 1)
    fvec = np.stack([np.asarray(inp[k], np.float32) for k in
                     ("pre_mix_norm", "post_mix_norm", "pre_ffn_norm", "post_ffn_norm")], axis=1)
    lamv = np.concatenate([np.asarray(inp[k], np.float32) for k in
                           ("lambda_q1", "lambda_k1", "lambda_q2", "lambda_k2")], axis=1).reshape(L, 1, 256)
    sgv = np.asarray(inp["subln_g"], np.float32).reshape(L, 1, P)
    return pvec, np.ascontiguousarray(fvec), np.ascontiguousarray(lamv), sgv


_PROG_CACHE = {}


def kernel(**inputs):
    x = np.asarray(inputs["x"], np.float32)
    Bsz, S, _ = x.shape
    L = np.asarray(inputs["w_in"]).shape[0]
    lam_inits = [0.8 - 0.6 * math.exp(-0.3 * l) for l in range(L)]
    key = (L, S)
    if key not in _PROG_CACHE:
        _PROG_CACHE[key] = build_program(L, S, lam_inits)
    nc = _PROG_CACHE[key]
    cmat, rope = _consts(S)
    pvec, fvec, lamv, sgv = _pack_small(inputs, L)
    shared = {
        "w_in": np.ascontiguousarray(inputs["w_in"], np.float32),
        "w_out": np.ascontiguousarray(inputs["w_out"], np.float32),
        "w_up": np.ascontiguousarray(inputs["w_up"], np.float32),
        "w_down": np.ascontiguousarray(inputs["w_down"], np.float32),
        "pvec": pvec, "fvec": fvec, "lamv": lamv, "sublng": sgv, "cmat": cmat, "rope": rope,
    }
    in_maps = []
    for c in range(Bsz):
        m = dict(shared)
        m["x"] = np.ascontiguousarray(x[c])
        in_maps.append(m)
    res = run_bass_kernel_spmd(nc, in_maps, core_ids=list(range(Bsz)))
    return np.stack([np.asarray(r["out"], np.float32) for r in res.results], axis=0)
```

```python
import math
import numpy as np
import concourse.bass as bass
import concourse.mybir as mybir
from concourse.bass_utils import run_bass_kernel_spmd

F32 = mybir.dt.float32
BF16 = mybir.dt.bfloat16
AF = mybir.ActivationFunctionType
ALU = mybir.AluOpType
AX = mybir.AxisListType

P = 128
D = 1024
NK = 8
CIN = 2560
DFF = 2816
NJ = 22
CUP = 5632
TG = 512
EPS = 1e-6
CONV_K = 31
PV_CW = 0
PV_CB = PV_CW + 4 * 31
PV_LG = PV_CB + 4
PV_LB = PV_LG + 4
PV_FW = PV_LB + 4
PV_FB = PV_FW + 44 * 3
NPV = PV_FB + 44


class Buf:
    __slots__ = ("name", "w", "r", "excl")

    def __init__(self, name, excl=False):
        self.name = name
        self.w = None
        self.r = {}
        self.excl = excl


class Holder:
    def __init__(self, nc, name):
        self.name = name
        self.sem = nc.alloc_semaphore(name)
        self.cnt = 0


class Queue(Holder):
    def __init__(self, nc, eng, name):
        super().__init__(nc, "q_" + name)
        self.eng = eng
        self.seen = {}
        self.nwait = 0
        self.nins = 0
        self.inorder = False

    def wait_tok(self, holder, val):
        if holder is self and self.inorder:
            return
        if self.seen.get(holder, 0) >= val:
            return
        assert val <= holder.cnt, (self.name, holder.name, val, holder.cnt)
        self.eng.wait_ge(holder.sem, val)
        self.seen[holder] = val
        self.nwait += 1


def _deps(q, reads, writes):
    toks = {}
    for b in reads:
        if b.w is not None:
            h, v = b.w
            if toks.get(h, 0) < v:
                toks[h] = v
        if b.excl:
            for h, v in b.r.items():
                if h is not q and toks.get(h, 0) < v:
                    toks[h] = v
    for b in writes:
        if b.w is not None:
            h, v = b.w
            if toks.get(h, 0) < v:
                toks[h] = v
        for h, v in b.r.items():
            if toks.get(h, 0) < v:
                toks[h] = v
    for h, v in toks.items():
        q.wait_tok(h, v)


def _record(tok, reads, writes):
    h, v = tok
    for b in reads:
        if b.r.get(h, 0) < v:
            b.r[h] = v
    for b in writes:
        b.w = tok
        b.r = {}


def op(q, ins_fn, reads=(), writes=(), signal=True):
    _deps(q, reads, writes)
    ins = ins_fn()
    q.nins += 1
    if signal:
        ins.then_inc(q.sem, 1)
        q.cnt += 1
        tok = (q, q.cnt)
    else:
        tok = (q, q.cnt + 1)
    _record(tok, reads, writes)
    return ins


def dma(q, dsem, out, in_, reads=(), writes=()):
    _deps(q, reads, writes)
    ins = q.eng.dma_start(out=out, in_=in_)
    ins.then_inc(dsem.sem, 16)
    dsem.cnt += 16
    _record((dsem, dsem.cnt), reads, writes)
    return ins


class Arena:
    def __init__(self, nc, nbytes):
        self.nbytes = nbytes
        self.t = nc.alloc_sbuf_tensor("arena", [P, nbytes // 2], BF16)

    def view(self, off, shape, dt):
        esz = 4 if dt == F32 else 2
        n = 1
        for d_ in shape:
            n *= d_
        nb = n * esz
        assert off % 4 == 0 and off + nb <= self.nbytes, (off, nb, self.nbytes)
        ap = self.t[:, off // 2:(off + nb) // 2]
        if dt == F32:
            ap = ap.bitcast(F32)
        if len(shape) == 2:
            ap = ap.rearrange("p (a b) -> p a b", a=shape[0])
        elif len(shape) == 3:
            ap = ap.rearrange("p (a b c) -> p a b c", a=shape[0], b=shape[1])
        return ap


class Layout:
    def __init__(self, arena, base=0):
        self.arena = arena
        self.off = base
        self.hi = base

    def get(self, shape, dt):
        esz = 4 if dt == F32 else 2
        n = esz
        for d_ in shape:
            n *= d_
        off = (self.off + 31) // 32 * 32
        self.off = off + n
        self.hi = max(self.hi, self.off)
        return self.arena.view(off, shape, dt)


def build_program(L, S, lam_inits):
    NG = S // TG
    NT = S // P
    nc = bass.Bass("TRN2", target_bir_lowering=False)

    def din(name, shape):
        return nc.dram_tensor(name, list(shape), F32, kind="ExternalInput")

    x_d = din("x", [S, D])
    w_in_d = din("w_in", [L, D, CIN])
    w_out_d = din("w_out", [L, D, D])
    w_up_d = din("w_up", [L, D, CUP])
    w_dn_d = din("w_down", [L, DFF, D])
    pvec_d = din("pvec", [L, P, NPV])
    fvec_d = din("fvec", [L, 4, D])
    lamv_d = din("lamv", [L, 1, 256])
    sg_d = din("sublng", [L, 1, P])
    cmat_d = din("cmat", [3, P, P])
    rope_d = din("rope", [2, P, S])
    out_d = nc.dram_tensor("out", [S, D], F32, kind="ExternalOutput")
    xs_d = nc.dram_tensor("xs_scratch", [S, D], F32)

    pe = Queue(nc, nc.tensor, "pe")
    pe.inorder = True
    act = Queue(nc, nc.scalar, "act")
    dve = Queue(nc, nc.vector, "dve")
    pool = Queue(nc, nc.gpsimd, "pool")
    sp = Queue(nc, nc.sync, "sp")

    ARENA_BYTES = 212800
    arena = Arena(nc, ARENA_BYTES)
    com = Layout(arena, 0)
    ident = com.get([P], BF16)
    pm = com.get([P], BF16)
    cmask = com.get([P], BF16)
    ones = com.get([P], BF16)
    identf = com.get([P], F32)
    pv = com.get([NPV], F32)
    sg = com.get([P], F32)
    small = com.get([64], F32)
    g1 = com.get([D], F32)
    g2 = com.get([D], F32)
    xin = [com.get([D], F32) for _ in range(2)]
    xres = [com.get([D], F32)]
    hbf = [com.get([D], BF16) for _ in range(2)]
    hT = com.get([NK, TG], BF16)
    tmps = com.get([4, TG], F32)
    ropeCS = com.get([2, TG], F32)
    ropeC = ropeCS[:, 0, :]
    ropeS = ropeCS[:, 1, :]
    xres_rope = ropeCS.rearrange("p a b -> p (a b)")
    lamb = tmps[:, 3, 0:256]
    base_phase = (com.hi + 31) // 32 * 32
    mx = Layout(arena, base_phase)
    w_in = mx.get([NK, CIN], BF16)
    w_out = mx.get([NK, D], BF16)
    KT = mx.get([4, S], BF16)
    V = mx.get([NT, 4, 130], BF16)
    QT = mx.get([4, TG], BF16)
    Ubuf = mx.get([4, 30 + TG], BF16)
    UO = mx.get([NK, TG], BF16)
    ybf_off = (mx.off + 31) // 32 * 32
    ybf = mx.get([4, TG], BF16)
    xres_ybf = arena.view(ybf_off, [D], F32)
    ysq = mx.get([4, TG], BF16)
    rl = mx.get([2, 4], F32)
    ssr = mx.get([4], F32)
    xoff = (mx.off + 31) // 32 * 32
    Dg = mx.get([CONV_K, P], BF16)
    Pt = [[arena.view(xoff + (2 * c + s_) * 1024, [TG], BF16) for s_ in range(2)] for c in range(2)]
    obf = arena.view(xoff + 4096, [4, P], BF16)
    zbf = arena.view(xoff + 5120, [TG], BF16)
    DgB = mx.get([CONV_K, P], BF16)
    ff = Layout(arena, base_phase)
    w_up = ff.get([NK, CUP], BF16)
    w_dn = ff.get([NJ, D], BF16)
    HID = ff.get([NJ, TG], BF16)
    Rb = [ff.get([2 + TG], BF16) for _ in range(3)]
    halo = ff.get([44, 2], BF16)
    D3 = [ff.get([3, P], BF16) for _ in range(2)]
    print("SBUF layout: common %d, mixer %d, ffn %d (limit %d)" % (com.hi, mx.hi, ff.hi, ARENA_BYTES))
    assert mx.hi <= ARENA_BYTES and ff.hi <= ARENA_BYTES

    psum = nc.alloc_psum_tensor("psum", [P, 8 * 512], F32)

    def bank(i, n=1):
        return psum[:, i * 512:(i + n) * 512]

    pT = bank(0).bitcast(BF16)

    B = {}

    def b(name):
        if name not in B:
            B[name] = Buf(name)
        return B[name]

    pb = [b("psum%d" % i) for i in range(8)]
    for x_ in pb:
        x_.excl = True
    WI_BLK = ["wi_a", "wi_g", "wi_q", "wi_k", "wi_v"]
    xalias = ["Pt00", "Pt01", "Pt10", "Pt11", "obf", "zbf"]
    mixer_names = WI_BLK + ["w_out0", "w_out1", "QT", "Ubuf", "Dg", "DgB", "UO", "ybf", "ysq", "rl0", "rl1", "ssr", "Vones"] + xalias + \
        ["KT%d" % g for g in range(NG)] + ["V%d" % g for g in range(NG)]
    ffn_names = ["wu_g%d" % i for i in range(6)] + ["wu_v%d" % i for i in range(6)] + ["wd%d" % i for i in range(6)] + \
        ["HID", "R0", "R1", "R2", "halo", "D30", "D31"]
    mixer_bufs = [b(n) for n in mixer_names]
    ffn_bufs = [b(n) for n in ffn_names]
    xalias_bufs = [b(n) for n in xalias]
    hT_bufs = [b("hT%d" % i) for i in range(4)]

    dsem = {}

    def ds(name):
        if name not in dsem:
            dsem[name] = Holder(nc, "d_" + name)
        return dsem[name]

    def sc(i):
        return small[:, i:i + 1]
    S1, S2, E1, E2, NLAM = range(5)
    SSQ0, RSTD0, PSSQ0, PRSTD0 = 8, 10, 12, 14
    EPSC = 16

    dma(pool, ds("c0"), ident, cmat_d[0], writes=[b("ident")])
    dma(pool, ds("c0b"), pm, cmat_d[1], writes=[b("pm")])
    dma(pool, ds("c0c"), cmask, cmat_d[2], writes=[b("cmask")])
    dma(sp, ds("c1"), identf, cmat_d[0], writes=[b("identf")])
    op(dve, lambda: nc.vector.memset(ones, 1.0), writes=[b("ones")])
    op(dve, lambda: nc.vector.memset(sc(EPSC), EPS), writes=[b("epsc")])

    def norm_chain(T, src_d, srcbuf):
        s_ = T % 2
        xb, hb = b("xin%d" % s_), b("hbf%d" % s_)
        sq, rs = b("ssq%d" % s_), b("rstd%d" % s_)
        dma(sp, ds("xin%d" % s_), xin[s_], src_d[T * P:(T + 1) * P, :], reads=[b(srcbuf)], writes=[xb])
        op(dve, lambda: nc.vector.memset(sc(SSQ0 + s_), 0.0), writes=[sq])
        op(act, lambda: nc.scalar.activation(out=hbf[s_], in_=xin[s_], func=AF.Square, accum_out=sc(SSQ0 + s_)),
           reads=[xb], writes=[hb, sq])
        op(act, lambda: nc.scalar.activation(out=sc(RSTD0 + s_), in_=sc(SSQ0 + s_), func=AF.Ln, bias=sc(EPSC), scale=1.0 / D),
           reads=[sq, b("epsc")], writes=[rs])
        op(act, lambda: nc.scalar.activation(out=sc(RSTD0 + s_), in_=sc(RSTD0 + s_), func=AF.Exp, scale=-0.5), reads=[rs], writes=[rs])
        op(dve, lambda: nc.vector.scalar_tensor_tensor(out=hbf[s_], in0=xin[s_], scalar=sc(RSTD0 + s_), in1=g1,
                                                       op0=ALU.mult, op1=ALU.mult),
           reads=[xb, rs, b("g1")], writes=[hb])

    def norm_T(T):
        s_ = T % 2
        tt = T % 4
        hb = b("hbf%d" % s_)
        for kc in range(NK):
            op(pe, lambda kc=kc: nc.tensor.transpose(out=pT[:, kc * P:(kc + 1) * P], in_=hbf[s_][:, kc * P:(kc + 1) * P],
                                                   identity=ident),
               reads=[hb, b("ident")], writes=[pb[0]], signal=(kc == NK - 1))
        op(dve, lambda: nc.vector.tensor_copy(out=hT[:, :, tt * P:(tt + 1) * P],
                                              in_=pT[:, 0:NK * P].rearrange("p (k c) -> p k c", k=NK)),
           reads=[pb[0]], writes=[hT_bufs[tt]])

    def mm_group(out_ap, pairs, reads, wbuf, start=True):
        n = len(pairs)
        for i, (l_, r_) in enumerate(pairs):
            op(pe, lambda l_=l_, r_=r_, i=i: nc.tensor.matmul(out=out_ap, lhsT=l_, rhs=r_, start=(start and i == 0),
                                                              stop=(i == n - 1)),
               reads=reads, writes=[wbuf], signal=(i == n - 1))

    def post_norm_residual(T, mbanks, src_ap, dst_ap, src_buf_name, dst_buf_name):
        s_ = T % 2
        m_ap = bank(mbanks[0], 2)
        mb = [pb[mbanks[0]], pb[mbanks[1]]]
        if s_ == 0:
            xr, xbs = xres[0], [b("xres0")]
        elif phase[0] == "mixer":
            xr, xbs = xres_ybf, [b("ybf")]
        else:
            xr, xbs = xres_rope, [b("ropeC"), b("ropeS")]
        sq, rs = b("pssq%d" % s_), b("prstd%d" % s_)
        dma(sp, ds("xres%d" % s_), xr, src_ap, reads=[b(src_buf_name)], writes=xbs)
        op(dve, lambda: nc.vector.memset(sc(PSSQ0 + s_), 0.0), writes=[sq])
        tmp32 = tmps[:, 2 * s_:2 * s_ + 2, :].rearrange("p a b -> p (a b)")
        tb = [b("tmp%d" % (2 * s_)), b("tmp%d" % (2 * s_ + 1))]
        op(act, lambda: nc.scalar.activation(out=tmp32, in_=m_ap, func=AF.Square, accum_out=sc(PSSQ0 + s_)),
           reads=mb, writes=tb + [sq])
        op(act, lambda: nc.scalar.activation(out=sc(PRSTD0 + s_), in_=sc(PSSQ0 + s_), func=AF.Ln, bias=sc(EPSC), scale=1.0 / D),
           reads=[sq, b("epsc")], writes=[rs])
        op(act, lambda: nc.scalar.activation(out=sc(PRSTD0 + s_), in_=sc(PRSTD0 + s_), func=AF.Exp, scale=-0.5), reads=[rs], writes=[rs])
        op(dve, lambda: nc.vector.scalar_tensor_tensor(out=tmp32, in0=m_ap, scalar=sc(PRSTD0 + s_), in1=g2,
                                                       op0=ALU.mult, op1=ALU.mult),
           reads=mb + [rs, b("g2")], writes=tb)
        op(dve, lambda: nc.vector.tensor_tensor(out=xr, in0=xr, in1=tmp32, op=ALU.add),
           reads=tb + xbs, writes=xbs)
        dma(sp, ds("xst%d" % s_), dst_ap, xr, reads=xbs, writes=[b(dst_buf_name)])

    rot = [1, 2, 3]
    rr = [0]
    phase = ["mixer"]
    srot = [0]

    def nbs():
        i = srot[0] % 4
        srot[0] += 1
        return i

    def nb():
        i = rot[rr[0] % len(rot)]
        rr[0] += 1
        return i

    for l in range(L):
        lam_init = lam_inits[l]
        src_d = x_d if l == 0 else xs_d
        dma(sp, ds("pv"), pv, pvec_d[l], writes=[b("pv")])
        dma(sp, ds("lamb"), lamb, lamv_d[l].broadcast_to([P, 256]), writes=[b("tmp3")])
        dma(sp, ds("sg"), sg, sg_d[l].broadcast_to([P, P]), writes=[b("sg")])
        dma(sp, ds("g1"), g1, fvec_d[l, 0:1, :].broadcast_to([P, D]), writes=[b("g1")])
        dma(sp, ds("g2"), g2, fvec_d[l, 1:2, :].broadcast_to([P, D]), writes=[b("g2")])
        jt = tmps[:, 2, 0:64]
        op(dve, lambda: nc.vector.scalar_tensor_tensor(out=jt, in0=lamb[:, 0:64], scalar=1.0, in1=lamb[:, 64:128],
                                                       op0=ALU.mult, op1=ALU.mult, accum_out=sc(S1)),
           reads=[b("tmp3")], writes=[b("tmp2"), b("s1")])
        op(dve, lambda: nc.vector.scalar_tensor_tensor(out=jt, in0=lamb[:, 128:192], scalar=1.0, in1=lamb[:, 192:256],
                                                       op0=ALU.mult, op1=ALU.mult, accum_out=sc(S2)),
           reads=[b("tmp3")], writes=[b("tmp2"), b("s2")])
        op(act, lambda: nc.scalar.activation(out=sc(E1), in_=sc(S1), func=AF.Exp), reads=[b("s1")], writes=[b("e1")])
        op(act, lambda: nc.scalar.activation(out=sc(E2), in_=sc(S2), func=AF.Exp), reads=[b("s2")], writes=[b("e2")])
        op(dve, lambda: nc.vector.tensor_tensor(out=sc(NLAM), in0=sc(E2), in1=sc(E1), op=ALU.subtract),
           reads=[b("e1"), b("e2")], writes=[b("nlam")])
        op(dve, lambda: nc.vector.tensor_scalar(out=sc(NLAM), in0=sc(NLAM), scalar1=-float(lam_init), scalar2=None,
                                                op0=ALU.add),
           reads=[b("nlam")], writes=[b("nlam")])
        op(act, lambda: nc.scalar.mul(out=sg, in_=sg, mul=float(1.0 - lam_init)), reads=[b("sg")], writes=[b("sg")])

        first = True
        for blk in (1, 0, 2, 3, 4):
            c0 = blk * 512
            dma(pool, ds("w_in%d" % blk), w_in[:, :, c0:c0 + 512],
                w_in_d[l, :, c0:c0 + 512].rearrange("(k p) c -> p k c", p=P),
                writes=[b(WI_BLK[blk])] + (ffn_bufs if first else []))
            first = False
        for hf in range(2):
            dma(pool, ds("w_out%d" % hf), w_out[:, hf * 4:(hf + 1) * 4, :],
                w_out_d[l, hf * 512:(hf + 1) * 512, :].rearrange("(k p) c -> p k c", p=P), writes=[b("w_out%d" % hf)])
        op(pool, lambda: nc.gpsimd.memset(Ubuf[:, :, 0:30], 0.0), writes=[b("Ubuf")])
        op(pool, lambda: nc.gpsimd.memset(V[:, :, :, 128:130], 1.0), writes=[b("Vones")])

        phase[0] = "mixer"
        for T in range(4):
            norm_chain(T, src_d, "xs%d" % T) if T < 2 else None
        norm_T(0)
        norm_T(1)
        norm_chain(2, src_d, "xs2")
        norm_chain(3, src_d, "xs3")
        norm_T(2)
        norm_T(3)
        for G in range(NG):
            t0 = G * TG
            nxt = G + 1 < NG
            dma(sp, ds("ropeC"), ropeC, rope_d[0, :, t0:t0 + TG], writes=[b("ropeC")])
            dma(sp, ds("ropeS"), ropeS, rope_d[1, :, t0:t0 + TG], writes=[b("ropeS")])
            if nxt and G == 0:
                for T in (4, 5):
                    norm_chain(T, src_d, "xs%d" % T)

            def proj(col0, bi):
                mm_group(bank(bi), [(w_in[:, kc, col0:col0 + P], hT[:, kc, :]) for kc in range(NK)],
                         [b(WI_BLK[col0 // 512])] + hT_bufs, pb[bi])

            def dg_gen(ct):
                wv = pv[:, PV_CW + ct * CONV_K:PV_CW + (ct + 1) * CONV_K]
                dgt, dgb = (DgB, b("DgB")) if ct % 2 == 0 else (Dg, b("Dg"))
                op(pool, lambda: nc.gpsimd.tensor_tensor(out=dgt, in0=identf.unsqueeze(1).broadcast_to([P, CONV_K, P]),
                                                         in1=wv.unsqueeze(2).broadcast_to([P, CONV_K, P]), op=ALU.mult),
                   reads=[b("identf"), b("pv")], writes=[dgb] + (xalias_bufs if ct % 2 == 1 else []))

            def glu_step(ct):
                bg = nb()
                proj(512 + ct * P, bg)
                sgt = tmps[:, 3, :]
                sgb = b("tmp3")
                op(act, lambda: nc.scalar.activation(out=sgt, in_=bank(bg), func=AF.Sigmoid), reads=[pb[bg]], writes=[sgb])
                ba = nb()
                proj(ct * P, ba)
                op(dve, lambda: nc.vector.tensor_tensor(out=Ubuf[:, ct, 30:30 + TG], in0=bank(ba), in1=sgt, op=ALU.mult),
                   reads=[pb[ba], sgb], writes=[b("Ubuf")])

            def halo_copy():
                op(pool, lambda: nc.gpsimd.tensor_copy(out=Ubuf[:, :, 0:30], in_=Ubuf[:, :, TG:TG + 30]),
                   reads=[b("Ubuf")], writes=[b("Ubuf")])

            dg_gen(0)
            if G == 0:
                for ct in range(4):
                    glu_step(ct)
            for which in range(2):
                for h in range(4):
                    col0 = 1024 + which * 512 + h * P
                    bz = nb()
                    proj(col0, bz)
                    op(act, lambda bz=bz: nc.scalar.copy(out=zbf, in_=bank(bz)), reads=[pb[bz]], writes=[b("zbf")])
                    bp = nb()
                    mm_group(bank(bp), [(pm, zbf)], [b("pm"), b("zbf")], pb[bp])
                    op(dve, lambda bz=bz: nc.vector.tensor_tensor(out=tmps[:, 2, :], in0=bank(bz), in1=ropeC, op=ALU.mult),
                       reads=[pb[bz], b("ropeC")], writes=[b("tmp2")])
                    op(dve, lambda bp=bp: nc.vector.tensor_tensor(out=tmps[:, 3, :], in0=bank(bp), in1=ropeS, op=ALU.mult),
                       reads=[pb[bp], b("ropeS")], writes=[b("tmp3")])
                    if which == 0:
                        dst, dbuf = QT[:, h, :], b("QT")
                    else:
                        dst, dbuf = KT[:, h, t0:t0 + TG], b("KT%d" % G)
                    op(dve, lambda dst=dst: nc.vector.tensor_tensor(out=dst, in0=tmps[:, 2, :], in1=tmps[:, 3, :], op=ALU.add),
                       reads=[b("tmp2"), b("tmp3")], writes=[dbuf])
            for tt in range(4):
                T = G * 4 + tt
                bv = nb()
                mm_group(bank(bv), [(hT[:, kc, tt * P:(tt + 1) * P], w_in[:, kc, 2048:2560]) for kc in range(NK)],
                         [b("wi_v"), hT_bufs[tt]], pb[bv])
                op(act, lambda bv=bv, T=T: nc.scalar.copy(out=V[:, T, :, 0:128],
                                                          in_=bank(bv).rearrange("p (h c) -> p h c", h=4)),
                   reads=[pb[bv]], writes=[b("V%d" % G)])
            if nxt:
                norm_T(4 * G + 4)
                norm_T(4 * G + 5)
                for T in (4 * G + 6, 4 * G + 7):
                    norm_chain(T, src_d, "xs%d" % T)
            dg_gen(1)
            for ct in range(4):
                dgt, dgb = (DgB, b("DgB")) if ct % 2 == 0 else (Dg, b("Dg"))
                bc = nb()
                mm_group(bank(bc), [(dgt[:, k, :], Ubuf[:, ct, k:k + TG]) for k in range(CONV_K)],
                         [dgb, b("Ubuf")], pb[bc])
                if ct + 2 < 4:
                    dg_gen(ct + 2)
                cb = pv[:, PV_CB + ct:PV_CB + ct + 1]
                op(act, lambda bc=bc, ct=ct, cb=cb: nc.scalar.activation(out=ybf[:, ct, :], in_=bank(bc), func=AF.Identity, bias=cb),
                   reads=[pb[bc], b("pv")], writes=[b("ybf")])
                op(act, lambda bc=bc, ct=ct, cb=cb: nc.scalar.activation(out=ysq[:, ct, :], in_=bank(bc), func=AF.Square, bias=cb),
                   reads=[pb[bc], b("pv")], writes=[b("ysq")])
            b1 = nb()
            mm_group(bank(b1), [(ones, ybf[:, ct, :]) for ct in range(4)], [b("ones"), b("ybf")], pb[b1])
            b2 = nb()
            mm_group(bank(b2), [(ones, ysq[:, ct, :]) for ct in range(4)], [b("ones"), b("ysq")], pb[b2])
            mean, msq, rstd_t = tmps[:, 0, :], tmps[:, 1, :], tmps[:, 2, :]
            op(act, lambda: nc.scalar.mul(out=mean, in_=bank(b1), mul=1.0 / 512), reads=[pb[b1]], writes=[b("tmp0")])
            op(act, lambda: nc.scalar.activation(out=msq, in_=bank(b1), func=AF.Square, scale=1.0 / 512),
               reads=[pb[b1]], writes=[b("tmp1")])
            op(dve, lambda: nc.vector.scalar_tensor_tensor(out=rstd_t, in0=bank(b2), scalar=1.0 / 512, in1=msq,
                                                           op0=ALU.mult, op1=ALU.subtract),
               reads=[pb[b2], b("tmp1")], writes=[b("tmp2")])
            t3 = tmps[:, 3, :]

            def ln_a2():
                op(act, lambda: nc.scalar.activation(out=rstd_t, in_=rstd_t, func=AF.Ln, bias=sc(EPSC), scale=1.0),
                   reads=[b("tmp2"), b("epsc")], writes=[b("tmp2")])
                op(act, lambda: nc.scalar.activation(out=rstd_t, in_=rstd_t, func=AF.Exp, scale=-0.5), reads=[b("tmp2")], writes=[b("tmp2")])

            def ln_d2(ct):
                op(dve, lambda: nc.vector.tensor_tensor(out=t3, in0=ybf[:, ct, :], in1=mean, op=ALU.subtract),
                   reads=[b("ybf"), b("tmp0")], writes=[b("tmp3")])
                op(dve, lambda: nc.vector.tensor_tensor(out=t3, in0=t3, in1=rstd_t, op=ALU.mult),
                   reads=[b("tmp3"), b("tmp2")], writes=[b("tmp3")])

            def ln_a3(ct):
                op(act, lambda: nc.scalar.activation(out=UO[:, ct, :], in_=t3, func=AF.Silu,
                                                     scale=pv[:, PV_LG + ct:PV_LG + ct + 1],
                                                     bias=pv[:, PV_LB + ct:PV_LB + ct + 1]),
                   reads=[b("tmp3"), b("pv")], writes=[b("UO")])

            accv = [psum[:, (4 + 2 * c) * 512:(6 + 2 * c) * 512].rearrange("p (q w) -> p q w", q=4) for c in range(2)]
            accb = [[pb[4], pb[5]], [pb[6], pb[7]]]
            nkt = 4 * G + 4
            sbank = {}

            def emit_qk(h, kt, c):
                r = kt - 4 * G
                c0 = max(r, 0) * P
                sbk = nbs()
                sbank[(kt, c)] = sbk
                mm_group(bank(sbk)[:, c0:TG],
                         [(KT[c * 64:(c + 1) * 64, h, kt * P:(kt + 1) * P], QT[c * 64:(c + 1) * 64, h, c0:TG])],
                         [b("KT%d" % (kt // 4)), b("QT")], pb[sbk])

            def emit_exp(h, kt, c):
                r = kt - 4 * G
                c0 = max(r, 0) * P
                sbk = sbank[(kt, c)]
                sl = kt % 2
                ptb = b("Pt%d%d" % (c, sl))
                pt = Pt[c][sl]
                op(act, lambda: nc.scalar.activation(out=pt[:, c0:TG], in_=bank(sbk)[:, c0:TG], func=AF.Exp, scale=0.125),
                   reads=[pb[sbk]], writes=[ptb])
                if r >= 0:
                    op(dve, lambda: nc.vector.tensor_tensor(out=pt[:, c0:c0 + P], in0=pt[:, c0:c0 + P], in1=cmask, op=ALU.mult),
                       reads=[ptb, b("cmask")], writes=[ptb])

            def emit_av(h, kt, c):
                r = kt - 4 * G
                q0 = max(r, 0)
                sl = kt % 2
                ptb = b("Pt%d%d" % (c, sl))
                pt = Pt[c][sl]
                for ql in range(q0, 4):
                    bk = 4 + 2 * c + ql // 2
                    reg = psum[:, bk * 512 + (ql % 2) * 256: bk * 512 + (ql % 2) * 256 + 129]
                    first_ = (kt == 0 and ql % 2 == 0)
                    last_ = (kt == 4 * G + ql)
                    op(pe, lambda reg=reg, ql=ql, first_=first_, last_=last_:
                       nc.tensor.matmul(out=reg, lhsT=pt[:, ql * P:(ql + 1) * P], rhs=V[:, kt, h, 0:129],
                                        start=first_, stop=last_, skip_group_check=True),
                       reads=[ptb, b("V%d" % (kt // 4)), b("Vones")], writes=[pb[bk]], signal=(ql == 3))

            def finalize_a(h):
                t1 = tmps[:, 0, :].rearrange("p (q w) -> p q w", q=4)
                t2 = tmps[:, 1, :].rearrange("p (q w) -> p q w", q=4)
                o3 = tmps[:, 2, :].rearrange("p (q w) -> p q w", q=4)
                for c in range(2):
                    op(dve, lambda c=c: nc.vector.reciprocal(out=rl[:, c, :], in_=accv[c][:, :, 128]),
                       reads=accb[c], writes=[b("rl%d" % c)])
                op(dve, lambda: nc.vector.tensor_copy(out=t1, in_=accv[0][:, :, 0:128]), reads=accb[0], writes=[b("tmp0")])
                op(act, lambda: nc.scalar.copy(out=t2, in_=accv[1][:, :, 0:128]), reads=accb[1], writes=[b("tmp1")])
                op(dve, lambda: nc.vector.tensor_scalar(out=rl[:, 1, :], in0=rl[:, 1, :], scalar1=sc(NLAM), scalar2=None,
                                                        op0=ALU.mult),
                   reads=[b("rl1"), b("nlam")], writes=[b("rl1")])
                op(dve, lambda: nc.vector.tensor_tensor(out=t1, in0=t1,
                                                        in1=rl[:, 0, :].unsqueeze(2).broadcast_to([P, 4, P]), op=ALU.mult),
                   reads=[b("tmp0"), b("rl0")], writes=[b("tmp0")])
                op(dve, lambda: nc.vector.tensor_tensor(out=t2, in0=t2,
                                                        in1=rl[:, 1, :].unsqueeze(2).broadcast_to([P, 4, P]), op=ALU.mult),
                   reads=[b("tmp1"), b("rl1")], writes=[b("tmp1")])
                op(dve, lambda: nc.vector.tensor_tensor(out=o3, in0=t1, in1=t2, op=ALU.add),
                   reads=[b("tmp0"), b("tmp1")], writes=[b("tmp2")])
                op(dve, lambda: nc.vector.tensor_tensor(out=t1, in0=o3, in1=o3, op=ALU.mult),
                   reads=[b("tmp2")], writes=[b("tmp0")])
                op(dve, lambda: nc.vector.tensor_reduce(out=ssr, in_=t1, axis=AX.X, op=ALU.add),
                   reads=[b("tmp0")], writes=[b("ssr")])
                op(act, lambda: nc.scalar.activation(out=ssr, in_=ssr, func=AF.Ln, bias=sc(EPSC), scale=1.0 / 128),
                   reads=[b("ssr"), b("epsc")], writes=[b("ssr")])
                op(act, lambda: nc.scalar.activation(out=ssr, in_=ssr, func=AF.Exp, scale=-0.5), reads=[b("ssr")], writes=[b("ssr")])
                op(dve, lambda: nc.vector.tensor_tensor(out=o3, in0=o3, in1=ssr.unsqueeze(2).broadcast_to([P, 4, P]), op=ALU.mult),
                   reads=[b("tmp2"), b("ssr")], writes=[b("tmp2")])
                op(dve, lambda: nc.vector.tensor_tensor(out=obf, in0=o3, in1=sg.unsqueeze(1).broadcast_to([P, 4, P]), op=ALU.mult),
                   reads=[b("tmp2"), b("sg")], writes=[b("obf")])

            def finalize_b(h):
                for ql in range(4):
                    op(pe, lambda ql=ql: nc.tensor.transpose(out=pT[:, ql * P:(ql + 1) * P], in_=obf[:, ql, :], identity=ident),
                       reads=[b("obf"), b("ident")], writes=[pb[0]], signal=(ql == 3))
                op(act, lambda: nc.scalar.copy(out=UO[:, 4 + h, :], in_=pT[:, 0:TG]), reads=[pb[0]], writes=[b("UO")])

            deferred = None
            for h in range(4):
                emit_qk(h, 0, 0)
                emit_qk(h, 0, 1)
                for kt in range(nkt):
                    if kt + 1 < nkt:
                        emit_qk(h, kt + 1, 0)
                        emit_qk(h, kt + 1, 1)
                    emit_exp(h, kt, 0)
                    emit_exp(h, kt, 1)
                    if h == 0 and kt < 4:
                        if kt == 0:
                            ln_a2()
                        else:
                            ln_a3(kt - 1)
                        ln_d2(kt)
                    emit_av(h, kt, 0)
                    emit_av(h, kt, 1)
                    if kt == 2 and deferred is not None:
                        finalize_b(deferred)
                        deferred = None
                if h == 0:
                    ln_a3(3)
                if h == 3 and G + 2 < NG:
                    for T in (4 * G + 8, 4 * G + 9):
                        norm_chain(T, src_d, "xs%d" % T)
                if h == 3 and nxt:
                    halo_copy()
                    glu_step(0)
                    glu_step(1)
                if h == 0 and nxt:
                    norm_T(4 * G + 6)
                    norm_T(4 * G + 7)
                finalize_a(h)
                deferred = h
            finalize_b(deferred)
            def wout_tile(tt):
                T = G * 4 + tt
                mbanks = (4, 5) if tt % 2 == 0 else (6, 7)
                for half in range(2):
                    mm_group(bank(mbanks[half]),
                             [(UO[:, cc, tt * P:(tt + 1) * P], w_out[:, cc, half * 512:(half + 1) * 512]) for cc in range(NK)],
                             [b("UO"), b("w_out0"), b("w_out1")], pb[mbanks[half]])
                post_norm_residual(T, mbanks, src_d[T * P:(T + 1) * P, :], xs_d[T * P:(T + 1) * P, :], "xs%d" % T, "xs%d" % T)

            wout_tile(0)
            wout_tile(1)
            if nxt:
                glu_step(2)
                glu_step(3)
            wout_tile(2)
            wout_tile(3)

        phase[0] = "ffn"
        dma(sp, ds("g1"), g1, fvec_d[l, 2:3, :].broadcast_to([P, D]), writes=[b("g1")])
        dma(sp, ds("g2"), g2, fvec_d[l, 3:4, :].broadcast_to([P, D]), writes=[b("g2")])
        first = True
        for blk in range(6):
            w_ = min(512, DFF - blk * 512)
            for which in range(2):
                c0 = which * DFF + blk * 512
                dma(pool, ds("w_up%d%d" % (which, blk)), w_up[:, :, c0:c0 + w_],
                    w_up_d[l, :, c0:c0 + w_].rearrange("(k p) c -> p k c", p=P),
                    writes=[b(("wu_g%d" if which == 0 else "wu_v%d") % blk)] + (mixer_bufs if first else []))
                first = False
        for gq in range(6):
            j0 = gq * 4
            j1 = min(j0 + 4, NJ)
            dma(pool, ds("w_dn%d" % gq), w_dn[:, j0:j1, :],
                w_dn_d[l, j0 * P:j1 * P, :].rearrange("(j p) c -> p j c", p=P), writes=[b("wd%d" % gq)])
        op(dve, lambda: nc.vector.memset(halo, 0.0), writes=[b("halo")])
        dst_d = out_d if l == L - 1 else xs_d
        norm_chain(0, xs_d, "xs0")
        norm_chain(1, xs_d, "xs1")
        norm_T(0)
        norm_T(1)
        norm_chain(2, xs_d, "xs2")
        norm_chain(3, xs_d, "xs3")
        norm_T(2)
        norm_T(3)
        for G in range(NG):
            nxt = G + 1 < NG
            if nxt and G == 0:
                for T in (4, 5):
                    norm_chain(T, xs_d, "xs%d" % T)
            ubank = {}

            def up_tile(n):
                jp, which = n // 2, n % 2
                j = which * NJ + jp
                col0 = which * DFF + jp * P
                s_ = n % 3
                bu = 1 + n % 3
                ubank[n] = bu
                wb_ = b(("wu_g%d" if which == 0 else "wu_v%d") % (jp // 4))
                mm_group(bank(bu), [(w_up[:, kc, col0:col0 + P], hT[:, kc, :]) for kc in range(NK)],
                         [wb_] + hT_bufs, pb[bu])
                rbuf = b("R%d" % s_)
                op(dve, lambda: nc.vector.tensor_copy(out=Rb[s_][:, 0:2], in_=halo[:, j, :]),
                   reads=[b("halo")], writes=[rbuf])
                op(act, lambda: nc.scalar.copy(out=Rb[s_][:, 2:2 + TG], in_=bank(bu)), reads=[pb[bu]], writes=[rbuf])
                op(dve, lambda: nc.vector.tensor_copy(out=halo[:, j, :], in_=Rb[s_][:, TG:TG + 2]),
                   reads=[rbuf], writes=[b("halo")])
                base_ = 4 if jp % 2 == 0 else 6
                bcv = base_ + which
                if which == 0:
                    op(act, lambda: nc.scalar.activation(out=bank(bcv), in_=bank(bu), func=AF.Identity,
                                                         scale=pv[:, PV_FW + j * 3 + 2:PV_FW + j * 3 + 3],
                                                         bias=pv[:, PV_FB + j:PV_FB + j + 1]),
                       reads=[pb[bu], b("pv")], writes=[pb[bcv]])
                else:
                    fw = pv[:, PV_FW + j * 3:PV_FW + j * 3 + 3]
                    op(pool, lambda: nc.gpsimd.tensor_tensor(out=D3[jp % 2], in0=identf.unsqueeze(1).broadcast_to([P, 3, P]),
                                                             in1=fw.unsqueeze(2).broadcast_to([P, 3, P]), op=ALU.mult),
                       reads=[b("identf"), b("pv")], writes=[b("D3%d" % (jp % 2))])

            def conv_tile(n):
                jp, which = n // 2, n % 2
                j = which * NJ + jp
                s_ = n % 3
                base_ = 4 if jp % 2 == 0 else 6
                bcv = base_ + which
                rbuf = b("R%d" % s_)
                if which == 0:
                    for k in (1, 0):
                        op(dve, lambda k=k: nc.vector.scalar_tensor_tensor(out=bank(bcv), in0=Rb[s_][:, k:k + TG],
                                                                           scalar=pv[:, PV_FW + j * 3 + k:PV_FW + j * 3 + k + 1],
                                                                           in1=bank(bcv), op0=ALU.mult, op1=ALU.add),
                           reads=[rbuf, pb[bcv], b("pv")], writes=[pb[bcv]])
                    gl = tmps[:, 2 + jp % 2, :]
                    glb = b("tmp%d" % (2 + jp % 2))
                    op(act, lambda: nc.scalar.activation(out=gl, in_=bank(base_), func=AF.Gelu_apprx_tanh),
                       reads=[pb[base_]], writes=[glb])
                else:
                    mm_group(bank(bcv), [(D3[jp % 2][:, k, :], Rb[s_][:, k:k + TG]) for k in range(3)],
                             [b("D3%d" % (jp % 2)), rbuf], pb[bcv])
                    gl = tmps[:, 2 + jp % 2, :]
                    glb = b("tmp%d" % (2 + jp % 2))
                    op(dve, lambda: nc.vector.scalar_tensor_tensor(out=HID[:, jp, :], in0=bank(bcv),
                                                                   scalar=pv[:, PV_FB + j:PV_FB + j + 1],
                                                                   in1=gl, op0=ALU.add, op1=ALU.mult),
                       reads=[pb[bcv], glb, b("pv")], writes=[b("HID")])

            def down_tile(tt):
                T = G * 4 + tt
                mbanks = (6, 7) if tt % 2 == 0 else (4, 5)
                for half in range(2):
                    for jp in range(NJ):
                        op(pe, lambda jp=jp, half=half: nc.tensor.matmul(out=bank(mbanks[half]), lhsT=HID[:, jp, tt * P:(tt + 1) * P],
                                                                         rhs=w_dn[:, jp, half * 512:(half + 1) * 512],
                                                                         start=(jp == 0), stop=(jp == NJ - 1)),
                           reads=[b("HID"), b("wd%d" % (jp // 4))], writes=[pb[mbanks[half]]], signal=(jp == NJ - 1))
                post_norm_residual(T, mbanks, xs_d[T * P:(T + 1) * P, :], dst_d[T * P:(T + 1) * P, :],
                                   "xs%d" % T, ("out%d" % T) if l == L - 1 else ("xs%d" % T))

            NTL = 2 * NJ
            up_tile(0)
            up_tile(1)
            for n in range(NTL):
                if n + 2 < NTL:
                    up_tile(n + 2)
                conv_tile(n)
            if nxt:
                norm_T(4 * G + 4)
                norm_T(4 * G + 5)
                for T in (4 * G + 6, 4 * G + 7):
                    norm_chain(T, xs_d, "xs%d" % T)
            down_tile(0)
            down_tile(1)
            if nxt:
                norm_T(4 * G + 6)
                norm_T(4 * G + 7)
            if G + 2 < NG:
                for T in (4 * G + 8, 4 * G + 9):
                    norm_chain(T, xs_d, "xs%d" % T)
            down_tile(2)
            down_tile(3)

    for nm in ("xst0", "xst1"):
        sp.eng.wait_ge(dsem[nm].sem, dsem[nm].cnt)
    stats = {q.name: (q.nins, q.nwait) for q in (pe, act, dve, pool, sp)}
    print("instr/wait counts:", stats)
    return nc


def _consts(S):
    ident = np.eye(P, dtype=np.float32)
    pmat = np.zeros((P, P), np.float32)
    for base in (0, 64):
        for i in range(8):
            pmat[base + i + 8, base + i] = -1.0
            pmat[base + i, base + i + 8] = 1.0
    mask = (np.arange(P)[:, None] <= np.arange(P)[None, :]).astype(np.float32)
    pos = np.arange(S, dtype=np.float32)
    inv_freq = (np.float32(500000.0) ** (-np.arange(0, 16, 2, dtype=np.float32) / np.float32(16))).astype(np.float32)
    ang = (pos[:, None] * inv_freq[None, :]).astype(np.float32)
    cos = np.cos(ang).astype(np.float32)
    sin = np.sin(ang).astype(np.float32)
    C = np.ones((P, S), np.float32)
    Sn = np.zeros((P, S), np.float32)
    for base in (0, 64):
        for i in range(16):
            C[base + i] = cos[:, i % 8]
            Sn[base + i] = sin[:, i % 8]
    return np.stack([ident, pmat, mask]), np.stack([C, Sn])


def _pack_small(inp, L):
    pvec = np.zeros((L, P, NPV), np.float32)
    cw = np.asarray(inp["conv_w"], np.float32)
    pvec[:, :, PV_CW:PV_CW + 124] = cw.reshape(L, 31, 4, P).transpose(0, 3, 2, 1).reshape(L, P, 124)
    for name, off in (("conv_b", PV_CB), ("conv_ln_g", PV_LG), ("conv_ln_b", PV_LB)):
        pvec[:, :, off:off + 4] = np.asarray(inp[name], np.float32).reshape(L, 4, P).transpose(0, 2, 1)
    fw = np.asarray(inp["ffn_conv_w"], np.float32)
    pvec[:, :, PV_FW:PV_FW + 132] = fw.reshape(L, 3, 44, P).transpose(0, 3, 2, 1).reshape(L, P, 132)
    pvec[:, :, PV_FB:PV_FB + 44] = np.asarray(inp["ffn_conv_b"], np.float32).reshape(L, 44, P).transpose(0, 2, 1)
    fvec = np.stack([np.asarray(inp[k], np.float32) for k in
                     ("pre_mix_norm", "post_mix_norm", "pre_ffn_norm", "post_ffn_norm")], axis=1)
    lamv = np.concatenate([np.asarray(inp[k], np.float32) for k in
                           ("lambda_q1", "lambda_k1", "lambda_q2", "lambda_k2")], axis=1).reshape(L, 1, 256)
    sgv = np.asarray(inp["subln_g"], np.float32).reshape(L, 1, P)
    return pvec, np.ascontiguousarray(fvec), np.ascontiguousarray(lamv), sgv


_PROG_CACHE = {}


def kernel(**inputs):
    x = np.asarray(inputs["x"], np.float32)
    Bsz, S, _ = x.shape
    L = np.asarray(inputs["w_in"]).shape[0]
    lam_inits = [0.8 - 0.6 * math.exp(-0.3 * l) for l in range(L)]
    key = (L, S)
    if key not in _PROG_CACHE:
        _PROG_CACHE[key] = build_program(L, S, lam_inits)
    nc = _PROG_CACHE[key]
    cmat, rope = _consts(S)
    pvec, fvec, lamv, sgv = _pack_small(inputs, L)
    shared = {
        "w_in": np.ascontiguousarray(inputs["w_in"], np.float32),
        "w_out": np.ascontiguousarray(inputs["w_out"], np.float32),
        "w_up": np.ascontiguousarray(inputs["w_up"], np.float32),
        "w_down": np.ascontiguousarray(inputs["w_down"], np.float32),
        "pvec": pvec, "fvec": fvec, "lamv": lamv, "sublng": sgv, "cmat": cmat, "rope": rope,
    }
    in_maps = []
    for c in range(Bsz):
        m = dict(shared)
        m["x"] = np.ascontiguousarray(x[c])
        in_maps.append(m)
    res = run_bass_kernel_spmd(nc, in_maps, core_ids=list(range(Bsz)))
    return np.stack([np.asarray(r["out"], np.float32) for r in res.results], axis=0)
```

```python
import math
import numpy as np
import concourse.bass as bass
import concourse.mybir as mybir
from concourse.bass_utils import run_bass_kernel_spmd

F32 = mybir.dt.float32
BF16 = mybir.dt.bfloat16
AF = mybir.ActivationFunctionType
ALU = mybir.AluOpType
AX = mybir.AxisListType

P = 128
D = 1024
NK = 8
CIN = 2560
DFF = 2816
NJ = 22
CUP = 5632
TG = 512
EPS = 1e-6
CONV_K = 31
PV_CW = 0
PV_CB = PV_CW + 4 * 31
PV_LG = PV_CB + 4
PV_LB = PV_LG + 4
PV_FW = PV_LB + 4
PV_FB = PV_FW + 44 * 3
NPV = PV_FB + 44


class Buf:
    __slots__ = ("name", "w", "r", "excl")

    def __init__(self, name, excl=False):
        self.name = name
        self.w = None
        self.r = {}
        self.excl = excl


class Holder:
    def __init__(self, nc, name):
        self.name = name
        self.sem = nc.alloc_semaphore(name)
        self.cnt = 0


class Queue(Holder):
    def __init__(self, nc, eng, name):
        super().__init__(nc, "q_" + name)
        self.eng = eng
        self.seen = {}
        self.nwait = 0
        self.nins = 0
        self.inorder = False

    def wait_tok(self, holder, val):
        if holder is self and self.inorder:
            return
        if self.seen.get(holder, 0) >= val:
            return
        assert val <= holder.cnt, (self.name, holder.name, val, holder.cnt)
        self.eng.wait_ge(holder.sem, val)
        self.seen[holder] = val
        self.nwait += 1


def _deps(q, reads, writes):
    toks = {}
    for b in reads:
        if b.w is not None:
            h, v = b.w
            if toks.get(h, 0) < v:
                toks[h] = v
        if b.excl:
            for h, v in b.r.items():
                if h is not q and toks.get(h, 0) < v:
                    toks[h] = v
    for b in writes:
        if b.w is not None:
            h, v = b.w
            if toks.get(h, 0) < v:
                toks[h] = v
        for h, v in b.r.items():
            if toks.get(h, 0) < v:
                toks[h] = v
    for h, v in toks.items():
        q.wait_tok(h, v)


def _record(tok, reads, writes):
    h, v = tok
    for b in reads:
        if b.r.get(h, 0) < v:
            b.r[h] = v
    for b in writes:
        b.w = tok
        b.r = {}


def op(q, ins_fn, reads=(), writes=(), signal=True):
    _deps(q, reads, writes)
    ins = ins_fn()
    q.nins += 1
    if signal:
        ins.then_inc(q.sem, 1)
        q.cnt += 1
        tok = (q, q.cnt)
    else:
        tok = (q, q.cnt + 1)
    _record(tok, reads, writes)
    return ins


def dma(q, dsem, out, in_, reads=(), writes=()):
    _deps(q, reads, writes)
    ins = q.eng.dma_start(out=out, in_=in_)
    ins.then_inc(dsem.sem, 16)
    dsem.cnt += 16
    _record((dsem, dsem.cnt), reads, writes)
    return ins


class Arena:
    def __init__(self, nc, nbytes):
        self.nbytes = nbytes
        self.t = nc.alloc_sbuf_tensor("arena", [P, nbytes // 2], BF16)

    def view(self, off, shape, dt):
        esz = 4 if dt == F32 else 2
        n = 1
        for d_ in shape:
            n *= d_
        nb = n * esz
        assert off % 4 == 0 and off + nb <= self.nbytes, (off, nb, self.nbytes)
        ap = self.t[:, off // 2:(off + nb) // 2]
        if dt == F32:
            ap = ap.bitcast(F32)
        if len(shape) == 2:
            ap = ap.rearrange("p (a b) -> p a b", a=shape[0])
        elif len(shape) == 3:
            ap = ap.rearrange("p (a b c) -> p a b c", a=shape[0], b=shape[1])
        return ap


class Layout:
    def __init__(self, arena, base=0):
        self.arena = arena
        self.off = base
        self.hi = base

    def get(self, shape, dt):
        esz = 4 if dt == F32 else 2
        n = esz
        for d_ in shape:
            n *= d_
        off = (self.off + 31) // 32 * 32
        self.off = off + n
        self.hi = max(self.hi, self.off)
        return self.arena.view(off, shape, dt)


def build_program(L, S, lam_inits):
    NG = S // TG
    NT = S // P
    nc = bass.Bass("TRN2", target_bir_lowering=False)

    def din(name, shape):
        return nc.dram_tensor(name, list(shape), F32, kind="ExternalInput")

    x_d = din("x", [S, D])
    w_in_d = din("w_in", [L, D, CIN])
    w_out_d = din("w_out", [L, D, D])
    w_up_d = din("w_up", [L, D, CUP])
    w_dn_d = din("w_down", [L, DFF, D])
    pvec_d = din("pvec", [L, P, NPV])
    fvec_d = din("fvec", [L, 4, D])
    lamv_d = din("lamv", [L, 1, 256])
    sg_d = din("sublng", [L, 1, P])
    cmat_d = din("cmat", [3, P, P])
    rope_d = din("rope", [2, P, S])
    out_d = nc.dram_tensor("out", [S, D], F32, kind="ExternalOutput")
    xs_d = nc.dram_tensor("xs_scratch", [S, D], F32)

    pe = Queue(nc, nc.tensor, "pe")
    pe.inorder = True
    act = Queue(nc, nc.scalar, "act")
    dve = Queue(nc, nc.vector, "dve")
    pool = Queue(nc, nc.gpsimd, "pool")
    sp = Queue(nc, nc.sync, "sp")

    ARENA_BYTES = 212800
    arena = Arena(nc, ARENA_BYTES)
    com = Layout(arena, 0)
    ident = com.get([P], BF16)
    pm = com.get([P], BF16)
    cmask = com.get([P], BF16)
    ones = com.get([P], BF16)
    identf = com.get([P], F32)
    pv = com.get([NPV], F32)
    sg = com.get([P], F32)
    small = com.get([64], F32)
    g1 = com.get([D], F32)
    g2 = com.get([D], F32)
    xin = [com.get([D], F32) for _ in range(2)]
    xres = [com.get([D], F32)]
    hbf = [com.get([D], BF16) for _ in range(2)]
    hT = com.get([NK, TG], BF16)
    tmps = com.get([4, TG], F32)
    ropeCS = com.get([2, TG], F32)
    ropeC = ropeCS[:, 0, :]
    ropeS = ropeCS[:, 1, :]
    xres_rope = ropeCS.rearrange("p a b -> p (a b)")
    lamb = tmps[:, 3, 0:256]
    base_phase = (com.hi + 31) // 32 * 32
    mx = Layout(arena, base_phase)
    w_in = mx.get([NK, CIN], BF16)
    w_out = mx.get([NK, D], BF16)
    KT = mx.get([4, S], BF16)
    V = mx.get([NT, 4, 130], BF16)
    QT = mx.get([4, TG], BF16)
    Ubuf = mx.get([4, 30 + TG], BF16)
    UO = mx.get([NK, TG], BF16)
    ybf_off = (mx.off + 31) // 32 * 32
    ybf = mx.get([4, TG], BF16)
    xres_ybf = arena.view(ybf_off, [D], F32)
    ysq_off = (mx.off + 31) // 32 * 32
    ysq = mx.get([4, TG], BF16)
    rt2 = arena.view(ysq_off, [TG], F32)
    rt3 = arena.view(ysq_off + 2048, [TG], F32)
    rl = mx.get([2, 4], F32)
    ssr = mx.get([4], F32)
    xoff = (mx.off + 31) // 32 * 32
    Dg = mx.get([CONV_K, P], BF16)
    Pt = [[arena.view(xoff + (2 * c + s_) * 1024, [TG], BF16) for s_ in range(2)] for c in range(2)]
    obf = arena.view(xoff + 4096, [4, P], BF16)
    zbf = arena.view(xoff + 5120, [TG], BF16)
    DgB = mx.get([CONV_K, P], BF16)
    ff = Layout(arena, base_phase)
    w_up = ff.get([NK, CUP], BF16)
    w_dn = ff.get([NJ, D], BF16)
    HID = ff.get([NJ, TG], BF16)
    Rb = [ff.get([2 + TG], BF16) for _ in range(3)]
    halo = ff.get([44, 2], BF16)
    D3 = [ff.get([3, P], BF16) for _ in range(2)]
    print("SBUF layout: common %d, mixer %d, ffn %d (limit %d)" % (com.hi, mx.hi, ff.hi, ARENA_BYTES))
    assert mx.hi <= ARENA_BYTES and ff.hi <= ARENA_BYTES

    psum = nc.alloc_psum_tensor("psum", [P, 8 * 512], F32)

    def bank(i, n=1):
        return psum[:, i * 512:(i + n) * 512]

    pT = bank(0).bitcast(BF16)

    B = {}

    def b(name):
        if name not in B:
            B[name] = Buf(name)
        return B[name]

    pb = [b("psum%d" % i) for i in range(8)]
    for x_ in pb:
        x_.excl = True
    WI_BLK = ["wi_a", "wi_g", "wi_q", "wi_k", "wi_v"]
    xalias = ["Pt00", "Pt01", "Pt10", "Pt11", "obf", "zbf"]
    mixer_names = WI_BLK + ["w_out0", "w_out1", "QT", "Ubuf", "Dg", "DgB", "UO", "ybf", "ysq", "rl0", "rl1", "ssr", "Vones"] + xalias + \
        ["KT%d" % g for g in range(NG)] + ["V%d" % g for g in range(NG)]
    ffn_names = ["wu_g%d" % i for i in range(6)] + ["wu_v%d" % i for i in range(6)] + ["wd%d" % i for i in range(6)] + \
        ["HID", "R0", "R1", "R2", "halo", "D30", "D31"]
    mixer_bufs = [b(n) for n in mixer_names]
    ffn_bufs = [b(n) for n in ffn_names]
    xalias_bufs = [b(n) for n in xalias]
    hT_bufs = [b("hT%d" % i) for i in range(4)]

    dsem = {}

    def ds(name):
        if name not in dsem:
            dsem[name] = Holder(nc, "d_" + name)
        return dsem[name]

    def sc(i):
        return small[:, i:i + 1]
    S1, S2, E1, E2, NLAM = range(5)
    SSQ0, RSTD0, PSSQ0, PRSTD0 = 8, 10, 12, 14
    EPSC = 16

    dma(pool, ds("c0"), ident, cmat_d[0], writes=[b("ident")])
    dma(pool, ds("c0b"), pm, cmat_d[1], writes=[b("pm")])
    dma(pool, ds("c0c"), cmask, cmat_d[2], writes=[b("cmask")])
    dma(sp, ds("c1"), identf, cmat_d[0], writes=[b("identf")])
    op(dve, lambda: nc.vector.memset(ones, 1.0), writes=[b("ones")])
    op(dve, lambda: nc.vector.memset(sc(EPSC), EPS), writes=[b("epsc")])

    def norm_chain(T, src_d, srcbuf):
        s_ = T % 2
        xb, hb = b("xin%d" % s_), b("hbf%d" % s_)
        sq, rs = b("ssq%d" % s_), b("rstd%d" % s_)
        dma(sp, ds("xin%d" % s_), xin[s_], src_d[T * P:(T + 1) * P, :], reads=[b(srcbuf)], writes=[xb])
        op(dve, lambda: nc.vector.memset(sc(SSQ0 + s_), 0.0), writes=[sq])
        op(act, lambda: nc.scalar.activation(out=hbf[s_], in_=xin[s_], func=AF.Square, accum_out=sc(SSQ0 + s_)),
           reads=[xb], writes=[hb, sq])
        op(act, lambda: nc.scalar.activation(out=sc(RSTD0 + s_), in_=sc(SSQ0 + s_), func=AF.Ln, bias=sc(EPSC), scale=1.0 / D),
           reads=[sq, b("epsc")], writes=[rs])
        op(act, lambda: nc.scalar.activation(out=sc(RSTD0 + s_), in_=sc(RSTD0 + s_), func=AF.Exp, scale=-0.5), reads=[rs], writes=[rs])
        op(dve, lambda: nc.vector.scalar_tensor_tensor(out=hbf[s_], in0=xin[s_], scalar=sc(RSTD0 + s_), in1=g1,
                                                       op0=ALU.mult, op1=ALU.mult),
           reads=[xb, rs, b("g1")], writes=[hb])

    def norm_T(T):
        s_ = T % 2
        tt = T % 4
        hb = b("hbf%d" % s_)
        for kc in range(NK):
            op(pe, lambda kc=kc: nc.tensor.transpose(out=pT[:, kc * P:(kc + 1) * P], in_=hbf[s_][:, kc * P:(kc + 1) * P],
                                                   identity=ident),
               reads=[hb, b("ident")], writes=[pb[0]], signal=(kc == NK - 1))
        op(dve, lambda: nc.vector.tensor_copy(out=hT[:, :, tt * P:(tt + 1) * P],
                                              in_=pT[:, 0:NK * P].rearrange("p (k c) -> p k c", k=NK)),
           reads=[pb[0]], writes=[hT_bufs[tt]])

    def mm_group(out_ap, pairs, reads, wbuf, start=True):
        n = len(pairs)
        for i, (l_, r_) in enumerate(pairs):
            op(pe, lambda l_=l_, r_=r_, i=i: nc.tensor.matmul(out=out_ap, lhsT=l_, rhs=r_, start=(start and i == 0),
                                                              stop=(i == n - 1)),
               reads=reads, writes=[wbuf], signal=(i == n - 1))

    def post_norm_residual(T, mbanks, src_ap, dst_ap, src_buf_name, dst_buf_name):
        s_ = T % 2
        m_ap = bank(mbanks[0], 2)
        mb = [pb[mbanks[0]], pb[mbanks[1]]]
        if s_ == 0:
            xr, xbs = xres[0], [b("xres0")]
        elif phase[0] == "mixer":
            xr, xbs = xres_ybf, [b("ybf")]
        else:
            xr, xbs = xres_rope, [b("ropeC"), b("ropeS")]
        sq, rs = b("pssq%d" % s_), b("prstd%d" % s_)
        dma(sp, ds("xres%d" % s_), xr, src_ap, reads=[b(src_buf_name)], writes=xbs)
        op(dve, lambda: nc.vector.memset(sc(PSSQ0 + s_), 0.0), writes=[sq])
        tmp32 = tmps[:, 2 * s_:2 * s_ + 2, :].rearrange("p a b -> p (a b)")
        tb = [b("tmp%d" % (2 * s_)), b("tmp%d" % (2 * s_ + 1))]
        op(act, lambda: nc.scalar.activation(out=tmp32, in_=m_ap, func=AF.Square, accum_out=sc(PSSQ0 + s_)),
           reads=mb, writes=tb + [sq])
        op(act, lambda: nc.scalar.activation(out=sc(PRSTD0 + s_), in_=sc(PSSQ0 + s_), func=AF.Ln, bias=sc(EPSC), scale=1.0 / D),
           reads=[sq, b("epsc")], writes=[rs])
        op(act, lambda: nc.scalar.activation(out=sc(PRSTD0 + s_), in_=sc(PRSTD0 + s_), func=AF.Exp, scale=-0.5), reads=[rs], writes=[rs])
        op(dve, lambda: nc.vector.scalar_tensor_tensor(out=tmp32, in0=m_ap, scalar=sc(PRSTD0 + s_), in1=g2,
                                                       op0=ALU.mult, op1=ALU.mult),
           reads=mb + [rs, b("g2")], writes=tb)
        op(dve, lambda: nc.vector.tensor_tensor(out=xr, in0=xr, in1=tmp32, op=ALU.add),
           reads=tb + xbs, writes=xbs)
        dma(sp, ds("xst%d" % s_), dst_ap, xr, reads=xbs, writes=[b(dst_buf_name)])

    rot = [1, 2, 3]
    rr = [0]
    phase = ["mixer"]
    srot = [0]

    def nbs():
        i = srot[0] % 4
        srot[0] += 1
        return i

    def nb():
        i = rot[rr[0] % len(rot)]
        rr[0] += 1
        return i

    for l in range(L):
        lam_init = lam_inits[l]
        src_d = x_d if l == 0 else xs_d
        dma(sp, ds("pv"), pv, pvec_d[l], writes=[b("pv")])
        dma(sp, ds("lamb"), lamb, lamv_d[l].broadcast_to([P, 256]), writes=[b("tmp3")])
        dma(sp, ds("sg"), sg, sg_d[l].broadcast_to([P, P]), writes=[b("sg")])
        dma(sp, ds("g1"), g1, fvec_d[l, 0:1, :].broadcast_to([P, D]), writes=[b("g1")])
        dma(sp, ds("g2"), g2, fvec_d[l, 1:2, :].broadcast_to([P, D]), writes=[b("g2")])
        jt = tmps[:, 2, 0:64]
        op(dve, lambda: nc.vector.scalar_tensor_tensor(out=jt, in0=lamb[:, 0:64], scalar=1.0, in1=lamb[:, 64:128],
                                                       op0=ALU.mult, op1=ALU.mult, accum_out=sc(S1)),
           reads=[b("tmp3")], writes=[b("tmp2"), b("s1")])
        op(dve, lambda: nc.vector.scalar_tensor_tensor(out=jt, in0=lamb[:, 128:192], scalar=1.0, in1=lamb[:, 192:256],
                                                       op0=ALU.mult, op1=ALU.mult, accum_out=sc(S2)),
           reads=[b("tmp3")], writes=[b("tmp2"), b("s2")])
        op(act, lambda: nc.scalar.activation(out=sc(E1), in_=sc(S1), func=AF.Exp), reads=[b("s1")], writes=[b("e1")])
        op(act, lambda: nc.scalar.activation(out=sc(E2), in_=sc(S2), func=AF.Exp), reads=[b("s2")], writes=[b("e2")])
        op(dve, lambda: nc.vector.tensor_tensor(out=sc(NLAM), in0=sc(E2), in1=sc(E1), op=ALU.subtract),
           reads=[b("e1"), b("e2")], writes=[b("nlam")])
        op(dve, lambda: nc.vector.tensor_scalar(out=sc(NLAM), in0=sc(NLAM), scalar1=-float(lam_init), scalar2=None,
                                                op0=ALU.add),
           reads=[b("nlam")], writes=[b("nlam")])
        op(act, lambda: nc.scalar.mul(out=sg, in_=sg, mul=float(1.0 - lam_init)), reads=[b("sg")], writes=[b("sg")])

        first = True
        for blk in (1, 0, 2, 3, 4):
            c0 = blk * 512
            dma(pool, ds("w_in%d" % blk), w_in[:, :, c0:c0 + 512],
                w_in_d[l, :, c0:c0 + 512].rearrange("(k p) c -> p k c", p=P),
                writes=[b(WI_BLK[blk])] + (ffn_bufs if first else []))
            first = False
        for hf in range(2):
            dma(pool, ds("w_out%d" % hf), w_out[:, hf * 4:(hf + 1) * 4, :],
                w_out_d[l, hf * 512:(hf + 1) * 512, :].rearrange("(k p) c -> p k c", p=P), writes=[b("w_out%d" % hf)])
        op(pool, lambda: nc.gpsimd.memset(Ubuf[:, :, 0:30], 0.0), writes=[b("Ubuf")])
        op(pool, lambda: nc.gpsimd.memset(V[:, :, :, 128:130], 1.0), writes=[b("Vones")])

        phase[0] = "mixer"
        for T in range(4):
            norm_chain(T, src_d, "xs%d" % T) if T < 2 else None
        norm_T(0)
        norm_T(1)
        norm_chain(2, src_d, "xs2")
        norm_chain(3, src_d, "xs3")
        norm_T(2)
        norm_T(3)
        for G in range(NG):
            t0 = G * TG
            nxt = G + 1 < NG
            def rope_load(Gt):
                dma(sp, ds("ropeC"), ropeC, rope_d[0, :, Gt * TG:(Gt + 1) * TG], writes=[b("ropeC")])
                dma(sp, ds("ropeS"), ropeS, rope_d[1, :, Gt * TG:(Gt + 1) * TG], writes=[b("ropeS")])

            if G == 0:
                rope_load(0)
            if nxt and G == 0:
                for T in (4, 5):
                    norm_chain(T, src_d, "xs%d" % T)

            def proj(col0, bi):
                mm_group(bank(bi), [(w_in[:, kc, col0:col0 + P], hT[:, kc, :]) for kc in range(NK)],
                         [b(WI_BLK[col0 // 512])] + hT_bufs, pb[bi])

            def dg_gen(ct):
                wv = pv[:, PV_CW + ct * CONV_K:PV_CW + (ct + 1) * CONV_K]
                dgt, dgb = (DgB, b("DgB")) if ct % 2 == 0 else (Dg, b("Dg"))
                op(pool, lambda: nc.gpsimd.tensor_tensor(out=dgt, in0=identf.unsqueeze(1).broadcast_to([P, CONV_K, P]),
                                                         in1=wv.unsqueeze(2).broadcast_to([P, CONV_K, P]), op=ALU.mult),
                   reads=[b("identf"), b("pv")], writes=[dgb] + (xalias_bufs if ct % 2 == 1 else []))

            def glu_step(ct):
                bg = nb()
                proj(512 + ct * P, bg)
                sgt = tmps[:, 3, :]
                sgb = b("tmp3")
                op(act, lambda: nc.scalar.activation(out=sgt, in_=bank(bg), func=AF.Sigmoid), reads=[pb[bg]], writes=[sgb])
                ba = nb()
                proj(ct * P, ba)
                op(dve, lambda: nc.vector.tensor_tensor(out=Ubuf[:, ct, 30:30 + TG], in0=bank(ba), in1=sgt, op=ALU.mult),
                   reads=[pb[ba], sgb], writes=[b("Ubuf")])

            def halo_copy():
                op(pool, lambda: nc.gpsimd.tensor_copy(out=Ubuf[:, :, 0:30], in_=Ubuf[:, :, TG:TG + 30]),
                   reads=[b("Ubuf")], writes=[b("Ubuf")])

            dg_gen(0)
            if G == 0:
                for ct in range(4):
                    glu_step(ct)
            def m3b_unit(Gt, which, h):
                col0 = 1024 + which * 512 + h * P
                bz = nb()
                proj(col0, bz)
                op(act, lambda: nc.scalar.copy(out=zbf, in_=bank(bz)), reads=[pb[bz]], writes=[b("zbf")])
                bp = nb()
                mm_group(bank(bp), [(pm, zbf)], [b("pm"), b("zbf")], pb[bp])
                op(dve, lambda: nc.vector.tensor_tensor(out=rt2, in0=bank(bz), in1=ropeC, op=ALU.mult),
                   reads=[pb[bz], b("ropeC")], writes=[b("ysqA")])
                op(dve, lambda: nc.vector.tensor_tensor(out=rt3, in0=bank(bp), in1=ropeS, op=ALU.mult),
                   reads=[pb[bp], b("ropeS")], writes=[b("ysqB")])
                if which == 0:
                    dst, dbuf = QT[:, h, :], b("QT")
                else:
                    dst, dbuf = KT[:, h, Gt * TG:(Gt + 1) * TG], b("KT%d" % Gt)
                op(dve, lambda: nc.vector.tensor_tensor(out=dst, in0=rt2, in1=rt3, op=ALU.add),
                   reads=[b("ysqA"), b("ysqB")], writes=[dbuf])

            if G == 0:
                for which in range(2):
                    for h in range(4):
                        m3b_unit(0, which, h)
            if nxt:
                rope_load(G + 1)
            for tt in range(4):
                T = G * 4 + tt
                bv = nb()
                mm_group(bank(bv), [(hT[:, kc, tt * P:(tt + 1) * P], w_in[:, kc, 2048:2560]) for kc in range(NK)],
                         [b("wi_v"), hT_bufs[tt]], pb[bv])
                op(act, lambda bv=bv, T=T: nc.scalar.copy(out=V[:, T, :, 0:128],
                                                          in_=bank(bv).rearrange("p (h c) -> p h c", h=4)),
                   reads=[pb[bv]], writes=[b("V%d" % G)])
            if nxt:
                norm_T(4 * G + 4)
                norm_T(4 * G + 5)
                for T in (4 * G + 6, 4 * G + 7):
                    norm_chain(T, src_d, "xs%d" % T)
            dg_gen(1)
            for ct in range(4):
                dgt, dgb = (DgB, b("DgB")) if ct % 2 == 0 else (Dg, b("Dg"))
                bc = nb()
                mm_group(bank(bc), [(dgt[:, k, :], Ubuf[:, ct, k:k + TG]) for k in range(CONV_K)],
                         [dgb, b("Ubuf")], pb[bc])
                if ct + 2 < 4:
                    dg_gen(ct + 2)
                cb = pv[:, PV_CB + ct:PV_CB + ct + 1]
                op(act, lambda bc=bc, ct=ct, cb=cb: nc.scalar.activation(out=ybf[:, ct, :], in_=bank(bc), func=AF.Identity, bias=cb),
                   reads=[pb[bc], b("pv")], writes=[b("ybf")])
                op(act, lambda bc=bc, ct=ct, cb=cb: nc.scalar.activation(out=ysq[:, ct, :], in_=bank(bc), func=AF.Square, bias=cb),
                   reads=[pb[bc], b("pv")], writes=[b("ysq"), b("ysqA"), b("ysqB")])
            b1 = nb()
            mm_group(bank(b1), [(ones, ybf[:, ct, :]) for ct in range(4)], [b("ones"), b("ybf")], pb[b1])
            b2 = nb()
            mm_group(bank(b2), [(ones, ysq[:, ct, :]) for ct in range(4)], [b("ones"), b("ysq")], pb[b2])
            mean, msq, rstd_t = tmps[:, 0, :], tmps[:, 1, :], tmps[:, 2, :]
            op(act, lambda: nc.scalar.mul(out=mean, in_=bank(b1), mul=1.0 / 512), reads=[pb[b1]], writes=[b("tmp0")])
            op(act, lambda: nc.scalar.activation(out=msq, in_=bank(b1), func=AF.Square, scale=1.0 / 512),
               reads=[pb[b1]], writes=[b("tmp1")])
            op(dve, lambda: nc.vector.scalar_tensor_tensor(out=rstd_t, in0=bank(b2), scalar=1.0 / 512, in1=msq,
                                                           op0=ALU.mult, op1=ALU.subtract),
               reads=[pb[b2], b("tmp1")], writes=[b("tmp2")])
            t3 = tmps[:, 3, :]

            def ln_a2():
                op(act, lambda: nc.scalar.activation(out=rstd_t, in_=rstd_t, func=AF.Ln, bias=sc(EPSC), scale=1.0),
                   reads=[b("tmp2"), b("epsc")], writes=[b("tmp2")])
                op(act, lambda: nc.scalar.activation(out=rstd_t, in_=rstd_t, func=AF.Exp, scale=-0.5), reads=[b("tmp2")], writes=[b("tmp2")])

            def ln_d2(ct):
                op(dve, lambda: nc.vector.tensor_tensor(out=t3, in0=ybf[:, ct, :], in1=mean, op=ALU.subtract),
                   reads=[b("ybf"), b("tmp0")], writes=[b("tmp3")])
                op(dve, lambda: nc.vector.tensor_tensor(out=t3, in0=t3, in1=rstd_t, op=ALU.mult),
                   reads=[b("tmp3"), b("tmp2")], writes=[b("tmp3")])

            def ln_a3(ct):
                op(act, lambda: nc.scalar.activation(out=UO[:, ct, :], in_=t3, func=AF.Silu,
                                                     scale=pv[:, PV_LG + ct:PV_LG + ct + 1],
                                                     bias=pv[:, PV_LB + ct:PV_LB + ct + 1]),
                   reads=[b("tmp3"), b("pv")], writes=[b("UO")])

            accv = [psum[:, (4 + 2 * c) * 512:(6 + 2 * c) * 512].rearrange("p (q w) -> p q w", q=4) for c in range(2)]
            accb = [[pb[4], pb[5]], [pb[6], pb[7]]]
            nkt = 4 * G + 4
            sbank = {}

            def emit_qk(h, kt, c):
                r = kt - 4 * G
                c0 = max(r, 0) * P
                sbk = nbs()
                sbank[(kt, c)] = sbk
                mm_group(bank(sbk)[:, c0:TG],
                         [(KT[c * 64:(c + 1) * 64, h, kt * P:(kt + 1) * P], QT[c * 64:(c + 1) * 64, h, c0:TG])],
                         [b("KT%d" % (kt // 4)), b("QT")], pb[sbk])

            def emit_exp(h, kt, c):
                r = kt - 4 * G
                c0 = max(r, 0) * P
                sbk = sbank[(kt, c)]
                sl = kt % 2
                ptb = b("Pt%d%d" % (c, sl))
                pt = Pt[c][sl]
                op(act, lambda: nc.scalar.activation(out=pt[:, c0:TG], in_=bank(sbk)[:, c0:TG], func=AF.Exp, scale=0.125),
                   reads=[pb[sbk]], writes=[ptb])
                if r >= 0:
                    op(dve, lambda: nc.vector.tensor_tensor(out=pt[:, c0:c0 + P], in0=pt[:, c0:c0 + P], in1=cmask, op=ALU.mult),
                       reads=[ptb, b("cmask")], writes=[ptb])

            def emit_av(h, kt, c):
                r = kt - 4 * G
                q0 = max(r, 0)
                sl = kt % 2
                ptb = b("Pt%d%d" % (c, sl))
                pt = Pt[c][sl]
                for ql in range(q0, 4):
                    bk = 4 + 2 * c + ql // 2
                    reg = psum[:, bk * 512 + (ql % 2) * 256: bk * 512 + (ql % 2) * 256 + 129]
                    first_ = (kt == 0 and ql % 2 == 0)
                    last_ = (kt == 4 * G + ql)
                    op(pe, lambda reg=reg, ql=ql, first_=first_, last_=last_:
                       nc.tensor.matmul(out=reg, lhsT=pt[:, ql * P:(ql + 1) * P], rhs=V[:, kt, h, 0:129],
                                        start=first_, stop=last_, skip_group_check=True),
                       reads=[ptb, b("V%d" % (kt // 4)), b("Vones")], writes=[pb[bk]], signal=(ql == 3))

            def finalize_a(h):
                t1 = tmps[:, 0, :].rearrange("p (q w) -> p q w", q=4)
                t2 = tmps[:, 1, :].rearrange("p (q w) -> p q w", q=4)
                o3 = tmps[:, 2, :].rearrange("p (q w) -> p q w", q=4)
                for c in range(2):
                    op(dve, lambda c=c: nc.vector.reciprocal(out=rl[:, c, :], in_=accv[c][:, :, 128]),
                       reads=accb[c], writes=[b("rl%d" % c)])
                op(dve, lambda: nc.vector.tensor_copy(out=t1, in_=accv[0][:, :, 0:128]), reads=accb[0], writes=[b("tmp0")])
                op(act, lambda: nc.scalar.copy(out=t2, in_=accv[1][:, :, 0:128]), reads=accb[1], writes=[b("tmp1")])
                op(dve, lambda: nc.vector.tensor_scalar(out=rl[:, 1, :], in0=rl[:, 1, :], scalar1=sc(NLAM), scalar2=None,
                                                        op0=ALU.mult),
                   reads=[b("rl1"), b("nlam")], writes=[b("rl1")])
                op(dve, lambda: nc.vector.tensor_tensor(out=t1, in0=t1,
                                                        in1=rl[:, 0, :].unsqueeze(2).broadcast_to([P, 4, P]), op=ALU.mult),
                   reads=[b("tmp0"), b("rl0")], writes=[b("tmp0")])
                op(dve, lambda: nc.vector.tensor_tensor(out=t2, in0=t2,
                                                        in1=rl[:, 1, :].unsqueeze(2).broadcast_to([P, 4, P]), op=ALU.mult),
                   reads=[b("tmp1"), b("rl1")], writes=[b("tmp1")])
                op(dve, lambda: nc.vector.tensor_tensor(out=o3, in0=t1, in1=t2, op=ALU.add),
                   reads=[b("tmp0"), b("tmp1")], writes=[b("tmp2")])
                op(dve, lambda: nc.vector.tensor_tensor(out=t1, in0=o3, in1=o3, op=ALU.mult),
                   reads=[b("tmp2")], writes=[b("tmp0")])
                op(dve, lambda: nc.vector.tensor_reduce(out=ssr, in_=t1, axis=AX.X, op=ALU.add),
                   reads=[b("tmp0")], writes=[b("ssr")])
                op(act, lambda: nc.scalar.activation(out=ssr, in_=ssr, func=AF.Ln, bias=sc(EPSC), scale=1.0 / 128),
                   reads=[b("ssr"), b("epsc")], writes=[b("ssr")])
                op(act, lambda: nc.scalar.activation(out=ssr, in_=ssr, func=AF.Exp, scale=-0.5), reads=[b("ssr")], writes=[b("ssr")])
                op(dve, lambda: nc.vector.tensor_tensor(out=o3, in0=o3, in1=ssr.unsqueeze(2).broadcast_to([P, 4, P]), op=ALU.mult),
                   reads=[b("tmp2"), b("ssr")], writes=[b("tmp2")])
                op(dve, lambda: nc.vector.tensor_tensor(out=obf, in0=o3, in1=sg.unsqueeze(1).broadcast_to([P, 4, P]), op=ALU.mult),
                   reads=[b("tmp2"), b("sg")], writes=[b("obf")])

            def finalize_b(h):
                for ql in range(4):
                    op(pe, lambda ql=ql: nc.tensor.transpose(out=pT[:, ql * P:(ql + 1) * P], in_=obf[:, ql, :], identity=ident),
                       reads=[b("obf"), b("ident")], writes=[pb[0]], signal=(ql == 3))
                op(act, lambda: nc.scalar.copy(out=UO[:, 4 + h, :], in_=pT[:, 0:TG]), reads=[pb[0]], writes=[b("UO")])

            deferred = None
            for h in range(4):
                emit_qk(h, 0, 0)
                emit_qk(h, 0, 1)
                for kt in range(nkt):
                    if kt + 1 < nkt:
                        emit_qk(h, kt + 1, 0)
                        emit_qk(h, kt + 1, 1)
                    emit_exp(h, kt, 0)
                    emit_exp(h, kt, 1)
                    if h == 0 and kt < 4:
                        if kt == 0:
                            ln_a2()
                        else:
                            ln_a3(kt - 1)
                        ln_d2(kt)
                    emit_av(h, kt, 0)
                    emit_av(h, kt, 1)
                    if kt == 2 and deferred is not None:
                        finalize_b(deferred)
                        deferred = None
                if h == 0:
                    ln_a3(3)
                if h == 3 and G + 2 < NG:
                    for T in (4 * G + 8, 4 * G + 9):
                        norm_chain(T, src_d, "xs%d" % T)
                if h == 3 and nxt:
                    halo_copy()
                    glu_step(0)
                    glu_step(1)
                if h == 0 and nxt:
                    norm_T(4 * G + 6)
                    norm_T(4 * G + 7)
                finalize_a(h)
                deferred = h
            finalize_b(deferred)
            def wout_tile(tt):
                T = G * 4 + tt
                mbanks = (4, 5) if tt % 2 == 0 else (6, 7)
                for half in range(2):
                    mm_group(bank(mbanks[half]),
                             [(UO[:, cc, tt * P:(tt + 1) * P], w_out[:, cc, half * 512:(half + 1) * 512]) for cc in range(NK)],
                             [b("UO"), b("w_out0"), b("w_out1")], pb[mbanks[half]])
                post_norm_residual(T, mbanks, src_d[T * P:(T + 1) * P, :], xs_d[T * P:(T + 1) * P, :], "xs%d" % T, "xs%d" % T)

            wout_tile(0)
            wout_tile(1)
            if nxt:
                glu_step(2)
                glu_step(3)
                m3b_unit(G + 1, 0, 0)
                m3b_unit(G + 1, 0, 1)
            wout_tile(2)
            if nxt:
                m3b_unit(G + 1, 0, 2)
                m3b_unit(G + 1, 0, 3)
            wout_tile(3)
            if nxt:
                for h in range(4):
                    m3b_unit(G + 1, 1, h)

        phase[0] = "ffn"
        dma(sp, ds("g1"), g1, fvec_d[l, 2:3, :].broadcast_to([P, D]), writes=[b("g1")])
        dma(sp, ds("g2"), g2, fvec_d[l, 3:4, :].broadcast_to([P, D]), writes=[b("g2")])
        first = True
        for blk in range(6):
            w_ = min(512, DFF - blk * 512)
            for which in range(2):
                c0 = which * DFF + blk * 512
                dma(pool, ds("w_up%d%d" % (which, blk)), w_up[:, :, c0:c0 + w_],
                    w_up_d[l, :, c0:c0 + w_].rearrange("(k p) c -> p k c", p=P),
                    writes=[b(("wu_g%d" if which == 0 else "wu_v%d") % blk)] + (mixer_bufs if first else []))
                first = False
        for gq in range(6):
            j0 = gq * 4
            j1 = min(j0 + 4, NJ)
            dma(pool, ds("w_dn%d" % gq), w_dn[:, j0:j1, :],
                w_dn_d[l, j0 * P:j1 * P, :].rearrange("(j p) c -> p j c", p=P), writes=[b("wd%d" % gq)])
        op(dve, lambda: nc.vector.memset(halo, 0.0), writes=[b("halo")])
        dst_d = out_d if l == L - 1 else xs_d
        norm_chain(0, xs_d, "xs0")
        norm_chain(1, xs_d, "xs1")
        norm_T(0)
        norm_T(1)
        norm_chain(2, xs_d, "xs2")
        norm_chain(3, xs_d, "xs3")
        norm_T(2)
        norm_T(3)
        for G in range(NG):
            nxt = G + 1 < NG
            if nxt and G == 0:
                for T in (4, 5):
                    norm_chain(T, xs_d, "xs%d" % T)
            ubank = {}

            def up_tile(n):
                jp, which = n // 2, n % 2
                j = which * NJ + jp
                col0 = which * DFF + jp * P
                s_ = n % 3
                bu = 1 + n % 3
                ubank[n] = bu
                wb_ = b(("wu_g%d" if which == 0 else "wu_v%d") % (jp // 4))
                mm_group(bank(bu), [(w_up[:, kc, col0:col0 + P], hT[:, kc, :]) for kc in range(NK)],
                         [wb_] + hT_bufs, pb[bu])
                rbuf = b("R%d" % s_)
                op(dve, lambda: nc.vector.tensor_copy(out=Rb[s_][:, 0:2], in_=halo[:, j, :]),
                   reads=[b("halo")], writes=[rbuf])
                op(act, lambda: nc.scalar.copy(out=Rb[s_][:, 2:2 + TG], in_=bank(bu)), reads=[pb[bu]], writes=[rbuf])
                op(dve, lambda: nc.vector.tensor_copy(out=halo[:, j, :], in_=Rb[s_][:, TG:TG + 2]),
                   reads=[rbuf], writes=[b("halo")])
                base_ = 4 if jp % 2 == 0 else 6
                bcv = base_ + which
                if which == 0:
                    op(act, lambda: nc.scalar.activation(out=bank(bcv), in_=bank(bu), func=AF.Identity,
                                                         scale=pv[:, PV_FW + j * 3 + 2:PV_FW + j * 3 + 3],
                                                         bias=pv[:, PV_FB + j:PV_FB + j + 1]),
                       reads=[pb[bu], b("pv")], writes=[pb[bcv]])
                else:
                    fw = pv[:, PV_FW + j * 3:PV_FW + j * 3 + 3]
                    op(pool, lambda: nc.gpsimd.tensor_tensor(out=D3[jp % 2], in0=identf.unsqueeze(1).broadcast_to([P, 3, P]),
                                                             in1=fw.unsqueeze(2).broadcast_to([P, 3, P]), op=ALU.mult),
                       reads=[b("identf"), b("pv")], writes=[b("D3%d" % (jp % 2))])

            def conv_tile(n):
                jp, which = n // 2, n % 2
                j = which * NJ + jp
                s_ = n % 3
                base_ = 4 if jp % 2 == 0 else 6
                bcv = base_ + which
                rbuf = b("R%d" % s_)
                if which == 0:
                    for k in (1, 0):
                        op(dve, lambda k=k: nc.vector.scalar_tensor_tensor(out=bank(bcv), in0=Rb[s_][:, k:k + TG],
                                                                           scalar=pv[:, PV_FW + j * 3 + k:PV_FW + j * 3 + k + 1],
                                                                           in1=bank(bcv), op0=ALU.mult, op1=ALU.add),
                           reads=[rbuf, pb[bcv], b("pv")], writes=[pb[bcv]])
                    gl = tmps[:, 2 + jp % 2, :]
                    glb = b("tmp%d" % (2 + jp % 2))
                    op(act, lambda: nc.scalar.activation(out=gl, in_=bank(base_), func=AF.Gelu_apprx_tanh),
                       reads=[pb[base_]], writes=[glb])
                else:
                    mm_group(bank(bcv), [(D3[jp % 2][:, k, :], Rb[s_][:, k:k + TG]) for k in range(3)],
                             [b("D3%d" % (jp % 2)), rbuf], pb[bcv])
                    gl = tmps[:, 2 + jp % 2, :]
                    glb = b("tmp%d" % (2 + jp % 2))
                    op(dve, lambda: nc.vector.scalar_tensor_tensor(out=HID[:, jp, :], in0=bank(bcv),
                                                                   scalar=pv[:, PV_FB + j:PV_FB + j + 1],
                                                                   in1=gl, op0=ALU.add, op1=ALU.mult),
                       reads=[pb[bcv], glb, b("pv")], writes=[b("HID")])

            def down_tile(tt):
                T = G * 4 + tt
                mbanks = (6, 7) if tt % 2 == 0 else (4, 5)
                for half in range(2):
                    for jp in range(NJ):
                        op(pe, lambda jp=jp, half=half: nc.tensor.matmul(out=bank(mbanks[half]), lhsT=HID[:, jp, tt * P:(tt + 1) * P],
                                                                         rhs=w_dn[:, jp, half * 512:(half + 1) * 512],
                                                                         start=(jp == 0), stop=(jp == NJ - 1)),
                           reads=[b("HID"), b("wd%d" % (jp // 4))], writes=[pb[mbanks[half]]], signal=(jp == NJ - 1))
                post_norm_residual(T, mbanks, xs_d[T * P:(T + 1) * P, :], dst_d[T * P:(T + 1) * P, :],
                                   "xs%d" % T, ("out%d" % T) if l == L - 1 else ("xs%d" % T))

            NTL = 2 * NJ
            up_tile(0)
            up_tile(1)
            for n in range(NTL):
                if n + 2 < NTL:
                    up_tile(n + 2)
                conv_tile(n)
            if nxt:
                norm_T(4 * G + 4)
                norm_T(4 * G + 5)
                for T in (4 * G + 6, 4 * G + 7):
                    norm_chain(T, xs_d, "xs%d" % T)
            down_tile(0)
            down_tile(1)
            if nxt:
                norm_T(4 * G + 6)
                norm_T(4 * G + 7)
            if G + 2 < NG:
                for T in (4 * G + 8, 4 * G + 9):
                    norm_chain(T, xs_d, "xs%d" % T)
            down_tile(2)
            down_tile(3)

    for nm in ("xst0", "xst1"):
        sp.eng.wait_ge(dsem[nm].sem, dsem[nm].cnt)
    stats = {q.name: (q.nins, q.nwait) for q in (pe, act, dve, pool, sp)}
    print("instr/wait counts:", stats)
    return nc


def _consts(S):
    ident = np.eye(P, dtype=np.float32)
    pmat = np.zeros((P, P), np.float32)
    for base in (0, 64):
        for i in range(8):
            pmat[base + i + 8, base + i] = -1.0
            pmat[base + i, base + i + 8] = 1.0
    mask = (np.arange(P)[:, None] <= np.arange(P)[None, :]).astype(np.float32)
    pos = np.arange(S, dtype=np.float32)
    inv_freq = (np.float32(500000.0) ** (-np.arange(0, 16, 2, dtype=np.float32) / np.float32(16))).astype(np.float32)
    ang = (pos[:, None] * inv_freq[None, :]).astype(np.float32)
    cos = np.cos(ang).astype(np.float32)
    sin = np.sin(ang).astype(np.float32)
    C = np.ones((P, S), np.float32)
    Sn = np.zeros((P, S), np.float32)
    for base in (0, 64):
        for i in range(16):
            C[base + i] = cos[:, i % 8]
            Sn[base + i] = sin[:, i % 8]
    return np.stack([ident, pmat, mask]), np.stack([C, Sn])


def _pack_small(inp, L):
    pvec = np.zeros((L, P, NPV), np.float32)
    cw = np.asarray(inp["conv_w"], np.float32)
    pvec[:, :, PV_CW:PV_CW + 124] = cw.reshape(L, 31, 4, P).transpose(0, 3, 2, 1).reshape(L, P, 124)
    for name, off in (("conv_b", PV_CB), ("conv_ln_g", PV_LG), ("conv_ln_b", PV_LB)):
        pvec[:, :, off:off + 4] = np.asarray(inp[name], np.float32).reshape(L, 4, P).transpose(0, 2, 1)
    fw = np.asarray(inp["ffn_conv_w"], np.float32)
    pvec[:, :, PV_FW:PV_FW + 132] = fw.reshape(L, 3, 44, P).transpose(0, 3, 2, 1).reshape(L, P, 132)
    pvec[:, :, PV_FB:PV_FB + 44] = np.asarray(inp["ffn_conv_b"], np.float32).reshape(L, 44, P).transpose(0, 2, 1)
    fvec = np.stack([np.asarray(inp[k], np.float32) for k in
                     ("pre_mix_norm", "post_mix_norm", "pre_ffn_norm", "post_ffn_norm")], axis=1)
    lamv = np.concatenate([np.asarray(inp[k], np.float32) for k in
                           ("lambda_q1", "lambda_k1", "lambda_q2", "lambda_k2")], axis=1).reshape(L, 1, 256)
    sgv = np.asarray(inp["subln_g"], np.float32).reshape(L, 1, P)
    return pvec, np.ascontiguousarray(fvec), np.ascontiguousarray(lamv), sgv


_PROG_CACHE = {}


def kernel(**inputs):
    x = np.asarray(inputs["x"], np.float32)
    Bsz, S, _ = x.shape
    L = np.asarray(inputs["w_in"]).shape[0]
    lam_inits = [0.8 - 0.6 * math.exp(-0.3 * l) for l in range(L)]
    key = (L, S)
    if key not in _PROG_CACHE:
        _PROG_CACHE[key] = build_program(L, S, lam_inits)
    nc = _PROG_CACHE[key]
    cmat, rope = _consts(S)
    pvec, fvec, lamv, sgv = _pack_small(inputs, L)
    shared = {
        "w_in": np.ascontiguousarray(inputs["w_in"], np.float32),
        "w_out": np.ascontiguousarray(inputs["w_out"], np.float32),
        "w_up": np.ascontiguousarray(inputs["w_up"], np.float32),
        "w_down": np.ascontiguousarray(inputs["w_down"], np.float32),
        "pvec": pvec, "fvec": fvec, "lamv": lamv, "sublng": sgv, "cmat": cmat, "rope": rope,
    }
    in_maps = []
    for c in range(Bsz):
        m = dict(shared)
        m["x"] = np.ascontiguousarray(x[c])
        in_maps.append(m)
    res = run_bass_kernel_spmd(nc, in_maps, core_ids=list(range(Bsz)))
    return np.stack([np.asarray(r["out"], np.float32) for r in res.results], axis=0)
```

```python
import math
import numpy as np
import concourse.bass as bass
import concourse.mybir as mybir
from concourse.bass_utils import run_bass_kernel_spmd

F32 = mybir.dt.float32
BF16 = mybir.dt.bfloat16
AF = mybir.ActivationFunctionType
ALU = mybir.AluOpType
AX = mybir.AxisListType

P = 128
D = 1024
NK = 8
CIN = 2560
DFF = 2816
NJ = 22
CUP = 5632
TG = 512
EPS = 1e-6
CONV_K = 31
PV_CW = 0
PV_CB = PV_CW + 4 * 31
PV_LG = PV_CB + 4
PV_LB = PV_LG + 4
PV_FW = PV_LB + 4
PV_FB = PV_FW + 44 * 3
NPV = PV_FB + 44


class Buf:
    __slots__ = ("name", "w", "r", "excl")

    def __init__(self, name, excl=False):
        self.name = name
        self.w = None
        self.r = {}
        self.excl = excl


class Holder:
    def __init__(self, nc, name):
        self.name = name
        self.sem = nc.alloc_semaphore(name)
        self.cnt = 0


class Queue(Holder):
    def __init__(self, nc, eng, name):
        super().__init__(nc, "q_" + name)
        self.eng = eng
        self.seen = {}
        self.nwait = 0
        self.nins = 0
        self.inorder = False

    def wait_tok(self, holder, val):
        if holder is self and self.inorder:
            return
        if self.seen.get(holder, 0) >= val:
            return
        assert val <= holder.cnt, (self.name, holder.name, val, holder.cnt)
        self.eng.wait_ge(holder.sem, val)
        self.seen[holder] = val
        self.nwait += 1


def _deps(q, reads, writes):
    toks = {}
    for b in reads:
        if b.w is not None:
            h, v = b.w
            if toks.get(h, 0) < v:
                toks[h] = v
        if b.excl:
            for h, v in b.r.items():
                if h is not q and toks.get(h, 0) < v:
                    toks[h] = v
    for b in writes:
        if b.w is not None:
            h, v = b.w
            if toks.get(h, 0) < v:
                toks[h] = v
        for h, v in b.r.items():
            if toks.get(h, 0) < v:
                toks[h] = v
    for h, v in toks.items():
        q.wait_tok(h, v)


def _record(tok, reads, writes):
    h, v = tok
    for b in reads:
        if b.r.get(h, 0) < v:
            b.r[h] = v
    for b in writes:
        b.w = tok
        b.r = {}


def op(q, ins_fn, reads=(), writes=(), signal=True):
    _deps(q, reads, writes)
    ins = ins_fn()
    q.nins += 1
    if signal:
        ins.then_inc(q.sem, 1)
        q.cnt += 1
        tok = (q, q.cnt)
    else:
        tok = (q, q.cnt + 1)
    _record(tok, reads, writes)
    return ins


def dma(q, dsem, out, in_, reads=(), writes=()):
    _deps(q, reads, writes)
    ins = q.eng.dma_start(out=out, in_=in_)
    ins.then_inc(dsem.sem, 16)
    dsem.cnt += 16
    _record((dsem, dsem.cnt), reads, writes)
    return ins


class Arena:
    def __init__(self, nc, nbytes):
        self.nbytes = nbytes
        self.t = nc.alloc_sbuf_tensor("arena", [P, nbytes // 2], BF16)

    def view(self, off, shape, dt):
        esz = 4 if dt == F32 else 2
        n = 1
        for d_ in shape:
            n *= d_
        nb = n * esz
        assert off % 4 == 0 and off + nb <= self.nbytes, (off, nb, self.nbytes)
        ap = self.t[:, off // 2:(off + nb) // 2]
        if dt == F32:
            ap = ap.bitcast(F32)
        if len(shape) == 2:
            ap = ap.rearrange("p (a b) -> p a b", a=shape[0])
        elif len(shape) == 3:
            ap = ap.rearrange("p (a b c) -> p a b c", a=shape[0], b=shape[1])
        return ap


class Layout:
    def __init__(self, arena, base=0):
        self.arena = arena
        self.off = base
        self.hi = base

    def get(self, shape, dt):
        esz = 4 if dt == F32 else 2
        n = esz
        for d_ in shape:
            n *= d_
        off = (self.off + 31) // 32 * 32
        self.off = off + n
        self.hi = max(self.hi, self.off)
        return self.arena.view(off, shape, dt)


def build_program(L, S, lam_inits):
    NG = S // TG
    NT = S // P
    nc = bass.Bass("TRN2", target_bir_lowering=False)

    def din(name, shape):
        return nc.dram_tensor(name, list(shape), F32, kind="ExternalInput")

    x_d = din("x", [S, D])
    w_in_d = din("w_in", [L, D, CIN])
    w_out_d = din("w_out", [L, D, D])
    w_up_d = din("w_up", [L, D, CUP])
    w_dn_d = din("w_down", [L, DFF, D])
    pvec_d = din("pvec", [L, P, NPV])
    fvec_d = din("fvec", [L, 4, D])
    lamv_d = din("lamv", [L, 1, 256])
    sg_d = din("sublng", [L, 1, P])
    cmat_d = din("cmat", [3, P, P])
    rope_d = din("rope", [2, P, S])
    out_d = nc.dram_tensor("out", [S, D], F32, kind="ExternalOutput")
    xs_d = nc.dram_tensor("xs_scratch", [S, D], F32)

    pe = Queue(nc, nc.tensor, "pe")
    pe.inorder = True
    act = Queue(nc, nc.scalar, "act")
    dve = Queue(nc, nc.vector, "dve")
    pool = Queue(nc, nc.gpsimd, "pool")
    sp = Queue(nc, nc.sync, "sp")

    ARENA_BYTES = 212800
    arena = Arena(nc, ARENA_BYTES)
    com = Layout(arena, 0)
    ident = com.get([P], BF16)
    pm = com.get([P], BF16)
    cmask = com.get([P], BF16)
    ones = com.get([P], BF16)
    identf = com.get([P], F32)
    pv = com.get([NPV], F32)
    sg = com.get([P], F32)
    small = com.get([64], F32)
    g1 = com.get([D], F32)
    g2 = com.get([D], F32)
    xin = [com.get([D], F32) for _ in range(2)]
    xres = [com.get([D], F32)]
    hbf = [com.get([D], BF16) for _ in range(2)]
    hT = com.get([NK, TG], BF16)
    tmps = com.get([4, TG], F32)
    ropeCS = com.get([2, TG], F32)
    ropeC = ropeCS[:, 0, :]
    ropeS = ropeCS[:, 1, :]
    xres_rope = ropeCS.rearrange("p a b -> p (a b)")
    lamb = tmps[:, 3, 0:256]
    base_phase = (com.hi + 31) // 32 * 32
    mx = Layout(arena, base_phase)
    w_in = mx.get([NK, CIN], BF16)
    w_out = mx.get([NK, D], BF16)
    KT = mx.get([4, S], BF16)
    V = mx.get([NT, 4, 130], BF16)
    QT = mx.get([4, TG], BF16)
    Ubuf = mx.get([4, 30 + TG], BF16)
    UO = mx.get([NK, TG], BF16)
    ybf_off = (mx.off + 31) // 32 * 32
    ybf = mx.get([4, TG], BF16)
    xres_ybf = arena.view(ybf_off, [D], F32)
    ysq_off = (mx.off + 31) // 32 * 32
    ysq = mx.get([4, TG], BF16)
    rt2 = arena.view(ysq_off, [TG], F32)
    rt3 = arena.view(ysq_off + 2048, [TG], F32)
    rl = mx.get([2, 4], F32)
    ssr = mx.get([4], F32)
    xoff = (mx.off + 31) // 32 * 32
    Dg = mx.get([CONV_K, P], BF16)
    Pt = [[arena.view(xoff + (2 * c + s_) * 1024, [TG], BF16) for s_ in range(2)] for c in range(2)]
    obf = arena.view(xoff + 4096, [4, P], BF16)
    zbf = arena.view(xoff + 5120, [TG], BF16)
    DgB = mx.get([CONV_K, P], BF16)
    ff = Layout(arena, base_phase)
    w_up = ff.get([NK, CUP], BF16)
    w_dn = ff.get([NJ, D], BF16)
    HID = ff.get([NJ, TG], BF16)
    Rb = [ff.get([2 + TG], BF16) for _ in range(3)]
    halo = ff.get([44, 2], BF16)
    D3 = [ff.get([3, P], BF16) for _ in range(2)]
    print("SBUF layout: common %d, mixer %d, ffn %d (limit %d)" % (com.hi, mx.hi, ff.hi, ARENA_BYTES))
    assert mx.hi <= ARENA_BYTES and ff.hi <= ARENA_BYTES

    psum = nc.alloc_psum_tensor("psum", [P, 8 * 512], F32)

    def bank(i, n=1):
        return psum[:, i * 512:(i + n) * 512]

    pT = bank(0).bitcast(BF16)

    B = {}

    def b(name):
        if name not in B:
            B[name] = Buf(name)
        return B[name]

    pb = [b("psum%d" % i) for i in range(8)]
    for x_ in pb:
        x_.excl = True
    WI_BLK = ["wi_a", "wi_g", "wi_q", "wi_k", "wi_v"]
    xalias = ["Pt00", "Pt01", "Pt10", "Pt11", "obf", "zbf"]
    mixer_names = WI_BLK + ["w_out0", "w_out1", "QT", "Ubuf", "Dg", "DgB", "UO", "ybf", "ysq", "rl0", "rl1", "ssr", "Vones"] + xalias + \
        ["KT%d" % g for g in range(NG)] + ["V%d" % g for g in range(NG)]
    ffn_names = ["wu_g%d" % i for i in range(6)] + ["wu_v%d" % i for i in range(6)] + ["wd%d" % i for i in range(6)] + \
        ["HID", "R0", "R1", "R2", "halo", "D30", "D31"]
    mixer_bufs = [b(n) for n in mixer_names]
    ffn_bufs = [b(n) for n in ffn_names]
    xalias_bufs = [b(n) for n in xalias]
    hT_bufs = [b("hT%d" % i) for i in range(4)]

    dsem = {}

    def ds(name):
        if name not in dsem:
            dsem[name] = Holder(nc, "d_" + name)
        return dsem[name]

    def sc(i):
        return small[:, i:i + 1]
    S1, S2, E1, E2, NLAM = range(5)
    SSQ0, RSTD0, PSSQ0, PRSTD0 = 8, 10, 12, 14
    EPSC = 16

    dma(pool, ds("c0"), ident, cmat_d[0], writes=[b("ident")])
    dma(pool, ds("c0b"), pm, cmat_d[1], writes=[b("pm")])
    dma(pool, ds("c0c"), cmask, cmat_d[2], writes=[b("cmask")])
    dma(sp, ds("c1"), identf, cmat_d[0], writes=[b("identf")])
    op(dve, lambda: nc.vector.memset(ones, 1.0), writes=[b("ones")])
    op(dve, lambda: nc.vector.memset(sc(EPSC), EPS), writes=[b("epsc")])

    def norm_chain(T, src_d, srcbuf):
        s_ = T % 2
        xb, hb = b("xin%d" % s_), b("hbf%d" % s_)
        sq, rs = b("ssq%d" % s_), b("rstd%d" % s_)
        dma(sp, ds("xin%d" % s_), xin[s_], src_d[T * P:(T + 1) * P, :], reads=[b(srcbuf)], writes=[xb])
        op(dve, lambda: nc.vector.memset(sc(SSQ0 + s_), 0.0), writes=[sq])
        op(act, lambda: nc.scalar.activation(out=hbf[s_], in_=xin[s_], func=AF.Square, accum_out=sc(SSQ0 + s_)),
           reads=[xb], writes=[hb, sq])
        op(act, lambda: nc.scalar.activation(out=sc(RSTD0 + s_), in_=sc(SSQ0 + s_), func=AF.Ln, bias=sc(EPSC), scale=1.0 / D),
           reads=[sq, b("epsc")], writes=[rs])
        op(act, lambda: nc.scalar.activation(out=sc(RSTD0 + s_), in_=sc(RSTD0 + s_), func=AF.Exp, scale=-0.5), reads=[rs], writes=[rs])
        op(dve, lambda: nc.vector.scalar_tensor_tensor(out=hbf[s_], in0=xin[s_], scalar=sc(RSTD0 + s_), in1=g1,
                                                       op0=ALU.mult, op1=ALU.mult),
           reads=[xb, rs, b("g1")], writes=[hb])

    def norm_T(T):
        s_ = T % 2
        tt = T % 4
        hb = b("hbf%d" % s_)
        for kc in range(NK):
            op(pe, lambda kc=kc: nc.tensor.transpose(out=pT[:, kc * P:(kc + 1) * P], in_=hbf[s_][:, kc * P:(kc + 1) * P],
                                                   identity=ident),
               reads=[hb, b("ident")], writes=[pb[0]], signal=(kc == NK - 1))
        op(dve, lambda: nc.vector.tensor_copy(out=hT[:, :, tt * P:(tt + 1) * P],
                                              in_=pT[:, 0:NK * P].rearrange("p (k c) -> p k c", k=NK)),
           reads=[pb[0]], writes=[hT_bufs[tt]])

    def mm_group(out_ap, pairs, reads, wbuf, start=True):
        n = len(pairs)
        for i, (l_, r_) in enumerate(pairs):
            op(pe, lambda l_=l_, r_=r_, i=i: nc.tensor.matmul(out=out_ap, lhsT=l_, rhs=r_, start=(start and i == 0),
                                                              stop=(i == n - 1)),
               reads=reads, writes=[wbuf], signal=(i == n - 1))

    def post_norm_residual(T, mbanks, src_ap, dst_ap, src_buf_name, dst_buf_name):
        s_ = T % 2
        m_ap = bank(mbanks[0], 2)
        mb = [pb[mbanks[0]], pb[mbanks[1]]]
        if s_ == 0:
            xr, xbs = xres[0], [b("xres0")]
        elif phase[0] == "mixer":
            xr, xbs = xres_ybf, [b("ybf")]
        else:
            xr, xbs = xres_rope, [b("ropeC"), b("ropeS")]
        sq, rs = b("pssq%d" % s_), b("prstd%d" % s_)
        dma(sp, ds("xres%d" % s_), xr, src_ap, reads=[b(src_buf_name)], writes=xbs)
        op(dve, lambda: nc.vector.memset(sc(PSSQ0 + s_), 0.0), writes=[sq])
        tmp32 = tmps[:, 2 * s_:2 * s_ + 2, :].rearrange("p a b -> p (a b)")
        tb = [b("tmp%d" % (2 * s_)), b("tmp%d" % (2 * s_ + 1))]
        op(act, lambda: nc.scalar.activation(out=tmp32, in_=m_ap, func=AF.Square, accum_out=sc(PSSQ0 + s_)),
           reads=mb, writes=tb + [sq])
        op(act, lambda: nc.scalar.activation(out=sc(PRSTD0 + s_), in_=sc(PSSQ0 + s_), func=AF.Ln, bias=sc(EPSC), scale=1.0 / D),
           reads=[sq, b("epsc")], writes=[rs])
        op(act, lambda: nc.scalar.activation(out=sc(PRSTD0 + s_), in_=sc(PRSTD0 + s_), func=AF.Exp, scale=-0.5), reads=[rs], writes=[rs])
        op(dve, lambda: nc.vector.scalar_tensor_tensor(out=tmp32, in0=m_ap, scalar=sc(PRSTD0 + s_), in1=g2,
                                                       op0=ALU.mult, op1=ALU.mult),
           reads=mb + [rs, b("g2")], writes=tb)
        op(dve, lambda: nc.vector.tensor_tensor(out=xr, in0=xr, in1=tmp32, op=ALU.add),
           reads=tb + xbs, writes=xbs)
        dma(sp, ds("xst%d" % s_), dst_ap, xr, reads=xbs, writes=[b(dst_buf_name)])

    rot = [1, 2, 3]
    rr = [0]
    phase = ["mixer"]
    srot = [0]

    def nbs():
        i = srot[0] % 4
        srot[0] += 1
        return i

    def nb():
        i = rot[rr[0] % len(rot)]
        rr[0] += 1
        return i

    for l in range(L):
        lam_init = lam_inits[l]
        src_d = x_d if l == 0 else xs_d
        dma(sp, ds("pv"), pv, pvec_d[l], writes=[b("pv")])
        dma(sp, ds("lamb"), lamb, lamv_d[l].broadcast_to([P, 256]), writes=[b("tmp3")])
        dma(sp, ds("sg"), sg, sg_d[l].broadcast_to([P, P]), writes=[b("sg")])
        dma(sp, ds("g1"), g1, fvec_d[l, 0:1, :].broadcast_to([P, D]), writes=[b("g1")])
        dma(sp, ds("g2"), g2, fvec_d[l, 1:2, :].broadcast_to([P, D]), writes=[b("g2")])
        jt = tmps[:, 2, 0:64]
        op(dve, lambda: nc.vector.scalar_tensor_tensor(out=jt, in0=lamb[:, 0:64], scalar=1.0, in1=lamb[:, 64:128],
                                                       op0=ALU.mult, op1=ALU.mult, accum_out=sc(S1)),
           reads=[b("tmp3")], writes=[b("tmp2"), b("s1")])
        op(dve, lambda: nc.vector.scalar_tensor_tensor(out=jt, in0=lamb[:, 128:192], scalar=1.0, in1=lamb[:, 192:256],
                                                       op0=ALU.mult, op1=ALU.mult, accum_out=sc(S2)),
           reads=[b("tmp3")], writes=[b("tmp2"), b("s2")])
        op(act, lambda: nc.scalar.activation(out=sc(E1), in_=sc(S1), func=AF.Exp), reads=[b("s1")], writes=[b("e1")])
        op(act, lambda: nc.scalar.activation(out=sc(E2), in_=sc(S2), func=AF.Exp), reads=[b("s2")], writes=[b("e2")])
        op(dve, lambda: nc.vector.tensor_tensor(out=sc(NLAM), in0=sc(E2), in1=sc(E1), op=ALU.subtract),
           reads=[b("e1"), b("e2")], writes=[b("nlam")])
        op(dve, lambda: nc.vector.tensor_scalar(out=sc(NLAM), in0=sc(NLAM), scalar1=-float(lam_init), scalar2=None,
                                                op0=ALU.add),
           reads=[b("nlam")], writes=[b("nlam")])
        op(act, lambda: nc.scalar.mul(out=sg, in_=sg, mul=float(1.0 - lam_init)), reads=[b("sg")], writes=[b("sg")])

        first = True
        for blk in (1, 0, 2, 3, 4):
            c0 = blk * 512
            dma(pool, ds("w_in%d" % blk), w_in[:, :, c0:c0 + 512],
                w_in_d[l, :, c0:c0 + 512].rearrange("(k p) c -> p k c", p=P),
                writes=[b(WI_BLK[blk])] + (ffn_bufs if first else []))
            first = False
        for hf in range(2):
            dma(pool, ds("w_out%d" % hf), w_out[:, hf * 4:(hf + 1) * 4, :],
                w_out_d[l, hf * 512:(hf + 1) * 512, :].rearrange("(k p) c -> p k c", p=P), writes=[b("w_out%d" % hf)])
        op(pool, lambda: nc.gpsimd.memset(Ubuf[:, :, 0:30], 0.0), writes=[b("Ubuf")])
        op(pool, lambda: nc.gpsimd.memset(V[:, :, :, 128:130], 1.0), writes=[b("Vones")])

        phase[0] = "mixer"
        for T in range(4):
            norm_chain(T, src_d, "xs%d" % T) if T < 2 else None
        norm_T(0)
        norm_T(1)
        norm_chain(2, src_d, "xs2")
        norm_chain(3, src_d, "xs3")
        norm_T(2)
        norm_T(3)
        for G in range(NG):
            t0 = G * TG
            nxt = G + 1 < NG
            def rope_load(Gt):
                dma(sp, ds("ropeC"), ropeC, rope_d[0, :, Gt * TG:(Gt + 1) * TG], writes=[b("ropeC")])
                dma(sp, ds("ropeS"), ropeS, rope_d[1, :, Gt * TG:(Gt + 1) * TG], writes=[b("ropeS")])

            if G == 0:
                rope_load(0)
            if nxt and G == 0:
                for T in (4, 5):
                    norm_chain(T, src_d, "xs%d" % T)

            def proj(col0, bi):
                mm_group(bank(bi), [(w_in[:, kc, col0:col0 + P], hT[:, kc, :]) for kc in range(NK)],
                         [b(WI_BLK[col0 // 512])] + hT_bufs, pb[bi])

            def dg_gen(ct):
                wv = pv[:, PV_CW + ct * CONV_K:PV_CW + (ct + 1) * CONV_K]
                dgt, dgb = (DgB, b("DgB")) if ct % 2 == 0 else (Dg, b("Dg"))
                op(pool, lambda: nc.gpsimd.tensor_tensor(out=dgt, in0=identf.unsqueeze(1).broadcast_to([P, CONV_K, P]),
                                                         in1=wv.unsqueeze(2).broadcast_to([P, CONV_K, P]), op=ALU.mult),
                   reads=[b("identf"), b("pv")], writes=[dgb] + (xalias_bufs if ct % 2 == 1 else []))

            def glu_step(ct):
                bg = nb()
                proj(512 + ct * P, bg)
                sgt = tmps[:, 3, :]
                sgb = b("tmp3")
                op(act, lambda: nc.scalar.activation(out=sgt, in_=bank(bg), func=AF.Sigmoid), reads=[pb[bg]], writes=[sgb])
                ba = nb()
                proj(ct * P, ba)
                op(dve, lambda: nc.vector.tensor_tensor(out=Ubuf[:, ct, 30:30 + TG], in0=bank(ba), in1=sgt, op=ALU.mult),
                   reads=[pb[ba], sgb], writes=[b("Ubuf")])

            def halo_copy():
                op(pool, lambda: nc.gpsimd.tensor_copy(out=Ubuf[:, :, 0:30], in_=Ubuf[:, :, TG:TG + 30]),
                   reads=[b("Ubuf")], writes=[b("Ubuf")])

            dg_gen(0)
            if G == 0:
                for ct in range(4):
                    glu_step(ct)
            def m3b_unit(Gt, which, h):
                col0 = 1024 + which * 512 + h * P
                bz = nb()
                proj(col0, bz)
                op(act, lambda: nc.scalar.copy(out=zbf, in_=bank(bz)), reads=[pb[bz]], writes=[b("zbf")])
                bp = nb()
                mm_group(bank(bp), [(pm, zbf)], [b("pm"), b("zbf")], pb[bp])
                op(dve, lambda: nc.vector.tensor_tensor(out=rt2, in0=bank(bz), in1=ropeC, op=ALU.mult),
                   reads=[pb[bz], b("ropeC")], writes=[b("ysqA")])
                op(dve, lambda: nc.vector.tensor_tensor(out=rt3, in0=bank(bp), in1=ropeS, op=ALU.mult),
                   reads=[pb[bp], b("ropeS")], writes=[b("ysqB")])
                if which == 0:
                    dst, dbuf = QT[:, h, :], b("QT")
                else:
                    dst, dbuf = KT[:, h, Gt * TG:(Gt + 1) * TG], b("KT%d" % Gt)
                op(dve, lambda: nc.vector.tensor_tensor(out=dst, in0=rt2, in1=rt3, op=ALU.add),
                   reads=[b("ysqA"), b("ysqB")], writes=[dbuf])

            if G == 0:
                for which in range(2):
                    for h in range(4):
                        m3b_unit(0, which, h)
            if nxt:
                rope_load(G + 1)
            for tt in range(4):
                T = G * 4 + tt
                bv = nb()
                mm_group(bank(bv), [(hT[:, kc, tt * P:(tt + 1) * P], w_in[:, kc, 2048:2560]) for kc in range(NK)],
                         [b("wi_v"), hT_bufs[tt]], pb[bv])
                op(act, lambda bv=bv, T=T: nc.scalar.copy(out=V[:, T, :, 0:128],
                                                          in_=bank(bv).rearrange("p (h c) -> p h c", h=4)),
                   reads=[pb[bv]], writes=[b("V%d" % G)])
            if nxt:
                norm_T(4 * G + 4)
                norm_T(4 * G + 5)
                for T in (4 * G + 6, 4 * G + 7):
                    norm_chain(T, src_d, "xs%d" % T)
            dg_gen(1)
            for ct in range(4):
                dgt, dgb = (DgB, b("DgB")) if ct % 2 == 0 else (Dg, b("Dg"))
                bc = nb()
                mm_group(bank(bc), [(dgt[:, k, :], Ubuf[:, ct, k:k + TG]) for k in range(CONV_K)],
                         [dgb, b("Ubuf")], pb[bc])
                if ct + 2 < 4:
                    dg_gen(ct + 2)
                cb = pv[:, PV_CB + ct:PV_CB + ct + 1]
                op(act, lambda bc=bc, ct=ct, cb=cb: nc.scalar.activation(out=ybf[:, ct, :], in_=bank(bc), func=AF.Identity, bias=cb),
                   reads=[pb[bc], b("pv")], writes=[b("ybf")])
                op(act, lambda bc=bc, ct=ct, cb=cb: nc.scalar.activation(out=ysq[:, ct, :], in_=bank(bc), func=AF.Square, bias=cb),
                   reads=[pb[bc], b("pv")], writes=[b("ysq"), b("ysqA"), b("ysqB")])
            b1 = nb()
            mm_group(bank(b1), [(ones, ybf[:, ct, :]) for ct in range(4)], [b("ones"), b("ybf")], pb[b1])
            b2 = nb()
            mm_group(bank(b2), [(ones, ysq[:, ct, :]) for ct in range(4)], [b("ones"), b("ysq")], pb[b2])
            mean, msq, rstd_t = tmps[:, 0, :], tmps[:, 1, :], tmps[:, 2, :]
            op(act, lambda: nc.scalar.mul(out=mean, in_=bank(b1), mul=1.0 / 512), reads=[pb[b1]], writes=[b("tmp0")])
            op(act, lambda: nc.scalar.activation(out=msq, in_=bank(b1), func=AF.Square, scale=1.0 / 512),
               reads=[pb[b1]], writes=[b("tmp1")])
            op(dve, lambda: nc.vector.scalar_tensor_tensor(out=rstd_t, in0=bank(b2), scalar=1.0 / 512, in1=msq,
                                                           op0=ALU.mult, op1=ALU.subtract),
               reads=[pb[b2], b("tmp1")], writes=[b("tmp2")])
            t3 = tmps[:, 3, :]

            def ln_a2():
                op(act, lambda: nc.scalar.activation(out=rstd_t, in_=rstd_t, func=AF.Ln, bias=sc(EPSC), scale=1.0),
                   reads=[b("tmp2"), b("epsc")], writes=[b("tmp2")])
                op(act, lambda: nc.scalar.activation(out=rstd_t, in_=rstd_t, func=AF.Exp, scale=-0.5), reads=[b("tmp2")], writes=[b("tmp2")])

            def ln_d2(ct):
                op(dve, lambda: nc.vector.tensor_tensor(out=t3, in0=ybf[:, ct, :], in1=mean, op=ALU.subtract),
                   reads=[b("ybf"), b("tmp0")], writes=[b("tmp3")])
                op(dve, lambda: nc.vector.tensor_tensor(out=t3, in0=t3, in1=rstd_t, op=ALU.mult),
                   reads=[b("tmp3"), b("tmp2")], writes=[b("tmp3")])

            def ln_a3(ct):
                op(act, lambda: nc.scalar.activation(out=UO[:, ct, :], in_=t3, func=AF.Silu,
                                                     scale=pv[:, PV_LG + ct:PV_LG + ct + 1],
                                                     bias=pv[:, PV_LB + ct:PV_LB + ct + 1]),
                   reads=[b("tmp3"), b("pv")], writes=[b("UO")])

            accv = [psum[:, (4 + 2 * c) * 512:(6 + 2 * c) * 512].rearrange("p (q w) -> p q w", q=4) for c in range(2)]
            accb = [[pb[4], pb[5]], [pb[6], pb[7]]]
            nkt = 4 * G + 4
            sbank = {}

            def emit_qk(h, kt, c):
                r = kt - 4 * G
                c0 = max(r, 0) * P
                sbk = nbs()
                sbank[(kt, c)] = sbk
                mm_group(bank(sbk)[:, c0:TG],
                         [(KT[c * 64:(c + 1) * 64, h, kt * P:(kt + 1) * P], QT[c * 64:(c + 1) * 64, h, c0:TG])],
                         [b("KT%d" % (kt // 4)), b("QT")], pb[sbk])

            def emit_exp(h, kt, c):
                r = kt - 4 * G
                c0 = max(r, 0) * P
                sbk = sbank[(kt, c)]
                sl = kt % 2
                ptb = b("Pt%d%d" % (c, sl))
                pt = Pt[c][sl]
                op(act, lambda: nc.scalar.activation(out=pt[:, c0:TG], in_=bank(sbk)[:, c0:TG], func=AF.Exp, scale=0.125),
                   reads=[pb[sbk]], writes=[ptb])
                if r >= 0:
                    op(dve, lambda: nc.vector.tensor_tensor(out=pt[:, c0:c0 + P], in0=pt[:, c0:c0 + P], in1=cmask, op=ALU.mult),
                       reads=[ptb, b("cmask")], writes=[ptb])

            def emit_av(h, kt, c):
                r = kt - 4 * G
                q0 = max(r, 0)
                sl = kt % 2
                ptb = b("Pt%d%d" % (c, sl))
                pt = Pt[c][sl]
                for ql in range(q0, 4):
                    bk = 4 + 2 * c + ql // 2
                    reg = psum[:, bk * 512 + (ql % 2) * 256: bk * 512 + (ql % 2) * 256 + 129]
                    first_ = (kt == 0 and ql % 2 == 0)
                    last_ = (kt == 4 * G + ql)
                    op(pe, lambda reg=reg, ql=ql, first_=first_, last_=last_:
                       nc.tensor.matmul(out=reg, lhsT=pt[:, ql * P:(ql + 1) * P], rhs=V[:, kt, h, 0:129],
                                        start=first_, stop=last_, skip_group_check=True),
                       reads=[ptb, b("V%d" % (kt // 4)), b("Vones")], writes=[pb[bk]], signal=(ql == 3))

            def finalize_a(h):
                t1 = tmps[:, 0, :].rearrange("p (q w) -> p q w", q=4)
                t2 = tmps[:, 1, :].rearrange("p (q w) -> p q w", q=4)
                o3 = tmps[:, 2, :].rearrange("p (q w) -> p q w", q=4)
                op(dve, lambda: nc.vector.reciprocal(out=rl[:, 0, :], in_=accv[0][:, :, 128]), reads=accb[0], writes=[b("rl0")])
                op(dve, lambda: nc.vector.tensor_copy(out=t1, in_=accv[0][:, :, 0:128]), reads=accb[0], writes=[b("tmp0")])
                op(dve, lambda: nc.vector.reciprocal(out=rl[:, 1, :], in_=accv[1][:, :, 128]), reads=accb[1], writes=[b("rl1")])
                op(dve, lambda: nc.vector.tensor_copy(out=t2, in_=accv[1][:, :, 0:128]), reads=accb[1], writes=[b("tmp1")])
                op(dve, lambda: nc.vector.tensor_scalar(out=rl[:, 1, :], in0=rl[:, 1, :], scalar1=sc(NLAM), scalar2=None,
                                                        op0=ALU.mult),
                   reads=[b("rl1"), b("nlam")], writes=[b("rl1")])
                op(dve, lambda: nc.vector.tensor_tensor(out=t1, in0=t1,
                                                        in1=rl[:, 0, :].unsqueeze(2).broadcast_to([P, 4, P]), op=ALU.mult),
                   reads=[b("tmp0"), b("rl0")], writes=[b("tmp0")])
                op(dve, lambda: nc.vector.tensor_tensor(out=t2, in0=t2,
                                                        in1=rl[:, 1, :].unsqueeze(2).broadcast_to([P, 4, P]), op=ALU.mult),
                   reads=[b("tmp1"), b("rl1")], writes=[b("tmp1")])
                op(dve, lambda: nc.vector.tensor_tensor(out=o3, in0=t1, in1=t2, op=ALU.add),
                   reads=[b("tmp0"), b("tmp1")], writes=[b("tmp2")])
                op(dve, lambda: nc.vector.tensor_tensor(out=t1, in0=o3, in1=o3, op=ALU.mult),
                   reads=[b("tmp2")], writes=[b("tmp0")])
                op(dve, lambda: nc.vector.tensor_reduce(out=ssr, in_=t1, axis=AX.X, op=ALU.add),
                   reads=[b("tmp0")], writes=[b("ssr")])
                op(act, lambda: nc.scalar.activation(out=ssr, in_=ssr, func=AF.Ln, bias=sc(EPSC), scale=1.0 / 128),
                   reads=[b("ssr"), b("epsc")], writes=[b("ssr")])
                op(act, lambda: nc.scalar.activation(out=ssr, in_=ssr, func=AF.Exp, scale=-0.5), reads=[b("ssr")], writes=[b("ssr")])
                op(dve, lambda: nc.vector.tensor_tensor(out=o3, in0=o3, in1=ssr.unsqueeze(2).broadcast_to([P, 4, P]), op=ALU.mult),
                   reads=[b("tmp2"), b("ssr")], writes=[b("tmp2")])
                op(dve, lambda: nc.vector.tensor_tensor(out=obf, in0=o3, in1=sg.unsqueeze(1).broadcast_to([P, 4, P]), op=ALU.mult),
                   reads=[b("tmp2"), b("sg")], writes=[b("obf")])

            def finalize_b(h):
                for ql in range(4):
                    op(pe, lambda ql=ql: nc.tensor.transpose(out=pT[:, ql * P:(ql + 1) * P], in_=obf[:, ql, :], identity=ident),
                       reads=[b("obf"), b("ident")], writes=[pb[0]], signal=(ql == 3))
                op(act, lambda: nc.scalar.copy(out=UO[:, 4 + h, :], in_=pT[:, 0:TG]), reads=[pb[0]], writes=[b("UO")])

            deferred = None
            if nxt:
                norm_T(4 * G + 6)
                norm_T(4 * G + 7)
                halo_copy()
                glu_step(0)
                glu_step(1)
            for h in range(4):
                emit_qk(h, 0, 0)
                emit_qk(h, 0, 1)
                for kt in range(nkt):
                    if kt + 1 < nkt:
                        emit_qk(h, kt + 1, 0)
                        emit_qk(h, kt + 1, 1)
                    emit_exp(h, kt, 0)
                    emit_exp(h, kt, 1)
                    if h == 0 and kt < 4:
                        if kt == 0:
                            ln_a2()
                        else:
                            ln_a3(kt - 1)
                        ln_d2(kt)
                    emit_av(h, kt, 0)
                    emit_av(h, kt, 1)
                    if kt == 2 and deferred is not None:
                        finalize_b(deferred)
                        deferred = None
                if h == 0:
                    ln_a3(3)
                if h == 3 and G + 2 < NG:
                    for T in (4 * G + 8, 4 * G + 9):
                        norm_chain(T, src_d, "xs%d" % T)
                finalize_a(h)
                deferred = h
                if h == 3 and nxt:
                    glu_step(2)
                    glu_step(3)
            finalize_b(deferred)
            def wout_tile(tt):
                T = G * 4 + tt
                mbanks = (4, 5) if tt % 2 == 0 else (6, 7)
                for half in range(2):
                    mm_group(bank(mbanks[half]),
                             [(UO[:, cc, tt * P:(tt + 1) * P], w_out[:, cc, half * 512:(half + 1) * 512]) for cc in range(NK)],
                             [b("UO"), b("w_out0"), b("w_out1")], pb[mbanks[half]])
                post_norm_residual(T, mbanks, src_d[T * P:(T + 1) * P, :], xs_d[T * P:(T + 1) * P, :], "xs%d" % T, "xs%d" % T)

            wout_tile(0)
            wout_tile(1)
            if nxt:
                m3b_unit(G + 1, 0, 0)
                m3b_unit(G + 1, 0, 1)
            wout_tile(2)
            if nxt:
                m3b_unit(G + 1, 0, 2)
                m3b_unit(G + 1, 0, 3)
            wout_tile(3)
            if nxt:
                for h in range(4):
                    m3b_unit(G + 1, 1, h)

        phase[0] = "ffn"
        dma(sp, ds("g1"), g1, fvec_d[l, 2:3, :].broadcast_to([P, D]), writes=[b("g1")])
        dma(sp, ds("g2"), g2, fvec_d[l, 3:4, :].broadcast_to([P, D]), writes=[b("g2")])
        first = True
        for blk in range(6):
            w_ = min(512, DFF - blk * 512)
            for which in range(2):
                c0 = which * DFF + blk * 512
                dma(pool, ds("w_up%d%d" % (which, blk)), w_up[:, :, c0:c0 + w_],
                    w_up_d[l, :, c0:c0 + w_].rearrange("(k p) c -> p k c", p=P),
                    writes=[b(("wu_g%d" if which == 0 else "wu_v%d") % blk)] + (mixer_bufs if first else []))
                first = False
        for gq in range(6):
            j0 = gq * 4
            j1 = min(j0 + 4, NJ)
            dma(pool, ds("w_dn%d" % gq), w_dn[:, j0:j1, :],
                w_dn_d[l, j0 * P:j1 * P, :].rearrange("(j p) c -> p j c", p=P), writes=[b("wd%d" % gq)])
        op(dve, lambda: nc.vector.memset(halo, 0.0), writes=[b("halo")])
        dst_d = out_d if l == L - 1 else xs_d
        norm_chain(0, xs_d, "xs0")
        norm_chain(1, xs_d, "xs1")
        norm_T(0)
        norm_T(1)
        norm_chain(2, xs_d, "xs2")
        norm_chain(3, xs_d, "xs3")
        norm_T(2)
        norm_T(3)
        for G in range(NG):
            nxt = G + 1 < NG
            if nxt and G == 0:
                for T in (4, 5):
                    norm_chain(T, xs_d, "xs%d" % T)
            ubank = {}

            def up_tile(n):
                jp, which = n // 2, n % 2
                j = which * NJ + jp
                col0 = which * DFF + jp * P
                s_ = n % 3
                bu = 1 + n % 3
                ubank[n] = bu
                wb_ = b(("wu_g%d" if which == 0 else "wu_v%d") % (jp // 4))
                mm_group(bank(bu), [(w_up[:, kc, col0:col0 + P], hT[:, kc, :]) for kc in range(NK)],
                         [wb_] + hT_bufs, pb[bu])
                rbuf = b("R%d" % s_)
                op(dve, lambda: nc.vector.tensor_copy(out=Rb[s_][:, 0:2], in_=halo[:, j, :]),
                   reads=[b("halo")], writes=[rbuf])
                op(act, lambda: nc.scalar.copy(out=Rb[s_][:, 2:2 + TG], in_=bank(bu)), reads=[pb[bu]], writes=[rbuf])
                op(dve, lambda: nc.vector.tensor_copy(out=halo[:, j, :], in_=Rb[s_][:, TG:TG + 2]),
                   reads=[rbuf], writes=[b("halo")])
                base_ = 4 if jp % 2 == 0 else 6
                bcv = base_ + which
                if which == 0:
                    op(act, lambda: nc.scalar.activation(out=bank(bcv), in_=bank(bu), func=AF.Identity,
                                                         scale=pv[:, PV_FW + j * 3 + 2:PV_FW + j * 3 + 3],
                                                         bias=pv[:, PV_FB + j:PV_FB + j + 1]),
                       reads=[pb[bu], b("pv")], writes=[pb[bcv]])
                else:
                    fw = pv[:, PV_FW + j * 3:PV_FW + j * 3 + 3]
                    op(pool, lambda: nc.gpsimd.tensor_tensor(out=D3[jp % 2], in0=identf.unsqueeze(1).broadcast_to([P, 3, P]),
                                                             in1=fw.unsqueeze(2).broadcast_to([P, 3, P]), op=ALU.mult),
                       reads=[b("identf"), b("pv")], writes=[b("D3%d" % (jp % 2))])

            def conv_tile(n):
                jp, which = n // 2, n % 2
                j = which * NJ + jp
                s_ = n % 3
                base_ = 4 if jp % 2 == 0 else 6
                bcv = base_ + which
                rbuf = b("R%d" % s_)
                if which == 0:
                    for k in (1, 0):
                        op(dve, lambda k=k: nc.vector.scalar_tensor_tensor(out=bank(bcv), in0=Rb[s_][:, k:k + TG],
                                                                           scalar=pv[:, PV_FW + j * 3 + k:PV_FW + j * 3 + k + 1],
                                                                           in1=bank(bcv), op0=ALU.mult, op1=ALU.add),
                           reads=[rbuf, pb[bcv], b("pv")], writes=[pb[bcv]])
                    gl = tmps[:, 2 + jp % 2, :]
                    glb = b("tmp%d" % (2 + jp % 2))
                    op(act, lambda: nc.scalar.activation(out=gl, in_=bank(base_), func=AF.Gelu_apprx_tanh),
                       reads=[pb[base_]], writes=[glb])
                else:
                    mm_group(bank(bcv), [(D3[jp % 2][:, k, :], Rb[s_][:, k:k + TG]) for k in range(3)],
                             [b("D3%d" % (jp % 2)), rbuf], pb[bcv])
                    gl = tmps[:, 2 + jp % 2, :]
                    glb = b("tmp%d" % (2 + jp % 2))
                    op(dve, lambda: nc.vector.scalar_tensor_tensor(out=HID[:, jp, :], in0=bank(bcv),
                                                                   scalar=pv[:, PV_FB + j:PV_FB + j + 1],
                                                                   in1=gl, op0=ALU.add, op1=ALU.mult),
                       reads=[pb[bcv], glb, b("pv")], writes=[b("HID")])

            def down_tile(tt):
                T = G * 4 + tt
                mbanks = (6, 7) if tt % 2 == 0 else (4, 5)
                for half in range(2):
                    for jp in range(NJ):
                        op(pe, lambda jp=jp, half=half: nc.tensor.matmul(out=bank(mbanks[half]), lhsT=HID[:, jp, tt * P:(tt + 1) * P],
                                                                         rhs=w_dn[:, jp, half * 512:(half + 1) * 512],
                                                                         start=(jp == 0), stop=(jp == NJ - 1)),
                           reads=[b("HID"), b("wd%d" % (jp // 4))], writes=[pb[mbanks[half]]], signal=(jp == NJ - 1))
                post_norm_residual(T, mbanks, xs_d[T * P:(T + 1) * P, :], dst_d[T * P:(T + 1) * P, :],
                                   "xs%d" % T, ("out%d" % T) if l == L - 1 else ("xs%d" % T))

            NTL = 2 * NJ
            up_tile(0)
            up_tile(1)
            for n in range(NTL):
                if n + 2 < NTL:
                    up_tile(n + 2)
                conv_tile(n)
            if nxt:
                norm_T(4 * G + 4)
                norm_T(4 * G + 5)
                for T in (4 * G + 6, 4 * G + 7):
                    norm_chain(T, xs_d, "xs%d" % T)
            down_tile(0)
            down_tile(1)
            if nxt:
                norm_T(4 * G + 6)
                norm_T(4 * G + 7)
            if G + 2 < NG:
                for T in (4 * G + 8, 4 * G + 9):
                    norm_chain(T, xs_d, "xs%d" % T)
            down_tile(2)
            down_tile(3)

    for nm in ("xst0", "xst1"):
        sp.eng.wait_ge(dsem[nm].sem, dsem[nm].cnt)
    stats = {q.name: (q.nins, q.nwait) for q in (pe, act, dve, pool, sp)}
    print("instr/wait counts:", stats)
    return nc


def _consts(S):
    ident = np.eye(P, dtype=np.float32)
    pmat = np.zeros((P, P), np.float32)
    for base in (0, 64):
        for i in range(8):
            pmat[base + i + 8, base + i] = -1.0
            pmat[base + i, base + i + 8] = 1.0
    mask = (np.arange(P)[:, None] <= np.arange(P)[None, :]).astype(np.float32)
    pos = np.arange(S, dtype=np.float32)
    inv_freq = (np.float32(500000.0) ** (-np.arange(0, 16, 2, dtype=np.float32) / np.float32(16))).astype(np.float32)
    ang = (pos[:, None] * inv_freq[None, :]).astype(np.float32)
    cos = np.cos(ang).astype(np.float32)
    sin = np.sin(ang).astype(np.float32)
    C = np.ones((P, S), np.float32)
    Sn = np.zeros((P, S), np.float32)
    for base in (0, 64):
        for i in range(16):
            C[base + i] = cos[:, i % 8]
            Sn[base + i] = sin[:, i % 8]
    return np.stack([ident, pmat, mask]), np.stack([C, Sn])


def _pack_small(inp, L):
    pvec = np.zeros((L, P, NPV), np.float32)
    cw = np.asarray(inp["conv_w"], np.float32)
    pvec[:, :, PV_CW:PV_CW + 124] = cw.reshape(L, 31, 4, P).transpose(0, 3, 2, 1).reshape(L, P, 124)
    for name, off in (("conv_b", PV_CB), ("conv_ln_g", PV_LG), ("conv_ln_b", PV_LB)):
        pvec[:, :, off:off + 4] = np.asarray(inp[name], np.float32).reshape(L, 4, P).transpose(0, 2, 1)
    fw = np.asarray(inp["ffn_conv_w"], np.float32)
    pvec[:, :, PV_FW:PV_FW + 132] = fw.reshape(L, 3, 44, P).transpose(0, 3, 2, 1).reshape(L, P, 132)
    pvec[:, :, PV_FB:PV_FB + 44] = np.asarray(inp["ffn_conv_b"], np.float32).reshape(L, 44, P).transpose(0, 2, 1)
    fvec = np.stack([np.asarray(inp[k], np.float32) for k in
                     ("pre_mix_norm", "post_mix_norm", "pre_ffn_norm", "post_ffn_norm")], axis=1)
    lamv = np.concatenate([np.asarray(inp[k], np.float32) for k in
                           ("lambda_q1", "lambda_k1", "lambda_q2", "lambda_k2")], axis=1).reshape(L, 1, 256)
    sgv = np.asarray(inp["subln_g"], np.float32).reshape(L, 1, P)
    return pvec, np.ascontiguousarray(fvec), np.ascontiguousarray(lamv), sgv


_PROG_CACHE = {}


def kernel(**inputs):
    x = np.asarray(inputs["x"], np.float32)
    Bsz, S, _ = x.shape
    L = np.asarray(inputs["w_in"]).shape[0]
    lam_inits = [0.8 - 0.6 * math.exp(-0.3 * l) for l in range(L)]
    key = (L, S)
    if key not in _PROG_CACHE:
        _PROG_CACHE[key] = build_program(L, S, lam_inits)
    nc = _PROG_CACHE[key]
    cmat, rope = _consts(S)
    pvec, fvec, lamv, sgv = _pack_small(inputs, L)
    shared = {
        "w_in": np.ascontiguousarray(inputs["w_in"], np.float32),
        "w_out": np.ascontiguousarray(inputs["w_out"], np.float32),
        "w_up": np.ascontiguousarray(inputs["w_up"], np.float32),
        "w_down": np.ascontiguousarray(inputs["w_down"], np.float32),
        "pvec": pvec, "fvec": fvec, "lamv": lamv, "sublng": sgv, "cmat": cmat, "rope": rope,
    }
    in_maps = []
    for c in range(Bsz):
        m = dict(shared)
        m["x"] = np.ascontiguousarray(x[c])
        in_maps.append(m)
    res = run_bass_kernel_spmd(nc, in_maps, core_ids=list(range(Bsz)))
    return np.stack([np.asarray(r["out"], np.float32) for r in res.results], axis=0)
```

```python
import math
import numpy as np
import concourse.bass as bass
import concourse.mybir as mybir
from concourse.bass_utils import run_bass_kernel_spmd

F32 = mybir.dt.float32
BF16 = mybir.dt.bfloat16
AF = mybir.ActivationFunctionType
ALU = mybir.AluOpType
AX = mybir.AxisListType

P = 128
D = 1024
NK = 8
CIN = 2560
DFF = 2816
NJ = 22
CUP = 5632
TG = 512
EPS = 1e-6
CONV_K = 31
PV_CW = 0
PV_CB = PV_CW + 4 * 31
PV_LG = PV_CB + 4
PV_LB = PV_LG + 4
PV_FW = PV_LB + 4
PV_FB = PV_FW + 44 * 3
NPV = PV_FB + 44


class Buf:
    __slots__ = ("name", "w", "r", "excl")

    def __init__(self, name, excl=False):
        self.name = name
        self.w = None
        self.r = {}
        self.excl = excl


class Holder:
    def __init__(self, nc, name):
        self.name = name
        self.sem = nc.alloc_semaphore(name)
        self.cnt = 0


class Queue(Holder):
    def __init__(self, nc, eng, name):
        super().__init__(nc, "q_" + name)
        self.eng = eng
        self.seen = {}
        self.nwait = 0
        self.nins = 0
        self.inorder = False

    def wait_tok(self, holder, val):
        if holder is self and self.inorder:
            return
        if self.seen.get(holder, 0) >= val:
            return
        assert val <= holder.cnt, (self.name, holder.name, val, holder.cnt)
        self.eng.wait_ge(holder.sem, val)
        self.seen[holder] = val
        self.nwait += 1


def _deps(q, reads, writes):
    toks = {}
    for b in reads:
        if b.w is not None:
            h, v = b.w
            if toks.get(h, 0) < v:
                toks[h] = v
        if b.excl:
            for h, v in b.r.items():
                if h is not q and toks.get(h, 0) < v:
                    toks[h] = v
    for b in writes:
        if b.w is not None:
            h, v = b.w
            if toks.get(h, 0) < v:
                toks[h] = v
        for h, v in b.r.items():
            if toks.get(h, 0) < v:
                toks[h] = v
    for h, v in toks.items():
        q.wait_tok(h, v)


def _record(tok, reads, writes):
    h, v = tok
    for b in reads:
        if b.r.get(h, 0) < v:
            b.r[h] = v
    for b in writes:
        b.w = tok
        b.r = {}


def op(q, ins_fn, reads=(), writes=(), signal=True):
    _deps(q, reads, writes)
    ins = ins_fn()
    q.nins += 1
    if signal:
        ins.then_inc(q.sem, 1)
        q.cnt += 1
        tok = (q, q.cnt)
    else:
        tok = (q, q.cnt + 1)
    _record(tok, reads, writes)
    return ins


def dma(q, dsem, out, in_, reads=(), writes=()):
    _deps(q, reads, writes)
    ins = q.eng.dma_start(out=out, in_=in_)
    ins.then_inc(dsem.sem, 16)
    dsem.cnt += 16
    _record((dsem, dsem.cnt), reads, writes)
    return ins


class Arena:
    def __init__(self, nc, nbytes):
        self.nbytes = nbytes
        self.t = nc.alloc_sbuf_tensor("arena", [P, nbytes // 2], BF16)

    def view(self, off, shape, dt):
        esz = 4 if dt == F32 else 2
        n = 1
        for d_ in shape:
            n *= d_
        nb = n * esz
        assert off % 4 == 0 and off + nb <= self.nbytes, (off, nb, self.nbytes)
        ap = self.t[:, off // 2:(off + nb) // 2]
        if dt == F32:
            ap = ap.bitcast(F32)
        if len(shape) == 2:
            ap = ap.rearrange("p (a b) -> p a b", a=shape[0])
        elif len(shape) == 3:
            ap = ap.rearrange("p (a b c) -> p a b c", a=shape[0], b=shape[1])
        return ap


class Layout:
    def __init__(self, arena, base=0):
        self.arena = arena
        self.off = base
        self.hi = base

    def get(self, shape, dt):
        esz = 4 if dt == F32 else 2
        n = esz
        for d_ in shape:
            n *= d_
        off = (self.off + 31) // 32 * 32
        self.off = off + n
        self.hi = max(self.hi, self.off)
        return self.arena.view(off, shape, dt)


def build_program(L, S, lam_inits):
    NG = S // TG
    NT = S // P
    nc = bass.Bass("TRN2", target_bir_lowering=False)

    def din(name, shape):
        return nc.dram_tensor(name, list(shape), F32, kind="ExternalInput")

    x_d = din("x", [S, D])
    w_in_d = din("w_in", [L, D, CIN])
    w_out_d = din("w_out", [L, D, D])
    w_up_d = din("w_up", [L, D, CUP])
    w_dn_d = din("w_down", [L, DFF, D])
    pvec_d = din("pvec", [L, P, NPV])
    fvec_d = din("fvec", [L, 4, D])
    lamv_d = din("lamv", [L, 1, 256])
    sg_d = din("sublng", [L, 1, P])
    cmat_d = din("cmat", [3, P, P])
    rope_d = din("rope", [2, P, S])
    out_d = nc.dram_tensor("out", [S, D], F32, kind="ExternalOutput")
    xs_d = nc.dram_tensor("xs_scratch", [S, D], F32)

    pe = Queue(nc, nc.tensor, "pe")
    pe.inorder = True
    act = Queue(nc, nc.scalar, "act")
    dve = Queue(nc, nc.vector, "dve")
    pool = Queue(nc, nc.gpsimd, "pool")
    sp = Queue(nc, nc.sync, "sp")

    ARENA_BYTES = 212800
    arena = Arena(nc, ARENA_BYTES)
    com = Layout(arena, 0)
    ident = com.get([P], BF16)
    pm = com.get([P], BF16)
    cmask = com.get([P], BF16)
    ones = com.get([P], BF16)
    identf = com.get([P], F32)
    pv = com.get([NPV], F32)
    sg = com.get([P], F32)
    small = com.get([64], F32)
    g1 = com.get([D], F32)
    g2 = com.get([D], F32)
    xin = [com.get([D], F32) for _ in range(2)]
    xres = [com.get([D], F32)]
    hbf = [com.get([D], BF16) for _ in range(2)]
    hT = com.get([NK, TG], BF16)
    tmps = com.get([4, TG], F32)
    ropeCS = com.get([2, TG], F32)
    ropeC = ropeCS[:, 0, :]
    ropeS = ropeCS[:, 1, :]
    xres_rope = ropeCS.rearrange("p a b -> p (a b)")
    lamb = tmps[:, 3, 0:256]
    base_phase = (com.hi + 31) // 32 * 32
    mx = Layout(arena, base_phase)
    w_in = mx.get([NK, CIN], BF16)
    w_out = mx.get([NK, D], BF16)
    KT = mx.get([4, S], BF16)
    V = mx.get([NT, 4, 130], BF16)
    QT = mx.get([4, TG], BF16)
    Ubuf = mx.get([4, 30 + TG], BF16)
    UO = mx.get([NK, TG], BF16)
    ybf_off = (mx.off + 31) // 32 * 32
    ybf = mx.get([4, TG], BF16)
    xres_ybf = arena.view(ybf_off, [D], F32)
    ysq_off = (mx.off + 31) // 32 * 32
    ysq = mx.get([4, TG], BF16)
    rt2 = arena.view(ysq_off, [TG], F32)
    rt3 = arena.view(ysq_off + 2048, [TG], F32)
    rl = mx.get([2, 4], F32)
    ssr = mx.get([4], F32)
    xoff = (mx.off + 31) // 32 * 32
    Dg = mx.get([CONV_K, P], BF16)
    Pt = [[arena.view(xoff + (2 * c + s_) * 1024, [TG], BF16) for s_ in range(2)] for c in range(2)]
    obf = arena.view(xoff + 4096, [4, P], BF16)
    zbf = arena.view(xoff + 5120, [TG], BF16)
    DgB = mx.get([CONV_K, P], BF16)
    ff = Layout(arena, base_phase)
    w_up = ff.get([NK, CUP], BF16)
    w_dn = ff.get([NJ, D], BF16)
    HID = ff.get([NJ, TG], BF16)
    Rb = [ff.get([2 + TG], BF16) for _ in range(3)]
    halo = ff.get([44, 2], BF16)
    D3 = [ff.get([3, P], BF16) for _ in range(2)]
    print("SBUF layout: common %d, mixer %d, ffn %d (limit %d)" % (com.hi, mx.hi, ff.hi, ARENA_BYTES))
    assert mx.hi <= ARENA_BYTES and ff.hi <= ARENA_BYTES

    psum = nc.alloc_psum_tensor("psum", [P, 8 * 512], F32)

    def bank(i, n=1):
        return psum[:, i * 512:(i + n) * 512]

    pT = bank(0).bitcast(BF16)

    B = {}

    def b(name):
        if name not in B:
            B[name] = Buf(name)
        return B[name]

    pb = [b("psum%d" % i) for i in range(8)]
    for x_ in pb:
        x_.excl = True
    WI_BLK = ["wi_a", "wi_g", "wi_q", "wi_k", "wi_v"]
    xalias = ["Pt00", "Pt01", "Pt10", "Pt11", "obf", "zbf"]
    mixer_names = WI_BLK + ["w_out0", "w_out1", "QT", "Ubuf", "Dg", "DgB", "UO", "ybf", "ysq", "rl0", "rl1", "ssr", "Vones"] + xalias + \
        ["KT%d" % g for g in range(NG)] + ["V%d" % g for g in range(NG)]
    ffn_names = ["wu_g%d" % i for i in range(6)] + ["wu_v%d" % i for i in range(6)] + ["wd%d" % i for i in range(6)] + \
        ["HID", "R0", "R1", "R2", "halo", "D30", "D31"]
    mixer_bufs = [b(n) for n in mixer_names]
    ffn_bufs = [b(n) for n in ffn_names]
    xalias_bufs = [b(n) for n in xalias]
    hT_bufs = [b("hT%d" % i) for i in range(4)]

    dsem = {}

    def ds(name):
        if name not in dsem:
            dsem[name] = Holder(nc, "d_" + name)
        return dsem[name]

    def sc(i):
        return small[:, i:i + 1]
    S1, S2, E1, E2, NLAM = range(5)
    SSQ0, RSTD0, PSSQ0, PRSTD0 = 8, 10, 12, 14
    EPSC = 16

    dma(pool, ds("c0"), ident, cmat_d[0], writes=[b("ident")])
    dma(pool, ds("c0b"), pm, cmat_d[1], writes=[b("pm")])
    dma(pool, ds("c0c"), cmask, cmat_d[2], writes=[b("cmask")])
    dma(sp, ds("c1"), identf, cmat_d[0], writes=[b("identf")])
    op(dve, lambda: nc.vector.memset(ones, 1.0), writes=[b("ones")])
    op(dve, lambda: nc.vector.memset(sc(EPSC), EPS), writes=[b("epsc")])

    def norm_chain(T, src_d, srcbuf):
        s_ = T % 2
        xb, hb = b("xin%d" % s_), b("hbf%d" % s_)
        sq, rs = b("ssq%d" % s_), b("rstd%d" % s_)
        dma(sp, ds("xin%d" % s_), xin[s_], src_d[T * P:(T + 1) * P, :], reads=[b(srcbuf)], writes=[xb])
        op(dve, lambda: nc.vector.memset(sc(SSQ0 + s_), 0.0), writes=[sq])
        op(act, lambda: nc.scalar.activation(out=hbf[s_], in_=xin[s_], func=AF.Square, accum_out=sc(SSQ0 + s_)),
           reads=[xb], writes=[hb, sq])
        op(act, lambda: nc.scalar.activation(out=sc(RSTD0 + s_), in_=sc(SSQ0 + s_), func=AF.Ln, bias=sc(EPSC), scale=1.0 / D),
           reads=[sq, b("epsc")], writes=[rs])
        op(act, lambda: nc.scalar.activation(out=sc(RSTD0 + s_), in_=sc(RSTD0 + s_), func=AF.Exp, scale=-0.5), reads=[rs], writes=[rs])
        op(dve, lambda: nc.vector.scalar_tensor_tensor(out=hbf[s_], in0=xin[s_], scalar=sc(RSTD0 + s_), in1=g1,
                                                       op0=ALU.mult, op1=ALU.mult),
           reads=[xb, rs, b("g1")], writes=[hb])

    def norm_T(T):
        s_ = T % 2
        tt = T % 4
        hb = b("hbf%d" % s_)
        for kc in range(NK):
            op(pe, lambda kc=kc: nc.tensor.transpose(out=pT[:, kc * P:(kc + 1) * P], in_=hbf[s_][:, kc * P:(kc + 1) * P],
                                                   identity=ident),
               reads=[hb, b("ident")], writes=[pb[0]], signal=(kc == NK - 1))
        op(dve, lambda: nc.vector.tensor_copy(out=hT[:, :, tt * P:(tt + 1) * P],
                                              in_=pT[:, 0:NK * P].rearrange("p (k c) -> p k c", k=NK)),
           reads=[pb[0]], writes=[hT_bufs[tt]])

    def mm_group(out_ap, pairs, reads, wbuf, start=True):
        n = len(pairs)
        for i, (l_, r_) in enumerate(pairs):
            op(pe, lambda l_=l_, r_=r_, i=i: nc.tensor.matmul(out=out_ap, lhsT=l_, rhs=r_, start=(start and i == 0),
                                                              stop=(i == n - 1)),
               reads=reads, writes=[wbuf], signal=(i == n - 1))

    def post_norm_residual(T, mbanks, src_ap, dst_ap, src_buf_name, dst_buf_name):
        s_ = T % 2
        m_ap = bank(mbanks[0], 2)
        mb = [pb[mbanks[0]], pb[mbanks[1]]]
        if s_ == 0:
            xr, xbs = xres[0], [b("xres0")]
        elif phase[0] == "mixer":
            xr, xbs = xres_ybf, [b("ybf")]
        else:
            xr, xbs = xres_rope, [b("ropeC"), b("ropeS")]
        sq, rs = b("pssq%d" % s_), b("prstd%d" % s_)
        dma(sp, ds("xres%d" % s_), xr, src_ap, reads=[b(src_buf_name)], writes=xbs)
        op(dve, lambda: nc.vector.memset(sc(PSSQ0 + s_), 0.0), writes=[sq])
        tmp32 = tmps[:, 2 * s_:2 * s_ + 2, :].rearrange("p a b -> p (a b)")
        tb = [b("tmp%d" % (2 * s_)), b("tmp%d" % (2 * s_ + 1))]
        op(act, lambda: nc.scalar.activation(out=tmp32, in_=m_ap, func=AF.Square, accum_out=sc(PSSQ0 + s_)),
           reads=mb, writes=tb + [sq])
        op(act, lambda: nc.scalar.activation(out=sc(PRSTD0 + s_), in_=sc(PSSQ0 + s_), func=AF.Ln, bias=sc(EPSC), scale=1.0 / D),
           reads=[sq, b("epsc")], writes=[rs])
        op(act, lambda: nc.scalar.activation(out=sc(PRSTD0 + s_), in_=sc(PRSTD0 + s_), func=AF.Exp, scale=-0.5), reads=[rs], writes=[rs])
        op(dve, lambda: nc.vector.scalar_tensor_tensor(out=tmp32, in0=m_ap, scalar=sc(PRSTD0 + s_), in1=g2,
                                                       op0=ALU.mult, op1=ALU.mult),
           reads=mb + [rs, b("g2")], writes=tb)
        op(dve, lambda: nc.vector.tensor_tensor(out=xr, in0=xr, in1=tmp32, op=ALU.add),
           reads=tb + xbs, writes=xbs)
        dma(sp, ds("xst%d" % s_), dst_ap, xr, reads=xbs, writes=[b(dst_buf_name)])

    rot = [1, 2, 3]
    rr = [0]
    phase = ["mixer"]
    srot = [0]

    def nbs():
        i = srot[0] % 4
        srot[0] += 1
        return i

    def nb():
        i = rot[rr[0] % len(rot)]
        rr[0] += 1
        return i

    for l in range(L):
        lam_init = lam_inits[l]
        src_d = x_d if l == 0 else xs_d
        dma(sp, ds("pv"), pv, pvec_d[l], writes=[b("pv")])
        dma(sp, ds("lamb"), lamb, lamv_d[l].broadcast_to([P, 256]), writes=[b("tmp3")])
        dma(sp, ds("sg"), sg, sg_d[l].broadcast_to([P, P]), writes=[b("sg")])
        dma(sp, ds("g1"), g1, fvec_d[l, 0:1, :].broadcast_to([P, D]), writes=[b("g1")])
        dma(sp, ds("g2"), g2, fvec_d[l, 1:2, :].broadcast_to([P, D]), writes=[b("g2")])
        jt = tmps[:, 2, 0:64]
        op(dve, lambda: nc.vector.scalar_tensor_tensor(out=jt, in0=lamb[:, 0:64], scalar=1.0, in1=lamb[:, 64:128],
                                                       op0=ALU.mult, op1=ALU.mult, accum_out=sc(S1)),
           reads=[b("tmp3")], writes=[b("tmp2"), b("s1")])
        op(dve, lambda: nc.vector.scalar_tensor_tensor(out=jt, in0=lamb[:, 128:192], scalar=1.0, in1=lamb[:, 192:256],
                                                       op0=ALU.mult, op1=ALU.mult, accum_out=sc(S2)),
           reads=[b("tmp3")], writes=[b("tmp2"), b("s2")])
        op(act, lambda: nc.scalar.activation(out=sc(E1), in_=sc(S1), func=AF.Exp), reads=[b("s1")], writes=[b("e1")])
        op(act, lambda: nc.scalar.activation(out=sc(E2), in_=sc(S2), func=AF.Exp), reads=[b("s2")], writes=[b("e2")])
        op(dve, lambda: nc.vector.tensor_tensor(out=sc(NLAM), in0=sc(E2), in1=sc(E1), op=ALU.subtract),
           reads=[b("e1"), b("e2")], writes=[b("nlam")])
        op(dve, lambda: nc.vector.tensor_scalar(out=sc(NLAM), in0=sc(NLAM), scalar1=-float(lam_init), scalar2=None,
                                                op0=ALU.add),
           reads=[b("nlam")], writes=[b("nlam")])
        op(act, lambda: nc.scalar.mul(out=sg, in_=sg, mul=float(1.0 - lam_init)), reads=[b("sg")], writes=[b("sg")])

        first = True
        for blk in (1, 0, 2, 3, 4):
            c0 = blk * 512
            dma(pool, ds("w_in%d" % blk), w_in[:, :, c0:c0 + 512],
                w_in_d[l, :, c0:c0 + 512].rearrange("(k p) c -> p k c", p=P),
                writes=[b(WI_BLK[blk])] + (ffn_bufs if first else []))
            first = False
        for hf in range(2):
            dma(pool, ds("w_out%d" % hf), w_out[:, hf * 4:(hf + 1) * 4, :],
                w_out_d[l, hf * 512:(hf + 1) * 512, :].rearrange("(k p) c -> p k c", p=P), writes=[b("w_out%d" % hf)])
        op(pool, lambda: nc.gpsimd.memset(Ubuf[:, :, 0:30], 0.0), writes=[b("Ubuf")])
        op(pool, lambda: nc.gpsimd.memset(V[:, :, :, 128:130], 1.0), writes=[b("Vones")])

        phase[0] = "mixer"
        for T in range(4):
            norm_chain(T, src_d, "xs%d" % T) if T < 2 else None
        norm_T(0)
        norm_T(1)
        norm_chain(2, src_d, "xs2")
        norm_chain(3, src_d, "xs3")
        norm_T(2)
        norm_T(3)
        for G in range(NG):
            t0 = G * TG
            nxt = G + 1 < NG
            def rope_load(Gt):
                dma(sp, ds("ropeC"), ropeC, rope_d[0, :, Gt * TG:(Gt + 1) * TG], writes=[b("ropeC")])
                dma(sp, ds("ropeS"), ropeS, rope_d[1, :, Gt * TG:(Gt + 1) * TG], writes=[b("ropeS")])

            if G == 0:
                rope_load(0)
            if nxt and G == 0:
                for T in (4, 5):
                    norm_chain(T, src_d, "xs%d" % T)

            def proj(col0, bi):
                mm_group(bank(bi), [(w_in[:, kc, col0:col0 + P], hT[:, kc, :]) for kc in range(NK)],
                         [b(WI_BLK[col0 // 512])] + hT_bufs, pb[bi])

            def dg_gen(ct):
                wv = pv[:, PV_CW + ct * CONV_K:PV_CW + (ct + 1) * CONV_K]
                dgt, dgb = (DgB, b("DgB")) if ct % 2 == 0 else (Dg, b("Dg"))
                op(pool, lambda: nc.gpsimd.tensor_tensor(out=dgt, in0=identf.unsqueeze(1).broadcast_to([P, CONV_K, P]),
                                                         in1=wv.unsqueeze(2).broadcast_to([P, CONV_K, P]), op=ALU.mult),
                   reads=[b("identf"), b("pv")], writes=[dgb] + (xalias_bufs if ct % 2 == 1 else []))

            def glu_step(ct):
                bg = nb()
                proj(512 + ct * P, bg)
                sgt = tmps[:, 3, :]
                sgb = b("tmp3")
                op(act, lambda: nc.scalar.activation(out=sgt, in_=bank(bg), func=AF.Sigmoid), reads=[pb[bg]], writes=[sgb])
                ba = nb()
                proj(ct * P, ba)
                op(dve, lambda: nc.vector.tensor_tensor(out=Ubuf[:, ct, 30:30 + TG], in0=bank(ba), in1=sgt, op=ALU.mult),
                   reads=[pb[ba], sgb], writes=[b("Ubuf")])

            def halo_copy():
                op(pool, lambda: nc.gpsimd.tensor_copy(out=Ubuf[:, :, 0:30], in_=Ubuf[:, :, TG:TG + 30]),
                   reads=[b("Ubuf")], writes=[b("Ubuf")])

            dg_gen(0)
            if G == 0:
                for ct in range(4):
                    glu_step(ct)
            def m3b_unit(Gt, which, h):
                col0 = 1024 + which * 512 + h * P
                bz = nb()
                proj(col0, bz)
                op(act, lambda: nc.scalar.copy(out=zbf, in_=bank(bz)), reads=[pb[bz]], writes=[b("zbf")])
                bp = nb()
                mm_group(bank(bp), [(pm, zbf)], [b("pm"), b("zbf")], pb[bp])
                op(dve, lambda: nc.vector.tensor_tensor(out=rt2, in0=bank(bz), in1=ropeC, op=ALU.mult),
                   reads=[pb[bz], b("ropeC")], writes=[b("ysqA")])
                op(dve, lambda: nc.vector.tensor_tensor(out=rt3, in0=bank(bp), in1=ropeS, op=ALU.mult),
                   reads=[pb[bp], b("ropeS")], writes=[b("ysqB")])
                if which == 0:
                    dst, dbuf = QT[:, h, :], b("QT")
                else:
                    dst, dbuf = KT[:, h, Gt * TG:(Gt + 1) * TG], b("KT%d" % Gt)
                op(dve, lambda: nc.vector.tensor_tensor(out=dst, in0=rt2, in1=rt3, op=ALU.add),
                   reads=[b("ysqA"), b("ysqB")], writes=[dbuf])

            if G == 0:
                for which in range(2):
                    for h in range(4):
                        m3b_unit(0, which, h)
            if nxt:
                rope_load(G + 1)
            for tt in range(4):
                T = G * 4 + tt
                bv = nb()
                mm_group(bank(bv), [(hT[:, kc, tt * P:(tt + 1) * P], w_in[:, kc, 2048:2560]) for kc in range(NK)],
                         [b("wi_v"), hT_bufs[tt]], pb[bv])
                op(act, lambda bv=bv, T=T: nc.scalar.copy(out=V[:, T, :, 0:128],
                                                          in_=bank(bv).rearrange("p (h c) -> p h c", h=4)),
                   reads=[pb[bv]], writes=[b("V%d" % G)])
            if nxt:
                norm_T(4 * G + 4)
                norm_T(4 * G + 5)
                for T in (4 * G + 6, 4 * G + 7):
                    norm_chain(T, src_d, "xs%d" % T)
            dg_gen(1)
            for ct in range(4):
                dgt, dgb = (DgB, b("DgB")) if ct % 2 == 0 else (Dg, b("Dg"))
                bc = nb()
                mm_group(bank(bc), [(dgt[:, k, :], Ubuf[:, ct, k:k + TG]) for k in range(CONV_K)],
                         [dgb, b("Ubuf")], pb[bc])
                if ct + 2 < 4:
                    dg_gen(ct + 2)
                cb = pv[:, PV_CB + ct:PV_CB + ct + 1]
                op(act, lambda bc=bc, ct=ct, cb=cb: nc.scalar.activation(out=ybf[:, ct, :], in_=bank(bc), func=AF.Identity, bias=cb),
                   reads=[pb[bc], b("pv")], writes=[b("ybf")])
                op(act, lambda bc=bc, ct=ct, cb=cb: nc.scalar.activation(out=ysq[:, ct, :], in_=bank(bc), func=AF.Square, bias=cb),
                   reads=[pb[bc], b("pv")], writes=[b("ysq"), b("ysqA"), b("ysqB")])
            b1 = nb()
            mm_group(bank(b1), [(ones, ybf[:, ct, :]) for ct in range(4)], [b("ones"), b("ybf")], pb[b1])
            b2 = nb()
            mm_group(bank(b2), [(ones, ysq[:, ct, :]) for ct in range(4)], [b("ones"), b("ysq")], pb[b2])
            mean, msq, rstd_t = tmps[:, 0, :], tmps[:, 1, :], tmps[:, 2, :]
            op(act, lambda: nc.scalar.mul(out=mean, in_=bank(b1), mul=1.0 / 512), reads=[pb[b1]], writes=[b("tmp0")])
            op(act, lambda: nc.scalar.activation(out=msq, in_=bank(b1), func=AF.Square, scale=1.0 / 512),
               reads=[pb[b1]], writes=[b("tmp1")])
            op(dve, lambda: nc.vector.scalar_tensor_tensor(out=rstd_t, in0=bank(b2), scalar=1.0 / 512, in1=msq,
                                                           op0=ALU.mult, op1=ALU.subtract),
               reads=[pb[b2], b("tmp1")], writes=[b("tmp2")])
            t3 = tmps[:, 3, :]

            def ln_a2():
                op(act, lambda: nc.scalar.activation(out=rstd_t, in_=rstd_t, func=AF.Ln, bias=sc(EPSC), scale=1.0),
                   reads=[b("tmp2"), b("epsc")], writes=[b("tmp2")])
                op(act, lambda: nc.scalar.activation(out=rstd_t, in_=rstd_t, func=AF.Exp, scale=-0.5), reads=[b("tmp2")], writes=[b("tmp2")])

            def ln_d2(ct):
                op(dve, lambda: nc.vector.tensor_tensor(out=t3, in0=ybf[:, ct, :], in1=mean, op=ALU.subtract),
                   reads=[b("ybf"), b("tmp0")], writes=[b("tmp3")])
                op(dve, lambda: nc.vector.tensor_tensor(out=t3, in0=t3, in1=rstd_t, op=ALU.mult),
                   reads=[b("tmp3"), b("tmp2")], writes=[b("tmp3")])

            def ln_a3(ct):
                op(act, lambda: nc.scalar.activation(out=UO[:, ct, :], in_=t3, func=AF.Silu,
                                                     scale=pv[:, PV_LG + ct:PV_LG + ct + 1],
                                                     bias=pv[:, PV_LB + ct:PV_LB + ct + 1]),
                   reads=[b("tmp3"), b("pv")], writes=[b("UO")])

            accv = [psum[:, (4 + 2 * c) * 512:(6 + 2 * c) * 512].rearrange("p (q w) -> p q w", q=4) for c in range(2)]
            accb = [[pb[4], pb[5]], [pb[6], pb[7]]]
            nkt = 4 * G + 4
            sbank = {}

            def emit_qk(h, kt, c):
                r = kt - 4 * G
                c0 = max(r, 0) * P
                sbk = nbs()
                sbank[(kt, c)] = sbk
                mm_group(bank(sbk)[:, c0:TG],
                         [(KT[c * 64:(c + 1) * 64, h, kt * P:(kt + 1) * P], QT[c * 64:(c + 1) * 64, h, c0:TG])],
                         [b("KT%d" % (kt // 4)), b("QT")], pb[sbk])

            def emit_exp(h, kt, c):
                r = kt - 4 * G
                c0 = max(r, 0) * P
                sbk = sbank[(kt, c)]
                sl = kt % 2
                ptb = b("Pt%d%d" % (c, sl))
                pt = Pt[c][sl]
                op(act, lambda: nc.scalar.activation(out=pt[:, c0:TG], in_=bank(sbk)[:, c0:TG], func=AF.Exp, scale=0.125),
                   reads=[pb[sbk]], writes=[ptb])
                if r >= 0:
                    op(dve, lambda: nc.vector.tensor_tensor(out=pt[:, c0:c0 + P], in0=pt[:, c0:c0 + P], in1=cmask, op=ALU.mult),
                       reads=[ptb, b("cmask")], writes=[ptb])

            def emit_av(h, kt, c):
                r = kt - 4 * G
                q0 = max(r, 0)
                sl = kt % 2
                ptb = b("Pt%d%d" % (c, sl))
                pt = Pt[c][sl]
                for ql in range(q0, 4):
                    bk = 4 + 2 * c + ql // 2
                    reg = psum[:, bk * 512 + (ql % 2) * 256: bk * 512 + (ql % 2) * 256 + 129]
                    first_ = (kt == 0 and ql % 2 == 0)
                    last_ = (kt == 4 * G + ql)
                    op(pe, lambda reg=reg, ql=ql, first_=first_, last_=last_:
                       nc.tensor.matmul(out=reg, lhsT=pt[:, ql * P:(ql + 1) * P], rhs=V[:, kt, h, 0:129],
                                        start=first_, stop=last_, skip_group_check=True),
                       reads=[ptb, b("V%d" % (kt // 4)), b("Vones")], writes=[pb[bk]], signal=(ql == 3))

            def finalize_a(h):
                t1 = tmps[:, 0, :].rearrange("p (q w) -> p q w", q=4)
                t2 = tmps[:, 1, :].rearrange("p (q w) -> p q w", q=4)
                o3 = tmps[:, 2, :].rearrange("p (q w) -> p q w", q=4)
                op(dve, lambda: nc.vector.reciprocal(out=rl[:, 0, :], in_=accv[0][:, :, 128]), reads=accb[0], writes=[b("rl0")])
                op(dve, lambda: nc.vector.tensor_copy(out=t1, in_=accv[0][:, :, 0:128]), reads=accb[0], writes=[b("tmp0")])
                op(dve, lambda: nc.vector.reciprocal(out=rl[:, 1, :], in_=accv[1][:, :, 128]), reads=accb[1], writes=[b("rl1")])
                op(dve, lambda: nc.vector.tensor_copy(out=t2, in_=accv[1][:, :, 0:128]), reads=accb[1], writes=[b("tmp1")])
                op(dve, lambda: nc.vector.tensor_scalar(out=rl[:, 1, :], in0=rl[:, 1, :], scalar1=sc(NLAM), scalar2=None,
                                                        op0=ALU.mult),
                   reads=[b("rl1"), b("nlam")], writes=[b("rl1")])
                op(dve, lambda: nc.vector.tensor_tensor(out=t1, in0=t1,
                                                        in1=rl[:, 0, :].unsqueeze(2).broadcast_to([P, 4, P]), op=ALU.mult),
                   reads=[b("tmp0"), b("rl0")], writes=[b("tmp0")])
                op(dve, lambda: nc.vector.tensor_tensor(out=t2, in0=t2,
                                                        in1=rl[:, 1, :].unsqueeze(2).broadcast_to([P, 4, P]), op=ALU.mult),
                   reads=[b("tmp1"), b("rl1")], writes=[b("tmp1")])
                op(dve, lambda: nc.vector.tensor_tensor(out=o3, in0=t1, in1=t2, op=ALU.add),
                   reads=[b("tmp0"), b("tmp1")], writes=[b("tmp2")])
                op(dve, lambda: nc.vector.tensor_tensor(out=t1, in0=o3, in1=o3, op=ALU.mult),
                   reads=[b("tmp2")], writes=[b("tmp0")])
                op(dve, lambda: nc.vector.tensor_reduce(out=ssr, in_=t1, axis=AX.X, op=ALU.add),
                   reads=[b("tmp0")], writes=[b("ssr")])

            def finalize_a2(h):
                o3 = tmps[:, 2, :].rearrange("p (q w) -> p q w", q=4)
                op(act, lambda: nc.scalar.activation(out=ssr, in_=ssr, func=AF.Ln, bias=sc(EPSC), scale=1.0 / 128),
                   reads=[b("ssr"), b("epsc")], writes=[b("ssr")])
                op(act, lambda: nc.scalar.activation(out=ssr, in_=ssr, func=AF.Exp, scale=-0.5), reads=[b("ssr")], writes=[b("ssr")])
                op(dve, lambda: nc.vector.tensor_tensor(out=o3, in0=o3, in1=ssr.unsqueeze(2).broadcast_to([P, 4, P]), op=ALU.mult),
                   reads=[b("tmp2"), b("ssr")], writes=[b("tmp2")])
                op(dve, lambda: nc.vector.tensor_tensor(out=obf, in0=o3, in1=sg.unsqueeze(1).broadcast_to([P, 4, P]), op=ALU.mult),
                   reads=[b("tmp2"), b("sg")], writes=[b("obf")])

            def finalize_b(h):
                for ql in range(4):
                    op(pe, lambda ql=ql: nc.tensor.transpose(out=pT[:, ql * P:(ql + 1) * P], in_=obf[:, ql, :], identity=ident),
                       reads=[b("obf"), b("ident")], writes=[pb[0]], signal=(ql == 3))
                op(act, lambda: nc.scalar.copy(out=UO[:, 4 + h, :], in_=pT[:, 0:TG]), reads=[pb[0]], writes=[b("UO")])

            deferred = None
            if nxt:
                norm_T(4 * G + 6)
                norm_T(4 * G + 7)
                halo_copy()
                glu_step(0)
                glu_step(1)
            for h in range(4):
                emit_qk(h, 0, 0)
                emit_qk(h, 0, 1)
                for kt in range(nkt):
                    if kt + 1 < nkt:
                        emit_qk(h, kt + 1, 0)
                        emit_qk(h, kt + 1, 1)
                    emit_exp(h, kt, 0)
                    emit_exp(h, kt, 1)
                    if h == 0 and kt < 4:
                        if kt == 0:
                            ln_a2()
                        else:
                            ln_a3(kt - 1)
                        ln_d2(kt)
                    if kt == 1 and deferred is not None:
                        finalize_a2(deferred)
                    emit_av(h, kt, 0)
                    emit_av(h, kt, 1)
                    if kt == (4 if nkt > 4 else 2) and deferred is not None:
                        finalize_b(deferred)
                        deferred = None
                if h == 0:
                    ln_a3(3)
                if h == 3 and G + 2 < NG:
                    for T in (4 * G + 8, 4 * G + 9):
                        norm_chain(T, src_d, "xs%d" % T)
                finalize_a(h)
                deferred = h
                if h == 3 and nxt:
                    glu_step(2)
                    glu_step(3)
            finalize_a2(deferred)
            finalize_b(deferred)
            def wout_tile(tt):
                T = G * 4 + tt
                mbanks = (4, 5) if tt % 2 == 0 else (6, 7)
                for half in range(2):
                    mm_group(bank(mbanks[half]),
                             [(UO[:, cc, tt * P:(tt + 1) * P], w_out[:, cc, half * 512:(half + 1) * 512]) for cc in range(NK)],
                             [b("UO"), b("w_out0"), b("w_out1")], pb[mbanks[half]])
                post_norm_residual(T, mbanks, src_d[T * P:(T + 1) * P, :], xs_d[T * P:(T + 1) * P, :], "xs%d" % T, "xs%d" % T)

            wout_tile(0)
            wout_tile(1)
            if nxt:
                m3b_unit(G + 1, 0, 0)
                m3b_unit(G + 1, 0, 1)
            wout_tile(2)
            if nxt:
                m3b_unit(G + 1, 0, 2)
                m3b_unit(G + 1, 0, 3)
            wout_tile(3)
            if nxt:
                for h in range(4):
                    m3b_unit(G + 1, 1, h)

        phase[0] = "ffn"
        dma(sp, ds("g1"), g1, fvec_d[l, 2:3, :].broadcast_to([P, D]), writes=[b("g1")])
        dma(sp, ds("g2"), g2, fvec_d[l, 3:4, :].broadcast_to([P, D]), writes=[b("g2")])
        first = True
        for blk in range(6):
            w_ = min(512, DFF - blk * 512)
            for which in range(2):
                c0 = which * DFF + blk * 512
                dma(pool, ds("w_up%d%d" % (which, blk)), w_up[:, :, c0:c0 + w_],
                    w_up_d[l, :, c0:c0 + w_].rearrange("(k p) c -> p k c", p=P),
                    writes=[b(("wu_g%d" if which == 0 else "wu_v%d") % blk)] + (mixer_bufs if first else []))
                first = False
        for gq in range(6):
            j0 = gq * 4
            j1 = min(j0 + 4, NJ)
            dma(pool, ds("w_dn%d" % gq), w_dn[:, j0:j1, :],
                w_dn_d[l, j0 * P:j1 * P, :].rearrange("(j p) c -> p j c", p=P), writes=[b("wd%d" % gq)])
        op(dve, lambda: nc.vector.memset(halo, 0.0), writes=[b("halo")])
        dst_d = out_d if l == L - 1 else xs_d
        norm_chain(0, xs_d, "xs0")
        norm_chain(1, xs_d, "xs1")
        norm_T(0)
        norm_T(1)
        norm_chain(2, xs_d, "xs2")
        norm_chain(3, xs_d, "xs3")
        norm_T(2)
        norm_T(3)
        for G in range(NG):
            nxt = G + 1 < NG
            if nxt and G == 0:
                for T in (4, 5):
                    norm_chain(T, xs_d, "xs%d" % T)
            ubank = {}

            def up_tile(n):
                jp, which = n // 2, n % 2
                j = which * NJ + jp
                col0 = which * DFF + jp * P
                s_ = n % 3
                bu = 1 + n % 3
                ubank[n] = bu
                wb_ = b(("wu_g%d" if which == 0 else "wu_v%d") % (jp // 4))
                mm_group(bank(bu), [(w_up[:, kc, col0:col0 + P], hT[:, kc, :]) for kc in range(NK)],
                         [wb_] + hT_bufs, pb[bu])
                rbuf = b("R%d" % s_)
                op(dve, lambda: nc.vector.tensor_copy(out=Rb[s_][:, 0:2], in_=halo[:, j, :]),
                   reads=[b("halo")], writes=[rbuf])
                op(act, lambda: nc.scalar.copy(out=Rb[s_][:, 2:2 + TG], in_=bank(bu)), reads=[pb[bu]], writes=[rbuf])
                op(dve, lambda: nc.vector.tensor_copy(out=halo[:, j, :], in_=Rb[s_][:, TG:TG + 2]),
                   reads=[rbuf], writes=[b("halo")])
                base_ = 4 if jp % 2 == 0 else 6
                bcv = base_ + which
                if which == 0:
                    op(act, lambda: nc.scalar.activation(out=bank(bcv), in_=bank(bu), func=AF.Identity,
                                                         scale=pv[:, PV_FW + j * 3 + 2:PV_FW + j * 3 + 3],
                                                         bias=pv[:, PV_FB + j:PV_FB + j + 1]),
                       reads=[pb[bu], b("pv")], writes=[pb[bcv]])
                else:
                    fw = pv[:, PV_FW + j * 3:PV_FW + j * 3 + 3]
                    op(pool, lambda: nc.gpsimd.tensor_tensor(out=D3[jp % 2], in0=identf.unsqueeze(1).broadcast_to([P, 3, P]),
                                                             in1=fw.unsqueeze(2).broadcast_to([P, 3, P]), op=ALU.mult),
                       reads=[b("identf"), b("pv")], writes=[b("D3%d" % (jp % 2))])

            def conv_tile(n):
                jp, which = n // 2, n % 2
                j = which * NJ + jp
                s_ = n % 3
                base_ = 4 if jp % 2 == 0 else 6
                bcv = base_ + which
                rbuf = b("R%d" % s_)
                if which == 0:
                    for k in (1, 0):
                        op(dve, lambda k=k: nc.vector.scalar_tensor_tensor(out=bank(bcv), in0=Rb[s_][:, k:k + TG],
                                                                           scalar=pv[:, PV_FW + j * 3 + k:PV_FW + j * 3 + k + 1],
                                                                           in1=bank(bcv), op0=ALU.mult, op1=ALU.add),
                           reads=[rbuf, pb[bcv], b("pv")], writes=[pb[bcv]])
                    gl = tmps[:, 2 + jp % 2, :]
                    glb = b("tmp%d" % (2 + jp % 2))
                    op(act, lambda: nc.scalar.activation(out=gl, in_=bank(base_), func=AF.Gelu_apprx_tanh),
                       reads=[pb[base_]], writes=[glb])
                else:
                    mm_group(bank(bcv), [(D3[jp % 2][:, k, :], Rb[s_][:, k:k + TG]) for k in range(3)],
                             [b("D3%d" % (jp % 2)), rbuf], pb[bcv])
                    gl = tmps[:, 2 + jp % 2, :]
                    glb = b("tmp%d" % (2 + jp % 2))
                    op(dve, lambda: nc.vector.scalar_tensor_tensor(out=HID[:, jp, :], in0=bank(bcv),
                                                                   scalar=pv[:, PV_FB + j:PV_FB + j + 1],
                                                                   in1=gl, op0=ALU.add, op1=ALU.mult),
                       reads=[pb[bcv], glb, b("pv")], writes=[b("HID")])

            def down_tile(tt):
                T = G * 4 + tt
                mbanks = (6, 7) if tt % 2 == 0 else (4, 5)
                for half in range(2):
                    for jp in range(NJ):
                        op(pe, lambda jp=jp, half=half: nc.tensor.matmul(out=bank(mbanks[half]), lhsT=HID[:, jp, tt * P:(tt + 1) * P],
                                                                         rhs=w_dn[:, jp, half * 512:(half + 1) * 512],
                                                                         start=(jp == 0), stop=(jp == NJ - 1)),
                           reads=[b("HID"), b("wd%d" % (jp // 4))], writes=[pb[mbanks[half]]], signal=(jp == NJ - 1))
                post_norm_residual(T, mbanks, xs_d[T * P:(T + 1) * P, :], dst_d[T * P:(T + 1) * P, :],
                                   "xs%d" % T, ("out%d" % T) if l == L - 1 else ("xs%d" % T))

            NTL = 2 * NJ
            up_tile(0)
            up_tile(1)
            for n in range(NTL):
                if n + 2 < NTL:
                    up_tile(n + 2)
                conv_tile(n)
            if nxt:
                norm_T(4 * G + 4)
                norm_T(4 * G + 5)
                for T in (4 * G + 6, 4 * G + 7):
                    norm_chain(T, xs_d, "xs%d" % T)
            down_tile(0)
            down_tile(1)
            if nxt:
                norm_T(4 * G + 6)
                norm_T(4 * G + 7)
            if G + 2 < NG:
                for T in (4 * G + 8, 4 * G + 9):
                    norm_chain(T, xs_d, "xs%d" % T)
            down_tile(2)
            down_tile(3)

    for nm in ("xst0", "xst1"):
        sp.eng.wait_ge(dsem[nm].sem, dsem[nm].cnt)
    stats = {q.name: (q.nins, q.nwait) for q in (pe, act, dve, pool, sp)}
    print("instr/wait counts:", stats)
    return nc


def _consts(S):
    ident = np.eye(P, dtype=np.float32)
    pmat = np.zeros((P, P), np.float32)
    for base in (0, 64):
        for i in range(8):
            pmat[base + i + 8, base + i] = -1.0
            pmat[base + i, base + i + 8] = 1.0
    mask = (np.arange(P)[:, None] <= np.arange(P)[None, :]).astype(np.float32)
    pos = np.arange(S, dtype=np.float32)
    inv_freq = (np.float32(500000.0) ** (-np.arange(0, 16, 2, dtype=np.float32) / np.float32(16))).astype(np.float32)
    ang = (pos[:, None] * inv_freq[None, :]).astype(np.float32)
    cos = np.cos(ang).astype(np.float32)
    sin = np.sin(ang).astype(np.float32)
    C = np.ones((P, S), np.float32)
    Sn = np.zeros((P, S), np.float32)
    for base in (0, 64):
        for i in range(16):
            C[base + i] = cos[:, i % 8]
            Sn[base + i] = sin[:, i % 8]
    return np.stack([ident, pmat, mask]), np.stack([C, Sn])


def _pack_small(inp, L):
    pvec = np.zeros((L, P, NPV), np.float32)
    cw = np.asarray(inp["conv_w"], np.float32)
    pvec[:, :, PV_CW:PV_CW + 124] = cw.reshape(L, 31, 4, P).transpose(0, 3, 2, 1).reshape(L, P, 124)
    for name, off in (("conv_b", PV_CB), ("conv_ln_g", PV_LG), ("conv_ln_b", PV_LB)):
        pvec[:, :, off:off + 4] = np.asarray(inp[name], np.float32).reshape(L, 4, P).transpose(0, 2, 1)
    fw = np.asarray(inp["ffn_conv_w"], np.float32)
    pvec[:, :, PV_FW:PV_FW + 132] = fw.reshape(L, 3, 44, P).transpose(0, 3, 2, 1).reshape(L, P, 132)
    pvec[:, :, PV_FB:PV_FB + 44] = np.asarray(inp["ffn_conv_b"], np.float32).reshape(L, 44, P).transpose(0, 2, 1)
    fvec = np.stack([np.asarray(inp[k], np.float32) for k in
                     ("pre_mix_norm", "post_mix_norm", "pre_ffn_norm", "post_ffn_norm")], axis=1)
    lamv = np.concatenate([np.asarray(inp[k], np.float32) for k in
                           ("lambda_q1", "lambda_k1", "lambda_q2", "lambda_k2")], axis=1).reshape(L, 1, 256)
    sgv = np.asarray(inp["subln_g"], np.float32).reshape(L, 1, P)
    return pvec, np.ascontiguousarray(fvec), np.ascontiguousarray(lamv), sgv


_PROG_CACHE = {}


def kernel(**inputs):
    x = np.asarray(inputs["x"], np.float32)
    Bsz, S, _ = x.shape
    L = np.asarray(inputs["w_in"]).shape[0]
    lam_inits = [0.8 - 0.6 * math.exp(-0.3 * l) for l in range(L)]
    key = (L, S)
    if key not in _PROG_CACHE:
        _PROG_CACHE[key] = build_program(L, S, lam_inits)
    nc = _PROG_CACHE[key]
    cmat, rope = _consts(S)
    pvec, fvec, lamv, sgv = _pack_small(inputs, L)
    shared = {
        "w_in": np.ascontiguousarray(inputs["w_in"], np.float32),
        "w_out": np.ascontiguousarray(inputs["w_out"], np.float32),
        "w_up": np.ascontiguousarray(inputs["w_up"], np.float32),
        "w_down": np.ascontiguousarray(inputs["w_down"], np.float32),
        "pvec": pvec, "fvec": fvec, "lamv": lamv, "sublng": sgv, "cmat": cmat, "rope": rope,
    }
    in_maps = []
    for c in range(Bsz):
        m = dict(shared)
        m["x"] = np.ascontiguousarray(x[c])
        in_maps.append(m)
    res = run_bass_kernel_spmd(nc, in_maps, core_ids=list(range(Bsz)))
    return np.stack([np.asarray(r["out"], np.float32) for r in res.results], axis=0)
```

```python
import math
import numpy as np
import concourse.bass as bass
import concourse.mybir as mybir
from concourse.bass_utils import run_bass_kernel_spmd

F32 = mybir.dt.float32
BF16 = mybir.dt.bfloat16
AF = mybir.ActivationFunctionType
ALU = mybir.AluOpType
AX = mybir.AxisListType

P = 128
D = 1024
NK = 8
CIN = 2560
DFF = 2816
NJ = 22
CUP = 5632
TG = 512
EPS = 1e-6
CONV_K = 31
PV_CW = 0
PV_CB = PV_CW + 4 * 31
PV_LG = PV_CB + 4
PV_LB = PV_LG + 4
PV_FW = PV_LB + 4
PV_FB = PV_FW + 44 * 3
NPV = PV_FB + 44


class Buf:
    __slots__ = ("name", "w", "r", "excl")

    def __init__(self, name, excl=False):
        self.name = name
        self.w = None
        self.r = {}
        self.excl = excl


class Holder:
    def __init__(self, nc, name):
        self.name = name
        self.sem = nc.alloc_semaphore(name)
        self.cnt = 0


class Queue(Holder):
    def __init__(self, nc, eng, name):
        super().__init__(nc, "q_" + name)
        self.eng = eng
        self.seen = {}
        self.nwait = 0
        self.nins = 0
        self.inorder = False

    def wait_tok(self, holder, val):
        if holder is self and self.inorder:
            return
        if self.seen.get(holder, 0) >= val:
            return
        assert val <= holder.cnt, (self.name, holder.name, val, holder.cnt)
        self.eng.wait_ge(holder.sem, val)
        self.seen[holder] = val
        self.nwait += 1


def _deps(q, reads, writes):
    toks = {}
    for b in reads:
        if b.w is not None:
            h, v = b.w
            if toks.get(h, 0) < v:
                toks[h] = v
        if b.excl:
            for h, v in b.r.items():
                if h is not q and toks.get(h, 0) < v:
                    toks[h] = v
    for b in writes:
        if b.w is not None:
            h, v = b.w
            if toks.get(h, 0) < v:
                toks[h] = v
        for h, v in b.r.items():
            if toks.get(h, 0) < v:
                toks[h] = v
    for h, v in toks.items():
        q.wait_tok(h, v)


def _record(tok, reads, writes):
    h, v = tok
    for b in reads:
        if b.r.get(h, 0) < v:
            b.r[h] = v
    for b in writes:
        b.w = tok
        b.r = {}


def op(q, ins_fn, reads=(), writes=(), signal=True):
    _deps(q, reads, writes)
    ins = ins_fn()
    q.nins += 1
    if signal:
        ins.then_inc(q.sem, 1)
        q.cnt += 1
        tok = (q, q.cnt)
    else:
        tok = (q, q.cnt + 1)
    _record(tok, reads, writes)
    return ins


def dma(q, dsem, out, in_, reads=(), writes=()):
    _deps(q, reads, writes)
    ins = q.eng.dma_start(out=out, in_=in_)
    ins.then_inc(dsem.sem, 16)
    dsem.cnt += 16
    _record((dsem, dsem.cnt), reads, writes)
    return ins


class Arena:
    def __init__(self, nc, nbytes):
        self.nbytes = nbytes
        self.t = nc.alloc_sbuf_tensor("arena", [P, nbytes // 2], BF16)

    def view(self, off, shape, dt):
        esz = 4 if dt == F32 else 2
        n = 1
        for d_ in shape:
            n *= d_
        nb = n * esz
        assert off % 4 == 0 and off + nb <= self.nbytes, (off, nb, self.nbytes)
        ap = self.t[:, off // 2:(off + nb) // 2]
        if dt == F32:
            ap = ap.bitcast(F32)
        if len(shape) == 2:
            ap = ap.rearrange("p (a b) -> p a b", a=shape[0])
        elif len(shape) == 3:
            ap = ap.rearrange("p (a b c) -> p a b c", a=shape[0], b=shape[1])
        return ap


class Layout:
    def __init__(self, arena, base=0):
        self.arena = arena
        self.off = base
        self.hi = base

    def get(self, shape, dt):
        esz = 4 if dt == F32 else 2
        n = esz
        for d_ in shape:
            n *= d_
        off = (self.off + 31) // 32 * 32
        self.off = off + n
        self.hi = max(self.hi, self.off)
        return self.arena.view(off, shape, dt)


def build_program(L, S, lam_inits):
    NG = S // TG
    NT = S // P
    nc = bass.Bass("TRN2", target_bir_lowering=False)

    def din(name, shape):
        return nc.dram_tensor(name, list(shape), F32, kind="ExternalInput")

    x_d = din("x", [S, D])
    w_in_d = din("w_in", [L, D, CIN])
    w_out_d = din("w_out", [L, D, D])
    w_up_d = din("w_up", [L, D, CUP])
    w_dn_d = din("w_down", [L, DFF, D])
    pvec_d = din("pvec", [L, P, NPV])
    fvec_d = din("fvec", [L, 4, D])
    lamv_d = din("lamv", [L, 1, 256])
    sg_d = din("sublng", [L, 1, P])
    cmat_d = din("cmat", [3, P, P])
    rope_d = din("rope", [2, P, S])
    out_d = nc.dram_tensor("out", [S, D], F32, kind="ExternalOutput")
    xs_d = nc.dram_tensor("xs_scratch", [S, D], F32)

    pe = Queue(nc, nc.tensor, "pe")
    pe.inorder = True
    act = Queue(nc, nc.scalar, "act")
    dve = Queue(nc, nc.vector, "dve")
    pool = Queue(nc, nc.gpsimd, "pool")
    sp = Queue(nc, nc.sync, "sp")

    ARENA_BYTES = 212800
    arena = Arena(nc, ARENA_BYTES)
    com = Layout(arena, 0)
    ident = com.get([P], BF16)
    pm = com.get([P], BF16)
    cmask = com.get([P], BF16)
    ones = com.get([P], BF16)
    identf = com.get([P], F32)
    pv = com.get([NPV], F32)
    sg = com.get([P], F32)
    small = com.get([64], F32)
    g1 = com.get([D], F32)
    g2 = com.get([D], F32)
    xin = [com.get([D], F32) for _ in range(2)]
    xres = [com.get([D], F32)]
    hbf = [com.get([D], BF16) for _ in range(2)]
    hT = com.get([NK, TG], BF16)
    tmps = com.get([4, TG], F32)
    ropeCS = com.get([2, TG], F32)
    ropeC = ropeCS[:, 0, :]
    ropeS = ropeCS[:, 1, :]
    xres_rope = ropeCS.rearrange("p a b -> p (a b)")
    lamb = tmps[:, 3, 0:256]
    base_phase = (com.hi + 31) // 32 * 32
    mx = Layout(arena, base_phase)
    w_in = mx.get([NK, CIN], BF16)
    w_out = mx.get([NK, D], BF16)
    KT = mx.get([4, S], BF16)
    V = mx.get([NT, 4, 130], BF16)
    QT = mx.get([4, TG], BF16)
    Ubuf = mx.get([4, 30 + TG], BF16)
    UO = mx.get([NK, TG], BF16)
    ybf_off = (mx.off + 31) // 32 * 32
    ybf = mx.get([4, TG], BF16)
    xres_ybf = arena.view(ybf_off, [D], F32)
    ysq_off = (mx.off + 31) // 32 * 32
    ysq = mx.get([4, TG], BF16)
    rt2 = arena.view(ysq_off, [TG], F32)
    rt3 = arena.view(ysq_off + 2048, [TG], F32)
    rl = mx.get([2, 4], F32)
    ssr = mx.get([4], F32)
    xoff = (mx.off + 31) // 32 * 32
    Dg = mx.get([CONV_K, P], BF16)
    Pt = [[arena.view(xoff + (2 * c + s_) * 1024, [TG], BF16) for s_ in range(2)] for c in range(2)]
    obf = arena.view(xoff + 4096, [4, P], BF16)
    zbf = arena.view(xoff + 5120, [TG], BF16)
    DgB = mx.get([CONV_K, P], BF16)
    ff = Layout(arena, base_phase)
    w_up = ff.get([NK, CUP], BF16)
    w_dn = ff.get([NJ, D], BF16)
    HID = ff.get([NJ, TG], BF16)
    Rb = [ff.get([2 + TG], BF16) for _ in range(3)]
    halo = ff.get([44, 2], BF16)
    D3 = [ff.get([3, P], BF16) for _ in range(2)]
    print("SBUF layout: common %d, mixer %d, ffn %d (limit %d)" % (com.hi, mx.hi, ff.hi, ARENA_BYTES))
    assert mx.hi <= ARENA_BYTES and ff.hi <= ARENA_BYTES

    psum = nc.alloc_psum_tensor("psum", [P, 8 * 512], F32)

    def bank(i, n=1):
        return psum[:, i * 512:(i + n) * 512]

    pT = bank(0).bitcast(BF16)

    B = {}

    def b(name):
        if name not in B:
            B[name] = Buf(name)
        return B[name]

    pb = [b("psum%d" % i) for i in range(8)]
    for x_ in pb:
        x_.excl = True
    WI_BLK = ["wi_a", "wi_g", "wi_q", "wi_k", "wi_v"]
    xalias = ["Pt00", "Pt01", "Pt10", "Pt11", "obf", "zbf"]
    mixer_names = WI_BLK + ["w_out0", "w_out1", "QT", "Ubuf", "Dg", "DgB", "UO", "ybf", "ysq", "rl0", "rl1", "ssr", "Vones"] + xalias + \
        ["KT%d" % g for g in range(NG)] + ["V%d" % g for g in range(NG)]
    ffn_names = ["wu_g%d" % i for i in range(6)] + ["wu_v%d" % i for i in range(6)] + ["wd%d" % i for i in range(6)] + \
        ["HID", "R0", "R1", "R2", "halo", "D30", "D31"]
    mixer_bufs = [b(n) for n in mixer_names]
    ffn_bufs = [b(n) for n in ffn_names]
    xalias_bufs = [b(n) for n in xalias]
    hT_bufs = [b("hT%d" % i) for i in range(4)]

    dsem = {}

    def ds(name):
        if name not in dsem:
            dsem[name] = Holder(nc, "d_" + name)
        return dsem[name]

    def sc(i):
        return small[:, i:i + 1]
    S1, S2, E1, E2, NLAM = range(5)
    SSQ0, RSTD0, PSSQ0, PRSTD0 = 8, 10, 12, 14
    EPSC = 16

    dma(pool, ds("c0"), ident, cmat_d[0], writes=[b("ident")])
    dma(pool, ds("c0b"), pm, cmat_d[1], writes=[b("pm")])
    dma(pool, ds("c0c"), cmask, cmat_d[2], writes=[b("cmask")])
    dma(sp, ds("c1"), identf, cmat_d[0], writes=[b("identf")])
    op(dve, lambda: nc.vector.memset(ones, 1.0), writes=[b("ones")])
    op(dve, lambda: nc.vector.memset(sc(EPSC), EPS), writes=[b("epsc")])

    def norm_chain(T, src_d, srcbuf):
        s_ = T % 2
        xb, hb = b("xin%d" % s_), b("hbf%d" % s_)
        sq, rs = b("ssq%d" % s_), b("rstd%d" % s_)
        dma(sp, ds("xin%d" % s_), xin[s_], src_d[T * P:(T + 1) * P, :], reads=[b(srcbuf)], writes=[xb])
        op(dve, lambda: nc.vector.memset(sc(SSQ0 + s_), 0.0), writes=[sq])
        op(act, lambda: nc.scalar.activation(out=hbf[s_], in_=xin[s_], func=AF.Square, accum_out=sc(SSQ0 + s_)),
           reads=[xb], writes=[hb, sq])
        op(act, lambda: nc.scalar.activation(out=sc(RSTD0 + s_), in_=sc(SSQ0 + s_), func=AF.Ln, bias=sc(EPSC), scale=1.0 / D),
           reads=[sq, b("epsc")], writes=[rs])
        op(act, lambda: nc.scalar.activation(out=sc(RSTD0 + s_), in_=sc(RSTD0 + s_), func=AF.Exp, scale=-0.5), reads=[rs], writes=[rs])
        op(dve, lambda: nc.vector.scalar_tensor_tensor(out=hbf[s_], in0=xin[s_], scalar=sc(RSTD0 + s_), in1=g1,
                                                       op0=ALU.mult, op1=ALU.mult),
           reads=[xb, rs, b("g1")], writes=[hb])

    def norm_T(T):
        s_ = T % 2
        tt = T % 4
        hb = b("hbf%d" % s_)
        for kc in range(NK):
            op(pe, lambda kc=kc: nc.tensor.transpose(out=pT[:, kc * P:(kc + 1) * P], in_=hbf[s_][:, kc * P:(kc + 1) * P],
                                                   identity=ident),
               reads=[hb, b("ident")], writes=[pb[0]], signal=(kc == NK - 1))
        op(dve, lambda: nc.vector.tensor_copy(out=hT[:, :, tt * P:(tt + 1) * P],
                                              in_=pT[:, 0:NK * P].rearrange("p (k c) -> p k c", k=NK)),
           reads=[pb[0]], writes=[hT_bufs[tt]])

    def mm_group(out_ap, pairs, reads, wbuf, start=True):
        n = len(pairs)
        for i, (l_, r_) in enumerate(pairs):
            op(pe, lambda l_=l_, r_=r_, i=i: nc.tensor.matmul(out=out_ap, lhsT=l_, rhs=r_, start=(start and i == 0),
                                                              stop=(i == n - 1)),
               reads=reads, writes=[wbuf], signal=(i == n - 1))

    def post_norm_residual(T, mbanks, src_ap, dst_ap, src_buf_name, dst_buf_name):
        s_ = T % 2
        m_ap = bank(mbanks[0], 2)
        mb = [pb[mbanks[0]], pb[mbanks[1]]]
        if s_ == 0:
            xr, xbs = xres[0], [b("xres0")]
        elif phase[0] == "mixer":
            xr, xbs = xres_ybf, [b("ybf")]
        else:
            xr, xbs = xres_rope, [b("ropeC"), b("ropeS")]
        sq, rs = b("pssq%d" % s_), b("prstd%d" % s_)
        dma(sp, ds("xres%d" % s_), xr, src_ap, reads=[b(src_buf_name)], writes=xbs)
        op(dve, lambda: nc.vector.memset(sc(PSSQ0 + s_), 0.0), writes=[sq])
        tmp32 = tmps[:, 2 * s_:2 * s_ + 2, :].rearrange("p a b -> p (a b)")
        tb = [b("tmp%d" % (2 * s_)), b("tmp%d" % (2 * s_ + 1))]
        op(act, lambda: nc.scalar.activation(out=tmp32, in_=m_ap, func=AF.Square, accum_out=sc(PSSQ0 + s_)),
           reads=mb, writes=tb + [sq])
        op(act, lambda: nc.scalar.activation(out=sc(PRSTD0 + s_), in_=sc(PSSQ0 + s_), func=AF.Ln, bias=sc(EPSC), scale=1.0 / D),
           reads=[sq, b("epsc")], writes=[rs])
        op(act, lambda: nc.scalar.activation(out=sc(PRSTD0 + s_), in_=sc(PRSTD0 + s_), func=AF.Exp, scale=-0.5), reads=[rs], writes=[rs])
        op(dve, lambda: nc.vector.scalar_tensor_tensor(out=tmp32, in0=m_ap, scalar=sc(PRSTD0 + s_), in1=g2,
                                                       op0=ALU.mult, op1=ALU.mult),
           reads=mb + [rs, b("g2")], writes=tb)
        op(dve, lambda: nc.vector.tensor_tensor(out=xr, in0=xr, in1=tmp32, op=ALU.add),
           reads=tb + xbs, writes=xbs)
        dma(sp, ds("xst%d" % s_), dst_ap, xr, reads=xbs, writes=[b(dst_buf_name)])

    rot = [1, 2, 3]
    rr = [0]
    phase = ["mixer"]
    srot = [0]

    def nbs():
        i = srot[0] % 4
        srot[0] += 1
        return i

    def nb():
        i = rot[rr[0] % len(rot)]
        rr[0] += 1
        return i

    for l in range(L):
        lam_init = lam_inits[l]
        src_d = x_d if l == 0 else xs_d
        dma(sp, ds("pv"), pv, pvec_d[l], writes=[b("pv")])
        dma(sp, ds("lamb"), lamb, lamv_d[l].broadcast_to([P, 256]), writes=[b("tmp3")])
        dma(sp, ds("sg"), sg, sg_d[l].broadcast_to([P, P]), writes=[b("sg")])
        dma(sp, ds("g1"), g1, fvec_d[l, 0:1, :].broadcast_to([P, D]), writes=[b("g1")])
        dma(sp, ds("g2"), g2, fvec_d[l, 1:2, :].broadcast_to([P, D]), writes=[b("g2")])
        jt = tmps[:, 2, 0:64]
        op(dve, lambda: nc.vector.scalar_tensor_tensor(out=jt, in0=lamb[:, 0:64], scalar=1.0, in1=lamb[:, 64:128],
                                                       op0=ALU.mult, op1=ALU.mult, accum_out=sc(S1)),
           reads=[b("tmp3")], writes=[b("tmp2"), b("s1")])
        op(dve, lambda: nc.vector.scalar_tensor_tensor(out=jt, in0=lamb[:, 128:192], scalar=1.0, in1=lamb[:, 192:256],
                                                       op0=ALU.mult, op1=ALU.mult, accum_out=sc(S2)),
           reads=[b("tmp3")], writes=[b("tmp2"), b("s2")])
        op(act, lambda: nc.scalar.activation(out=sc(E1), in_=sc(S1), func=AF.Exp), reads=[b("s1")], writes=[b("e1")])
        op(act, lambda: nc.scalar.activation(out=sc(E2), in_=sc(S2), func=AF.Exp), reads=[b("s2")], writes=[b("e2")])
        op(dve, lambda: nc.vector.tensor_tensor(out=sc(NLAM), in0=sc(E2), in1=sc(E1), op=ALU.subtract),
           reads=[b("e1"), b("e2")], writes=[b("nlam")])
        op(dve, lambda: nc.vector.tensor_scalar(out=sc(NLAM), in0=sc(NLAM), scalar1=-float(lam_init), scalar2=None,
                                                op0=ALU.add),
           reads=[b("nlam")], writes=[b("nlam")])
        op(act, lambda: nc.scalar.mul(out=sg, in_=sg, mul=float(1.0 - lam_init)), reads=[b("sg")], writes=[b("sg")])

        first = True
        for blk in (1, 0, 2, 3, 4):
            c0 = blk * 512
            dma(pool, ds("w_in%d" % blk), w_in[:, :, c0:c0 + 512],
                w_in_d[l, :, c0:c0 + 512].rearrange("(k p) c -> p k c", p=P),
                writes=[b(WI_BLK[blk])] + (ffn_bufs if first else []))
            first = False
        for hf in range(2):
            dma(pool, ds("w_out%d" % hf), w_out[:, hf * 4:(hf + 1) * 4, :],
                w_out_d[l, hf * 512:(hf + 1) * 512, :].rearrange("(k p) c -> p k c", p=P), writes=[b("w_out%d" % hf)])
        op(pool, lambda: nc.gpsimd.memset(Ubuf[:, :, 0:30], 0.0), writes=[b("Ubuf")])
        op(pool, lambda: nc.gpsimd.memset(V[:, :, :, 128:130], 1.0), writes=[b("Vones")])

        phase[0] = "mixer"
        for T in range(4):
            norm_chain(T, src_d, "xs%d" % T) if T < 2 else None
        norm_T(0)
        norm_T(1)
        norm_chain(2, src_d, "xs2")
        norm_chain(3, src_d, "xs3")
        norm_T(2)
        norm_T(3)
        for G in range(NG):
            t0 = G * TG
            nxt = G + 1 < NG
            def rope_load(Gt):
                dma(sp, ds("ropeC"), ropeC, rope_d[0, :, Gt * TG:(Gt + 1) * TG], writes=[b("ropeC")])
                dma(sp, ds("ropeS"), ropeS, rope_d[1, :, Gt * TG:(Gt + 1) * TG], writes=[b("ropeS")])

            if G == 0:
                rope_load(0)
            if nxt and G == 0:
                for T in (4, 5):
                    norm_chain(T, src_d, "xs%d" % T)

            def proj(col0, bi):
                mm_group(bank(bi), [(w_in[:, kc, col0:col0 + P], hT[:, kc, :]) for kc in range(NK)],
                         [b(WI_BLK[col0 // 512])] + hT_bufs, pb[bi])

            def dg_gen(ct):
                wv = pv[:, PV_CW + ct * CONV_K:PV_CW + (ct + 1) * CONV_K]
                dgt, dgb = (DgB, b("DgB")) if ct % 2 == 0 else (Dg, b("Dg"))
                op(pool, lambda: nc.gpsimd.tensor_tensor(out=dgt, in0=identf.unsqueeze(1).broadcast_to([P, CONV_K, P]),
                                                         in1=wv.unsqueeze(2).broadcast_to([P, CONV_K, P]), op=ALU.mult),
                   reads=[b("identf"), b("pv")], writes=[dgb] + (xalias_bufs if ct % 2 == 1 else []))

            def glu_step(ct):
                bg = nb()
                proj(512 + ct * P, bg)
                sgt = tmps[:, 3, :]
                sgb = b("tmp3")
                op(act, lambda: nc.scalar.activation(out=sgt, in_=bank(bg), func=AF.Sigmoid), reads=[pb[bg]], writes=[sgb])
                ba = nb()
                proj(ct * P, ba)
                op(dve, lambda: nc.vector.tensor_tensor(out=Ubuf[:, ct, 30:30 + TG], in0=bank(ba), in1=sgt, op=ALU.mult),
                   reads=[pb[ba], sgb], writes=[b("Ubuf")])

            def halo_copy():
                op(pool, lambda: nc.gpsimd.tensor_copy(out=Ubuf[:, :, 0:30], in_=Ubuf[:, :, TG:TG + 30]),
                   reads=[b("Ubuf")], writes=[b("Ubuf")])

            dg_gen(0)
            if G == 0:
                for ct in range(4):
                    glu_step(ct)
            def m3b_unit(Gt, which, h):
                col0 = 1024 + which * 512 + h * P
                bz = nb()
                proj(col0, bz)
                op(act, lambda: nc.scalar.copy(out=zbf, in_=bank(bz)), reads=[pb[bz]], writes=[b("zbf")])
                bp = nb()
                mm_group(bank(bp), [(pm, zbf)], [b("pm"), b("zbf")], pb[bp])
                op(dve, lambda: nc.vector.tensor_tensor(out=rt2, in0=bank(bz), in1=ropeC, op=ALU.mult),
                   reads=[pb[bz], b("ropeC")], writes=[b("ysqA")])
                op(dve, lambda: nc.vector.tensor_tensor(out=rt3, in0=bank(bp), in1=ropeS, op=ALU.mult),
                   reads=[pb[bp], b("ropeS")], writes=[b("ysqB")])
                if which == 0:
                    dst, dbuf = QT[:, h, :], b("QT")
                else:
                    dst, dbuf = KT[:, h, Gt * TG:(Gt + 1) * TG], b("KT%d" % Gt)
                op(dve, lambda: nc.vector.tensor_tensor(out=dst, in0=rt2, in1=rt3, op=ALU.add),
                   reads=[b("ysqA"), b("ysqB")], writes=[dbuf])

            if G == 0:
                for which in range(2):
                    for h in range(4):
                        m3b_unit(0, which, h)
            if nxt:
                rope_load(G + 1)
            for tt in range(4):
                T = G * 4 + tt
                bv = nb()
                mm_group(bank(bv), [(hT[:, kc, tt * P:(tt + 1) * P], w_in[:, kc, 2048:2560]) for kc in range(NK)],
                         [b("wi_v"), hT_bufs[tt]], pb[bv])
                op(act, lambda bv=bv, T=T: nc.scalar.copy(out=V[:, T, :, 0:128],
                                                          in_=bank(bv).rearrange("p (h c) -> p h c", h=4)),
                   reads=[pb[bv]], writes=[b("V%d" % G)])
            if nxt:
                norm_T(4 * G + 4)
                norm_T(4 * G + 5)
                for T in (4 * G + 6, 4 * G + 7):
                    norm_chain(T, src_d, "xs%d" % T)
            dg_gen(1)
            for ct in range(4):
                dgt, dgb = (DgB, b("DgB")) if ct % 2 == 0 else (Dg, b("Dg"))
                bc = nb()
                mm_group(bank(bc), [(dgt[:, k, :], Ubuf[:, ct, k:k + TG]) for k in range(CONV_K)],
                         [dgb, b("Ubuf")], pb[bc])
                if ct + 2 < 4:
                    dg_gen(ct + 2)
                cb = pv[:, PV_CB + ct:PV_CB + ct + 1]
                op(act, lambda bc=bc, ct=ct, cb=cb: nc.scalar.activation(out=ybf[:, ct, :], in_=bank(bc), func=AF.Identity, bias=cb),
                   reads=[pb[bc], b("pv")], writes=[b("ybf")])
                op(act, lambda bc=bc, ct=ct, cb=cb: nc.scalar.activation(out=ysq[:, ct, :], in_=bank(bc), func=AF.Square, bias=cb),
                   reads=[pb[bc], b("pv")], writes=[b("ysq"), b("ysqA"), b("ysqB")])
            b1 = nb()
            mm_group(bank(b1), [(ones, ybf[:, ct, :]) for ct in range(4)], [b("ones"), b("ybf")], pb[b1])
            b2 = nb()
            mm_group(bank(b2), [(ones, ysq[:, ct, :]) for ct in range(4)], [b("ones"), b("ysq")], pb[b2])
            mean, msq, rstd_t = tmps[:, 0, :], tmps[:, 1, :], tmps[:, 2, :]
            op(act, lambda: nc.scalar.mul(out=mean, in_=bank(b1), mul=1.0 / 512), reads=[pb[b1]], writes=[b("tmp0")])
            op(act, lambda: nc.scalar.activation(out=msq, in_=bank(b1), func=AF.Square, scale=1.0 / 512),
               reads=[pb[b1]], writes=[b("tmp1")])
            op(dve, lambda: nc.vector.scalar_tensor_tensor(out=rstd_t, in0=bank(b2), scalar=1.0 / 512, in1=msq,
                                                           op0=ALU.mult, op1=ALU.subtract),
               reads=[pb[b2], b("tmp1")], writes=[b("tmp2")])
            t3 = tmps[:, 3, :]

            def ln_a2():
                op(act, lambda: nc.scalar.activation(out=rstd_t, in_=rstd_t, func=AF.Ln, bias=sc(EPSC), scale=1.0),
                   reads=[b("tmp2"), b("epsc")], writes=[b("tmp2")])
                op(act, lambda: nc.scalar.activation(out=rstd_t, in_=rstd_t, func=AF.Exp, scale=-0.5), reads=[b("tmp2")], writes=[b("tmp2")])

            def ln_tb(ct):
                return (tmps[:, 3, :], b("tmp3")) if ct % 2 == 0 else (tmps[:, 1, :], b("tmp1"))

            def ln_d2(ct):
                tb_, tbb = ln_tb(ct)
                op(dve, lambda: nc.vector.tensor_tensor(out=tb_, in0=ybf[:, ct, :], in1=mean, op=ALU.subtract),
                   reads=[b("ybf"), b("tmp0")], writes=[tbb])
                op(dve, lambda: nc.vector.tensor_tensor(out=tb_, in0=tb_, in1=rstd_t, op=ALU.mult),
                   reads=[tbb, b("tmp2")], writes=[tbb])

            def ln_a3(ct):
                tb_, tbb = ln_tb(ct)
                op(act, lambda: nc.scalar.activation(out=UO[:, ct, :], in_=tb_, func=AF.Silu,
                                                     scale=pv[:, PV_LG + ct:PV_LG + ct + 1],
                                                     bias=pv[:, PV_LB + ct:PV_LB + ct + 1]),
                   reads=[tbb, b("pv")], writes=[b("UO")])

            accv = [psum[:, (4 + 2 * c) * 512:(6 + 2 * c) * 512].rearrange("p (q w) -> p q w", q=4) for c in range(2)]
            accb = [[pb[4], pb[5]], [pb[6], pb[7]]]
            nkt = 4 * G + 4
            sbank = {}

            def emit_qk(h, kt, c):
                r = kt - 4 * G
                c0 = max(r, 0) * P
                sbk = nbs()
                sbank[(kt, c)] = sbk
                mm_group(bank(sbk)[:, c0:TG],
                         [(KT[c * 64:(c + 1) * 64, h, kt * P:(kt + 1) * P], QT[c * 64:(c + 1) * 64, h, c0:TG])],
                         [b("KT%d" % (kt // 4)), b("QT")], pb[sbk])

            def emit_exp(h, kt, c):
                r = kt - 4 * G
                c0 = max(r, 0) * P
                sbk = sbank[(kt, c)]
                sl = kt % 2
                ptb = b("Pt%d%d" % (c, sl))
                pt = Pt[c][sl]
                op(act, lambda: nc.scalar.activation(out=pt[:, c0:TG], in_=bank(sbk)[:, c0:TG], func=AF.Exp, scale=0.125),
                   reads=[pb[sbk]], writes=[ptb])
                if r >= 0:
                    op(dve, lambda: nc.vector.tensor_tensor(out=pt[:, c0:c0 + P], in0=pt[:, c0:c0 + P], in1=cmask, op=ALU.mult),
                       reads=[ptb, b("cmask")], writes=[ptb])

            def emit_av(h, kt, c):
                r = kt - 4 * G
                q0 = max(r, 0)
                sl = kt % 2
                ptb = b("Pt%d%d" % (c, sl))
                pt = Pt[c][sl]
                for ql in range(q0, 4):
                    bk = 4 + 2 * c + ql // 2
                    reg = psum[:, bk * 512 + (ql % 2) * 256: bk * 512 + (ql % 2) * 256 + 129]
                    first_ = (kt == 0 and ql % 2 == 0)
                    last_ = (kt == 4 * G + ql)
                    op(pe, lambda reg=reg, ql=ql, first_=first_, last_=last_:
                       nc.tensor.matmul(out=reg, lhsT=pt[:, ql * P:(ql + 1) * P], rhs=V[:, kt, h, 0:129],
                                        start=first_, stop=last_, skip_group_check=True),
                       reads=[ptb, b("V%d" % (kt // 4)), b("Vones")], writes=[pb[bk]], signal=(ql == 3))

            def finalize_a(h):
                t1 = tmps[:, 0, :].rearrange("p (q w) -> p q w", q=4)
                t2 = tmps[:, 1, :].rearrange("p (q w) -> p q w", q=4)
                o3 = tmps[:, 2, :].rearrange("p (q w) -> p q w", q=4)
                op(dve, lambda: nc.vector.reciprocal(out=rl[:, 0, :], in_=accv[0][:, :, 128]), reads=accb[0], writes=[b("rl0")])
                op(dve, lambda: nc.vector.tensor_copy(out=t1, in_=accv[0][:, :, 0:128]), reads=accb[0], writes=[b("tmp0")])
                op(dve, lambda: nc.vector.reciprocal(out=rl[:, 1, :], in_=accv[1][:, :, 128]), reads=accb[1], writes=[b("rl1")])
                op(dve, lambda: nc.vector.tensor_copy(out=t2, in_=accv[1][:, :, 0:128]), reads=accb[1], writes=[b("tmp1")])
                op(dve, lambda: nc.vector.tensor_scalar(out=rl[:, 1, :], in0=rl[:, 1, :], scalar1=sc(NLAM), scalar2=None,
                                                        op0=ALU.mult),
                   reads=[b("rl1"), b("nlam")], writes=[b("rl1")])
                op(dve, lambda: nc.vector.tensor_tensor(out=t1, in0=t1,
                                                        in1=rl[:, 0, :].unsqueeze(2).broadcast_to([P, 4, P]), op=ALU.mult),
                   reads=[b("tmp0"), b("rl0")], writes=[b("tmp0")])
                op(dve, lambda: nc.vector.tensor_tensor(out=t2, in0=t2,
                                                        in1=rl[:, 1, :].unsqueeze(2).broadcast_to([P, 4, P]), op=ALU.mult),
                   reads=[b("tmp1"), b("rl1")], writes=[b("tmp1")])
                op(dve, lambda: nc.vector.tensor_tensor(out=o3, in0=t1, in1=t2, op=ALU.add),
                   reads=[b("tmp0"), b("tmp1")], writes=[b("tmp2")])
                op(dve, lambda: nc.vector.tensor_tensor(out=t1, in0=o3, in1=o3, op=ALU.mult),
                   reads=[b("tmp2")], writes=[b("tmp0")])
                op(dve, lambda: nc.vector.tensor_reduce(out=ssr, in_=t1, axis=AX.X, op=ALU.add),
                   reads=[b("tmp0")], writes=[b("ssr")])

            def finalize_a2(h):
                o3 = tmps[:, 2, :].rearrange("p (q w) -> p q w", q=4)
                op(act, lambda: nc.scalar.activation(out=ssr, in_=ssr, func=AF.Ln, bias=sc(EPSC), scale=1.0 / 128),
                   reads=[b("ssr"), b("epsc")], writes=[b("ssr")])
                op(act, lambda: nc.scalar.activation(out=ssr, in_=ssr, func=AF.Exp, scale=-0.5), reads=[b("ssr")], writes=[b("ssr")])
                op(dve, lambda: nc.vector.tensor_tensor(out=o3, in0=o3, in1=ssr.unsqueeze(2).broadcast_to([P, 4, P]), op=ALU.mult),
                   reads=[b("tmp2"), b("ssr")], writes=[b("tmp2")])
                op(dve, lambda: nc.vector.tensor_tensor(out=obf, in0=o3, in1=sg.unsqueeze(1).broadcast_to([P, 4, P]), op=ALU.mult),
                   reads=[b("tmp2"), b("sg")], writes=[b("obf")])

            def finalize_b(h):
                for ql in range(4):
                    op(pe, lambda ql=ql: nc.tensor.transpose(out=pT[:, ql * P:(ql + 1) * P], in_=obf[:, ql, :], identity=ident),
                       reads=[b("obf"), b("ident")], writes=[pb[0]], signal=(ql == 3))
                op(act, lambda: nc.scalar.copy(out=UO[:, 4 + h, :], in_=pT[:, 0:TG]), reads=[pb[0]], writes=[b("UO")])

            deferred = None
            if nxt:
                norm_T(4 * G + 6)
                norm_T(4 * G + 7)
                halo_copy()
                glu_step(0)
                glu_step(1)
            for h in range(4):
                emit_qk(h, 0, 0)
                emit_qk(h, 0, 1)
                for kt in range(nkt):
                    if kt + 1 < nkt:
                        emit_qk(h, kt + 1, 0)
                        emit_qk(h, kt + 1, 1)
                    emit_exp(h, kt, 0)
                    emit_exp(h, kt, 1)
                    if h == 0 and kt < 4:
                        if kt == 0:
                            ln_a2()
                        else:
                            ln_a3(kt - 1)
                        ln_d2(kt)
                    if kt == 1 and deferred is not None:
                        finalize_a2(deferred)
                    emit_av(h, kt, 0)
                    emit_av(h, kt, 1)
                    if kt == (4 if nkt > 4 else 2) and deferred is not None:
                        finalize_b(deferred)
                        deferred = None
                if h == 0:
                    ln_a3(3)
                if h == 3 and G + 2 < NG:
                    for T in (4 * G + 8, 4 * G + 9):
                        norm_chain(T, src_d, "xs%d" % T)
                finalize_a(h)
                deferred = h
                if h == 3 and nxt:
                    glu_step(2)
                    glu_step(3)
            finalize_a2(deferred)
            if nxt:
                m3b_unit(G + 1, 0, 0)
                m3b_unit(G + 1, 0, 1)
            finalize_b(deferred)
            def wout_tile(tt):
                T = G * 4 + tt
                mbanks = (4, 5) if tt % 2 == 0 else (6, 7)
                for half in range(2):
                    mm_group(bank(mbanks[half]),
                             [(UO[:, cc, tt * P:(tt + 1) * P], w_out[:, cc, half * 512:(half + 1) * 512]) for cc in range(NK)],
                             [b("UO"), b("w_out0"), b("w_out1")], pb[mbanks[half]])
                post_norm_residual(T, mbanks, src_d[T * P:(T + 1) * P, :], xs_d[T * P:(T + 1) * P, :], "xs%d" % T, "xs%d" % T)

            wout_tile(0)
            wout_tile(1)
            if nxt:
                m3b_unit(G + 1, 0, 2)
                m3b_unit(G + 1, 0, 3)
            wout_tile(2)
            if nxt:
                m3b_unit(G + 1, 1, 0)
                m3b_unit(G + 1, 1, 1)
            wout_tile(3)
            if nxt:
                m3b_unit(G + 1, 1, 2)
                m3b_unit(G + 1, 1, 3)

        phase[0] = "ffn"
        dma(sp, ds("g1"), g1, fvec_d[l, 2:3, :].broadcast_to([P, D]), writes=[b("g1")])
        dma(sp, ds("g2"), g2, fvec_d[l, 3:4, :].broadcast_to([P, D]), writes=[b("g2")])
        first = True
        for blk in range(6):
            w_ = min(512, DFF - blk * 512)
            for which in range(2):
                c0 = which * DFF + blk * 512
                dma(pool, ds("w_up%d%d" % (which, blk)), w_up[:, :, c0:c0 + w_],
                    w_up_d[l, :, c0:c0 + w_].rearrange("(k p) c -> p k c", p=P),
                    writes=[b(("wu_g%d" if which == 0 else "wu_v%d") % blk)] + (mixer_bufs if first else []))
                first = False
        for gq in range(6):
            j0 = gq * 4
            j1 = min(j0 + 4, NJ)
            dma(pool, ds("w_dn%d" % gq), w_dn[:, j0:j1, :],
                w_dn_d[l, j0 * P:j1 * P, :].rearrange("(j p) c -> p j c", p=P), writes=[b("wd%d" % gq)])
        op(dve, lambda: nc.vector.memset(halo, 0.0), writes=[b("halo")])
        dst_d = out_d if l == L - 1 else xs_d
        norm_chain(0, xs_d, "xs0")
        norm_chain(1, xs_d, "xs1")
        norm_T(0)
        norm_T(1)
        norm_chain(2, xs_d, "xs2")
        norm_chain(3, xs_d, "xs3")
        norm_T(2)
        norm_T(3)
        for G in range(NG):
            nxt = G + 1 < NG
            if nxt and G == 0:
                for T in (4, 5):
                    norm_chain(T, xs_d, "xs%d" % T)
            ubank = {}

            def up_tile(n):
                jp, which = n // 2, n % 2
                j = which * NJ + jp
                col0 = which * DFF + jp * P
                s_ = n % 3
                bu = 1 + n % 3
                ubank[n] = bu
                wb_ = b(("wu_g%d" if which == 0 else "wu_v%d") % (jp // 4))
                mm_group(bank(bu), [(w_up[:, kc, col0:col0 + P], hT[:, kc, :]) for kc in range(NK)],
                         [wb_] + hT_bufs, pb[bu])
                rbuf = b("R%d" % s_)
                op(dve, lambda: nc.vector.tensor_copy(out=Rb[s_][:, 0:2], in_=halo[:, j, :]),
                   reads=[b("halo")], writes=[rbuf])
                op(act, lambda: nc.scalar.copy(out=Rb[s_][:, 2:2 + TG], in_=bank(bu)), reads=[pb[bu]], writes=[rbuf])
                op(dve, lambda: nc.vector.tensor_copy(out=halo[:, j, :], in_=Rb[s_][:, TG:TG + 2]),
                   reads=[rbuf], writes=[b("halo")])
                base_ = 4 if jp % 2 == 0 else 6
                bcv = base_ + which
                if which == 0:
                    op(act, lambda: nc.scalar.activation(out=bank(bcv), in_=bank(bu), func=AF.Identity,
                                                         scale=pv[:, PV_FW + j * 3 + 2:PV_FW + j * 3 + 3],
                                                         bias=pv[:, PV_FB + j:PV_FB + j + 1]),
                       reads=[pb[bu], b("pv")], writes=[pb[bcv]])
                else:
                    fw = pv[:, PV_FW + j * 3:PV_FW + j * 3 + 3]
                    op(pool, lambda: nc.gpsimd.tensor_tensor(out=D3[jp % 2], in0=identf.unsqueeze(1).broadcast_to([P, 3, P]),
                                                             in1=fw.unsqueeze(2).broadcast_to([P, 3, P]), op=ALU.mult),
                       reads=[b("identf"), b("pv")], writes=[b("D3%d" % (jp % 2))])

            def conv_tile(n):
                jp, which = n // 2, n % 2
                j = which * NJ + jp
                s_ = n % 3
                base_ = 4 if jp % 2 == 0 else 6
                bcv = base_ + which
                rbuf = b("R%d" % s_)
                if which == 0:
                    for k in (1, 0):
                        op(dve, lambda k=k: nc.vector.scalar_tensor_tensor(out=bank(bcv), in0=Rb[s_][:, k:k + TG],
                                                                           scalar=pv[:, PV_FW + j * 3 + k:PV_FW + j * 3 + k + 1],
                                                                           in1=bank(bcv), op0=ALU.mult, op1=ALU.add),
                           reads=[rbuf, pb[bcv], b("pv")], writes=[pb[bcv]])
                    gl = tmps[:, 2 + jp % 2, :]
                    glb = b("tmp%d" % (2 + jp % 2))
                    op(act, lambda: nc.scalar.activation(out=gl, in_=bank(base_), func=AF.Gelu_apprx_tanh),
                       reads=[pb[base_]], writes=[glb])
                else:
                    mm_group(bank(bcv), [(D3[jp % 2][:, k, :], Rb[s_][:, k:k + TG]) for k in range(3)],
                             [b("D3%d" % (jp % 2)), rbuf], pb[bcv])
                    gl = tmps[:, 2 + jp % 2, :]
                    glb = b("tmp%d" % (2 + jp % 2))
                    op(dve, lambda: nc.vector.scalar_tensor_tensor(out=HID[:, jp, :], in0=bank(bcv),
                                                                   scalar=pv[:, PV_FB + j:PV_FB + j + 1],
                                                                   in1=gl, op0=ALU.add, op1=ALU.mult),
                       reads=[pb[bcv], glb, b("pv")], writes=[b("HID")])

            def down_tile(tt):
                T = G * 4 + tt
                mbanks = (6, 7) if tt % 2 == 0 else (4, 5)
                for half in range(2):
                    for jp in range(NJ):
                        op(pe, lambda jp=jp, half=half: nc.tensor.matmul(out=bank(mbanks[half]), lhsT=HID[:, jp, tt * P:(tt + 1) * P],
                                                                         rhs=w_dn[:, jp, half * 512:(half + 1) * 512],
                                                                         start=(jp == 0), stop=(jp == NJ - 1)),
                           reads=[b("HID"), b("wd%d" % (jp // 4))], writes=[pb[mbanks[half]]], signal=(jp == NJ - 1))
                post_norm_residual(T, mbanks, xs_d[T * P:(T + 1) * P, :], dst_d[T * P:(T + 1) * P, :],
                                   "xs%d" % T, ("out%d" % T) if l == L - 1 else ("xs%d" % T))

            NTL = 2 * NJ
            up_tile(0)
            up_tile(1)
            for n in range(NTL):
                if n + 2 < NTL:
                    up_tile(n + 2)
                conv_tile(n)
            if nxt:
                norm_T(4 * G + 4)
                norm_T(4 * G + 5)
                for T in (4 * G + 6, 4 * G + 7):
                    norm_chain(T, xs_d, "xs%d" % T)
            down_tile(0)
            down_tile(1)
            if nxt:
                norm_T(4 * G + 6)
                norm_T(4 * G + 7)
            if G + 2 < NG:
                for T in (4 * G + 8, 4 * G + 9):
                    norm_chain(T, xs_d, "xs%d" % T)
            down_tile(2)
            down_tile(3)

    for nm in ("xst0", "xst1"):
        sp.eng.wait_ge(dsem[nm].sem, dsem[nm].cnt)
    stats = {q.name: (q.nins, q.nwait) for q in (pe, act, dve, pool, sp)}
    print("instr/wait counts:", stats)
    return nc


def _consts(S):
    ident = np.eye(P, dtype=np.float32)
    pmat = np.zeros((P, P), np.float32)
    for base in (0, 64):
        for i in range(8):
            pmat[base + i + 8, base + i] = -1.0
            pmat[base + i, base + i + 8] = 1.0
    mask = (np.arange(P)[:, None] <= np.arange(P)[None, :]).astype(np.float32)
    pos = np.arange(S, dtype=np.float32)
    inv_freq = (np.float32(500000.0) ** (-np.arange(0, 16, 2, dtype=np.float32) / np.float32(16))).astype(np.float32)
    ang = (pos[:, None] * inv_freq[None, :]).astype(np.float32)
    cos = np.cos(ang).astype(np.float32)
    sin = np.sin(ang).astype(np.float32)
    C = np.ones((P, S), np.float32)
    Sn = np.zeros((P, S), np.float32)
    for base in (0, 64):
        for i in range(16):
            C[base + i] = cos[:, i % 8]
            Sn[base + i] = sin[:, i % 8]
    return np.stack([ident, pmat, mask]), np.stack([C, Sn])


def _pack_small(inp, L):
    pvec = np.zeros((L, P, NPV), np.float32)
    cw = np.asarray(inp["conv_w"], np.float32)
    pvec[:, :, PV_CW:PV_CW + 124] = cw.reshape(L, 31, 4, P).transpose(0, 3, 2, 1).reshape(L, P, 124)
    for name, off in (("conv_b", PV_CB), ("conv_ln_g", PV_LG), ("conv_ln_b", PV_LB)):
        pvec[:, :, off:off + 4] = np.asarray(inp[name], np.float32).reshape(L, 4, P).transpose(0, 2, 1)
    fw = np.asarray(inp["ffn_conv_w"], np.float32)
    pvec[:, :, PV_FW:PV_FW + 132] = fw.reshape(L, 3, 44, P).transpose(0, 3, 2, 1).reshape(L, P, 132)
    pvec[:, :, PV_FB:PV_FB + 44] = np.asarray(inp["ffn_conv_b"], np.float32).reshape(L, 44, P).transpose(0, 2, 1)
    fvec = np.stack([np.asarray(inp[k], np.float32) for k in
                     ("pre_mix_norm", "post_mix_norm", "pre_ffn_norm", "post_ffn_norm")], axis=1)
    lamv = np.concatenate([np.asarray(inp[k], np.float32) for k in
                           ("lambda_q1", "lambda_k1", "lambda_q2", "lambda_k2")], axis=1).reshape(L, 1, 256)
    sgv = np.asarray(inp["subln_g"], np.float32).reshape(L, 1, P)
    return pvec, np.ascontiguousarray(fvec), np.ascontiguousarray(lamv), sgv


_PROG_CACHE = {}


def kernel(**inputs):
    x = np.asarray(inputs["x"], np.float32)
    Bsz, S, _ = x.shape
    L = np.asarray(inputs["w_in"]).shape[0]
    lam_inits = [0.8 - 0.6 * math.exp(-0.3 * l) for l in range(L)]
    key = (L, S)
    if key not in _PROG_CACHE:
        _PROG_CACHE[key] = build_program(L, S, lam_inits)
    nc = _PROG_CACHE[key]
    cmat, rope = _consts(S)
    pvec, fvec, lamv, sgv = _pack_small(inputs, L)
    shared = {
        "w_in": np.ascontiguousarray(inputs["w_in"], np.float32),
        "w_out": np.ascontiguousarray(inputs["w_out"], np.float32),
        "w_up": np.ascontiguousarray(inputs["w_up"], np.float32),
        "w_down": np.ascontiguousarray(inputs["w_down"], np.float32),
        "pvec": pvec, "fvec": fvec, "lamv": lamv, "sublng": sgv, "cmat": cmat, "rope": rope,
    }
    in_maps = []
    for c in range(Bsz):
        m = dict(shared)
        m["x"] = np.ascontiguousarray(x[c])
        in_maps.append(m)
    res = run_bass_kernel_spmd(nc, in_maps, core_ids=list(range(Bsz)))
    return np.stack([np.asarray(r["out"], np.float32) for r in res.results], axis=0)
```
